# Optimizing a Trainium2 kernel written in Bass

```python
import jax, jax.numpy as jnp
from jax import lax
import numpy as np

D_MODEL = 1024
BATCH = 32
SEQ = 256
DEPTH = 2
DEC_BATCH = 4
DEC_SEQ = 2048
PAST_LEN = 256

GRID_W = 64
EPS = 1e-6
W_A = D_MODEL // 4
CONV_A_K = 31
H_B = 4
HGRN_HD = 64
W_B = H_B * HGRN_HD
CHUNK = 32
W_C = D_MODEL // 4
CONV_C_K = 3
W_D = D_MODEL // 4
POOL_WINDOWS = (2, 4, 8, 16)
POOL_GROUPS = 4
POOL_GC = W_D // POOL_GROUPS
POOL_OUT = D_MODEL // POOL_GROUPS
N_BRANCH = 4
D_FF = ((8 * D_MODEL + 3 * 256 - 1) // (3 * 256)) * 256
IN_WIDTHS = (W_A, W_A, W_B, W_B, W_B, W_B, W_B, W_C, W_C, W_C, W_D, N_BRANCH * D_MODEL)
N_IN = sum(IN_WIDTHS)

kernel_name = "hybrid_conv_hgrn2_pool_diffusion_step"


def rmsnorm(x, g):
    xf = x.astype(jnp.float32)
    y = xf * lax.rsqrt(jnp.mean(xf * xf, axis=-1, keepdims=True) + EPS)
    return y.astype(x.dtype) * g


def layernorm(x, g, b):
    xf = x.astype(jnp.float32)
    mu = jnp.mean(xf, axis=-1, keepdims=True)
    var = jnp.mean(jnp.square(xf - mu), axis=-1, keepdims=True)
    return ((xf - mu) * lax.rsqrt(var + EPS)).astype(x.dtype) * g + b


def along_grid(fn, u, rows, vertical):
    if rows is None:
        return fn(u)
    B, L, C = u.shape
    g = u.reshape(B, rows, GRID_W, C)
    if vertical:
        g = g.transpose(0, 2, 1, 3).reshape(B * GRID_W, rows, C)
        g = fn(g).reshape(B, GRID_W, rows, C).transpose(0, 2, 1, 3)
    else:
        g = fn(g.reshape(B * rows, GRID_W, C)).reshape(B, rows, GRID_W, C)
    return g.reshape(B, L, C)


def depthwise_conv(u, w):
    K, C = w.shape
    return lax.conv_general_dilated(
        u, w.astype(u.dtype)[:, None, :], window_strides=(1,),
        padding=[(K // 2, K // 2)], dimension_numbers=("NWC", "WIO", "NWC"),
        feature_group_count=C)


def multiscale_pool(u):
    N, L, C = u.shape
    uf = u.astype(jnp.float32)
    cs = jnp.concatenate([jnp.zeros((N, 1, C), jnp.float32), jnp.cumsum(uf, axis=1)], axis=1)
    t = jnp.arange(L)
    outs = []
    for gi, w in enumerate(POOL_WINDOWS):
        lo = jnp.clip(t - w // 2, 0, L)
        hi = jnp.clip(t + w - w // 2, 0, L)
        csg = cs[..., gi * POOL_GC:(gi + 1) * POOL_GC]
        cnt = (hi - lo).astype(jnp.float32)[None, :, None]
        mean = (csg[:, hi] - csg[:, lo]) / cnt
        outs.append(mean - uf[..., gi * POOL_GC:(gi + 1) * POOL_GC])
    return jnp.concatenate(outs, axis=-1).astype(u.dtype)


def log_forget(z, lb):
    zf = z.astype(jnp.float32)
    return jnp.logaddexp(jnp.log(lb), jnp.log1p(-lb) + jax.nn.log_sigmoid(zf))


def hgrn_chunk_scan(q, logf, v, s0):
    B, L, H, Dk = q.shape
    n = L // CHUNK

    def chunks(a):
        return a.astype(jnp.float32).reshape(B, n, CHUNK, H, a.shape[-1]).transpose(0, 3, 1, 2, 4)

    qc, lf, vc = chunks(q), chunks(logf), chunks(v)
    kc = -jnp.expm1(lf)
    bcum = jnp.cumsum(lf, axis=3)
    blast = bcum[..., -1:, :]
    mask = jnp.tril(jnp.ones((CHUNK, CHUNK), dtype=bool))
    diff = bcum[..., :, None, :] - bcum[..., None, :, :]
    decay = jnp.exp(jnp.where(mask[:, :, None], diff, -jnp.inf))
    scores = jnp.einsum("bhntk,bhntsk,bhnsk->bhnts", qc, decay, kc)
    o_intra = jnp.einsum("bhnts,bhnsv->bhntv", scores, vc)
    chunk_kv = jnp.einsum("bhnsk,bhnsv->bhnkv", kc * jnp.exp(blast - bcum), vc)
    chunk_decay = jnp.exp(blast[..., 0, :])

    def step(S, xs):
        dec, kv = xs
        return dec[..., None] * S + kv, S

    s_final, s_starts = lax.scan(
        step, s0.astype(jnp.float32),
        (chunk_decay.transpose(2, 0, 1, 3), chunk_kv.transpose(2, 0, 1, 3, 4)))
    o_inter = jnp.einsum("bhntk,nbhkv->bhntv", qc * jnp.exp(bcum), s_starts)
    o = (o_intra + o_inter).transpose(0, 2, 3, 1, 4).reshape(B, L, H, v.shape[-1])
    return o, s_final


def hgrn2_branch(q, f_fw, f_bw, i_in, g_out, s0, lb, norm_g, w_out):
    B, L, _ = q.shape
    heads = lambda t: t.reshape(B, L, H_B, HGRN_HD)
    flip = lambda t: jnp.flip(t, axis=1)
    qh, vh = heads(q), heads(i_in)
    lf_fw, lf_bw = heads(log_forget(f_fw, lb[0])), heads(log_forget(f_bw, lb[1]))
    o_fw, s_fw = hgrn_chunk_scan(qh, lf_fw, vh, s0[:, 0])
    o_bw, s_bw = hgrn_chunk_scan(flip(qh), flip(lf_bw), flip(vh), s0[:, 1])
    o = o_fw + flip(o_bw)
    o = o * lax.rsqrt(jnp.mean(o * o, axis=-1, keepdims=True) + EPS)
    o = o.reshape(B, L, W_B).astype(q.dtype) * norm_g * jax.nn.silu(g_out)
    return o @ w_out, jnp.stack([s_fw, s_bw], axis=1)


def token_mixers(h, s0, rows, lb, p):
    B, L, _ = h.shape
    offsets = np.cumsum(IN_WIDTHS)[:-1].tolist()
    z = h @ p["w_in"]
    a_val, a_gate, q, f_fw, f_bw, i_in, g_out, c_b, c_c, c_x, d_in, merge = jnp.split(z, offsets, axis=-1)
    u = a_val * jax.nn.sigmoid(a_gate)
    u = along_grid(lambda t: depthwise_conv(t, p["conv_a_w"]), u, rows, vertical=False) + p["conv_a_b"]
    y_a = jax.nn.silu(layernorm(u, p["ln_a_g"], p["ln_a_b"])) @ p["w_out_a"]
    y_b, s_final = hgrn2_branch(q, f_fw, f_bw, i_in, g_out, s0, lb, p["hgrn_norm_g"], p["w_out_b"])
    u = along_grid(lambda t: depthwise_conv(t, p["conv_c_w"]), c_c * c_x, rows, vertical=True)
    y_c = (c_b * u) @ p["w_out_c"]
    u = along_grid(multiscale_pool, d_in, rows, vertical=True)
    y_d = jnp.einsum("blgc,gcd->blgd", u.reshape(B, L, POOL_GROUPS, POOL_GC), p["pool_w"])
    y_d = y_d.reshape(B, L, D_MODEL) * p["pool_scale"]
    gates = jax.nn.sigmoid(merge).reshape(B, L, N_BRANCH, D_MODEL)
    merged = gates[..., 0, :] * y_a + gates[..., 1, :] * y_b + gates[..., 2, :] * y_c + gates[..., 3, :] * y_d
    return merged @ p["w_o"], s_final


def swiglu(h, w13, w2):
    gu = h @ w13
    gate, up = jnp.split(gu, 2, axis=-1)
    return (jax.nn.silu(gate) * up) @ w2


def trunk_layer(x, mod, s0, rows, lb, p):
    sh_m, sc_m, g_m, sh_f, sc_f, g_f = jnp.split(mod, 6, axis=-1)
    h = rmsnorm(x, p["norm_mix_g"]) * (1 + sc_m) + sh_m
    mix, s_final = token_mixers(h, s0, rows, lb, p)
    x = x + g_m * mix
    h = rmsnorm(x, p["norm_ffn_g"]) * (1 + sc_f) + sh_f
    x = x + g_f * swiglu(h, p["ffn_w13"], p["ffn_w2"])
    return x, s_final


def setup_inputs(seed: int = 0) -> dict:
    key = jax.random.key(seed)
    ks = jax.random.split(key, 32)

    def nrm(k, shape, scale):
        return jax.random.normal(k, shape, jnp.float32) * scale

    D = D_MODEL
    return {
        "x_prompt": nrm(ks[0], (BATCH, SEQ, D), 1.0),
        "x_sample": nrm(ks[1], (DEC_BATCH, DEC_SEQ, D), 1.0),
        "state_hgrn": nrm(ks[2], (DEC_BATCH, DEPTH, 2, H_B, HGRN_HD, HGRN_HD), 0.5),
        "c": nrm(ks[3], (DEC_BATCH, D), 1.0),
        "c_ctx": nrm(ks[4], (D,), 1.0),
        "ada_w": nrm(ks[5], (DEPTH, D, 6 * D), 0.5 * D ** -0.5),
        "ada_b": nrm(ks[6], (DEPTH, 6 * D), 0.02),
        "norm_mix_g": 1.0 + nrm(ks[7], (DEPTH, D), 0.05),
        "w_in": nrm(ks[8], (DEPTH, D, N_IN), D ** -0.5),
        "conv_a_w": nrm(ks[9], (DEPTH, CONV_A_K, W_A), CONV_A_K ** -0.5),
        "conv_a_b": nrm(ks[10], (DEPTH, W_A), 0.02),
        "ln_a_g": 1.0 + nrm(ks[11], (DEPTH, W_A), 0.05),
        "ln_a_b": nrm(ks[12], (DEPTH, W_A), 0.02),
        "w_out_a": nrm(ks[13], (DEPTH, W_A, D), W_A ** -0.5),
        "hgrn_lb_logits": nrm(ks[14], (DEPTH, 2, W_B), 0.5),
        "hgrn_norm_g": 1.0 + nrm(ks[15], (DEPTH, W_B), 0.05),
        "w_out_b": nrm(ks[16], (DEPTH, W_B, D), W_B ** -0.5),
        "conv_c_w": nrm(ks[17], (DEPTH, CONV_C_K, W_C), CONV_C_K ** -0.5),
        "w_out_c": nrm(ks[18], (DEPTH, W_C, D), W_C ** -0.5),
        "pool_w": nrm(ks[19], (DEPTH, POOL_GROUPS, POOL_GC, POOL_OUT), POOL_GC ** -0.5),
        "pool_scale": 1.0 + nrm(ks[20], (DEPTH, D), 0.1),
        "w_o": nrm(ks[21], (DEPTH, D, D), D ** -0.5),
        "norm_ffn_g": 1.0 + nrm(ks[22], (DEPTH, D), 0.05),
        "ffn_w13": nrm(ks[23], (DEPTH, D, 2 * D_FF), D ** -0.5),
        "ffn_w2": nrm(ks[24], (DEPTH, D_FF, D), D_FF ** -0.5),
        "final_norm_g": 1.0 + nrm(ks[25], (D,), 0.05),
    }


def reference(x_prompt, x_sample, state_hgrn, c, c_ctx, ada_w, ada_b, norm_mix_g, w_in,
              conv_a_w, conv_a_b, ln_a_g, ln_a_b, w_out_a, hgrn_lb_logits, hgrn_norm_g,
              w_out_b, conv_c_w, w_out_c, pool_w, pool_scale, w_o, norm_ffn_g,
              ffn_w13, ffn_w2, final_norm_g):
    rows = x_sample.shape[1] // GRID_W
    lb_all = jnp.cumsum(jax.nn.softmax(hgrn_lb_logits.astype(jnp.float32), axis=0), axis=0)
    lb_all = lb_all - lb_all[:1]

    y_p = x_prompt
    y_s = x_sample
    ctx_states = []
    for l in range(DEPTH):
        p = {
            "norm_mix_g": norm_mix_g[l], "w_in": w_in[l], "conv_a_w": conv_a_w[l],
            "conv_a_b": conv_a_b[l], "ln_a_g": ln_a_g[l], "ln_a_b": ln_a_b[l],
            "w_out_a": w_out_a[l], "hgrn_norm_g": hgrn_norm_g[l], "w_out_b": w_out_b[l],
            "conv_c_w": conv_c_w[l], "w_out_c": w_out_c[l], "pool_w": pool_w[l],
            "pool_scale": pool_scale[l], "w_o": w_o[l], "norm_ffn_g": norm_ffn_g[l],
            "ffn_w13": ffn_w13[l], "ffn_w2": ffn_w2[l],
        }
        mod_ctx = (jax.nn.silu(c_ctx) @ ada_w[l] + ada_b[l])[None, None, :]
        s0_ctx = jnp.zeros((y_p.shape[0], 2, H_B, HGRN_HD, HGRN_HD), jnp.float32)
        y_p, s_ctx = trunk_layer(y_p, mod_ctx, s0_ctx, None, lb_all[l], p)
        ctx_states.append(s_ctx.astype(x_prompt.dtype))
        mod_lat = (jax.nn.silu(c) @ ada_w[l] + ada_b[l])[:, None, :]
        y_s, _ = trunk_layer(y_s, mod_lat, state_hgrn[:, l], rows, lb_all[l], p)

    y_prompt = rmsnorm(y_p, final_norm_g)
    y_sample = rmsnorm(y_s, final_norm_g)
    new_state_hgrn = jnp.stack(ctx_states, axis=1)
    return (y_prompt, y_sample, new_state_hgrn)
```

```python
import numpy as np
from contextlib import ExitStack
import concourse.bass as bass
import concourse.mybir as mybir
from concourse.bass_utils import run_bass_kernel_spmd

F32 = mybir.dt.float32
BF16 = mybir.dt.bfloat16
AF = mybir.ActivationFunctionType
ALU = mybir.AluOpType

D = 1024
NT = 2048
KC = 8
NG = 4
GT = 512
L = 2
N_IN = 6912
DFF = 2816
EPS = 1e-6
OFF = dict(a_val=0, a_gate=256, q=512, f_fw=768, f_bw=1024, i=1280, g=1536, c_b=1792, c_c=2048, c_x=2304,
           d=2560, merge=2816)
WB = 4096
NWB = 3


class Sched:
    def __init__(self, nc, n_dma_sems=32):
        self.nc = nc
        self.engs = {"pe": nc.tensor, "dve": nc.vector, "act": nc.scalar, "pool": nc.gpsimd, "sp": nc.sync}
        self.sem = {k: nc.alloc_semaphore(name="s_" + k) for k in self.engs}
        self.cnt = {k: 0 for k in self.engs}
        self.seen = {k: {} for k in self.engs}
        self.dsems = [nc.alloc_semaphore(name=f"d{i}") for i in range(n_dma_sems)]
        self.dcnt = [0] * n_dma_sems
        hn = n_dma_sems // 2
        self.drange = {"sp": (0, hn), "pool": (hn, hn + 4), "poolw": (hn + 4, n_dma_sems), "act": (0, hn)}
        self.dnext = {"sp": 0, "pool": hn, "poolw": hn + 4}
        self.lastw = {}
        self.reads = {}
        self.nwaits = 0
        self.ninst = 0
        self.tfin = {}
        self.efree = {k: 0.0 for k in self.engs}
        self.actset = None

    def _wait(self, e, tok):
        key, val, sem = tok
        if key == e and e == "pe":
            return
        if self.seen[e].get(key, 0) >= val:
            return
        self.engs[e].wait_ge(sem, val)
        self.nwaits += 1
        self.seen[e][key] = val

    @staticmethod
    def _flat(keys):
        out = []
        for k in keys:
            if isinstance(k, list):
                out.extend(k)
            else:
                out.append(k)
        return out

    def _deps(self, e, reads, writes):
        toks = []
        for r in reads:
            if r in self.lastw:
                toks.append(self.lastw[r])
        for w in writes:
            if w in self.lastw:
                toks.append(self.lastw[w])
            toks.extend(self.reads.get(w, {}).values())
        for t in toks:
            self._wait(e, t)

    def _commit(self, tok, reads, writes):
        for r in reads:
            d = self.reads.setdefault(r, {})
            old = d.get(tok[0])
            if old is None or old[1] < tok[1]:
                d[tok[0]] = tok
        for w in writes:
            self.lastw[w] = tok
            self.reads[w] = {}

    def _dep_toks(self, reads, writes):
        toks = []
        for r in reads:
            if r in self.lastw:
                toks.append(self.lastw[r])
        for w in writes:
            if w in self.lastw:
                toks.append(self.lastw[w])
            toks.extend(self.reads.get(w, {}).values())
        return toks

    def est_start(self, e, reads, writes, aset=None):
        t = self.efree[e]
        for tk in self._dep_toks(self._flat(reads), self._flat(writes)):
            t = max(t, self.tfin.get((tk[0], tk[1]), 0.0) + (0.12 if tk[0] != "pe" or e != "pe" else 0.0))
        if e == "act" and aset is not None and aset != self.actset:
            t += 1.3
        return t

    def op(self, e, fn, reads=(), writes=(), expect=(), dur=None, aset=None):
        reads, writes = self._flat(reads), self._flat(writes)
        for (k_, t_) in expect:
            assert self.lastw.get(k_) == t_, f"emission-order violation on {k_}"
        t0 = self.est_start(e, reads, writes, aset)
        if e == "act" and aset is not None:
            self.actset = aset
        if e == "pool":
            for t in getattr(self, "fence_toks", []):
                self._wait(e, t)
        self._deps(e, reads, writes)
        inst = fn(self.engs[e])
        self.cnt[e] += 1
        self.ninst += 1
        inst.then_inc(self.sem[e], 1)
        tok = (e, self.cnt[e], self.sem[e])
        self.efree[e] = t0 + (dur if dur is not None else {"pe": 0.25, "act": 0.6, "dve": 0.65, "pool": 1.2}.get(e, 0.5))
        self.tfin[(e, self.cnt[e])] = self.efree[e]
        self._commit(tok, reads, writes)
        return tok

    def dma(self, e, out, in_, reads=(), writes=(), scoped=False, dma_us=None):
        reads, writes = self._flat(reads), self._flat(writes)
        if e == "sp" or scoped:
            for t in getattr(self, "fence_toks", []):
                self._wait(e, t)
        self._deps(e, reads, writes)
        rk = "poolw" if (e == "pool" and dma_us is not None) else e
        lo, hi = self.drange[rk]
        i = self.dnext[rk]
        self.dnext[rk] = lo + (i + 1 - lo) % (hi - lo)
        if self.dcnt[i] > 0:
            self._wait(e, (("d", i), self.dcnt[i], self.dsems[i]))
        t0 = max(self.efree.values())
        for tk in self._dep_toks(reads, writes):
            t0 = max(t0, self.tfin.get((tk[0], tk[1]), 0.0))
        self.engs[e].dma_start(out=out, in_=in_).then_inc(self.dsems[i], 16)
        self.ninst += 1
        self.dcnt[i] += 16
        tok = (("d", i), self.dcnt[i], self.dsems[i])
        self.tfin[(("d", i), self.dcnt[i])] = t0 + (dma_us if dma_us is not None else 3.0)
        self._commit(tok, reads, writes)
        return tok

    def fence(self):
        toks = [(k, self.cnt[k], self.sem[k]) for k in ("pe", "dve", "act", "pool") if self.cnt[k] > 0]
        wlo, whi = self.drange["poolw"]
        toks += [(("d", i), self.dcnt[i], self.dsems[i]) for i in range(len(self.dsems)) if self.dcnt[i] > 0 and not (wlo <= i < whi)]
        for e in ("pe", "dve", "act"):
            for t in toks:
                self._wait(e, t)
        self.fence_toks = toks

    def finish(self, e="sp"):
        for k in self.engs:
            if self.cnt[k] > 0:
                self._wait(e, (k, self.cnt[k], self.sem[k]))
        for i, s in enumerate(self.dsems):
            if self.dcnt[i] > 0:
                self._wait(e, (("d", i), self.dcnt[i], s))


def apx(base, free):
    return bass.AP(base.tensor, base.offset, [list(base.ap[0])] + [list(f) for f in free])


class Pack:
    def __init__(self):
        self.cols = {}
        self.parts = []
        self.n = 0

    def add(self, name, arr):
        arr = np.ascontiguousarray(arr, dtype=np.float32).reshape(128, -1)
        self.cols[name] = (self.n, arr.shape[1])
        self.parts.append(arr)
        self.n += arr.shape[1]

    def build(self):
        return np.ascontiguousarray(np.concatenate(self.parts, axis=1))


def fm(vec, nch):
    return np.asarray(vec, np.float32).reshape(nch, 128).T


def cpk_layout():
    p = Pack()
    z = lambda n: np.zeros((128, n), np.float32)
    p.add("cvec", z(8))
    p.add("fng", z(8))
    p.add("lbl", z(L * 4))
    for l in range(L):
        p.add(f"adab{l}", z(48))
        p.add(f"nmg{l}", z(8))
        p.add(f"nfg{l}", z(8))
        p.add(f"psc{l}", z(8))
        p.add(f"caw{l}", z(62))
        p.add(f"cab{l}", z(2))
        p.add(f"lag{l}", z(2))
        p.add(f"lab{l}", z(2))
        p.add(f"hng{l}", z(2))
        p.add(f"ccw{l}", z(6))
    return p.cols, p.n


def make_cpk(inp, cvec):
    p = Pack()
    p.add("cvec", fm(cvec, 8))
    p.add("fng", fm(inp["final_norm_g"], 8))
    lbl = np.asarray(inp["hgrn_lb_logits"], np.float32)
    p.add("lbl", lbl.reshape(L, 2, 2, 128).transpose(3, 0, 1, 2).reshape(128, L * 4))
    for l in range(L):
        p.add(f"adab{l}", fm(inp["ada_b"][l], 48))
        p.add(f"nmg{l}", fm(inp["norm_mix_g"][l], 8))
        p.add(f"nfg{l}", fm(inp["norm_ffn_g"][l], 8))
        p.add(f"psc{l}", fm(inp["pool_scale"][l], 8))
        caw = np.asarray(inp["conv_a_w"][l], np.float32)
        p.add(f"caw{l}", caw.reshape(31, 2, 128).transpose(2, 1, 0).reshape(128, 62))
        p.add(f"cab{l}", fm(inp["conv_a_b"][l], 2))
        p.add(f"lag{l}", fm(inp["ln_a_g"][l], 2))
        p.add(f"lab{l}", fm(inp["ln_a_b"][l], 2))
        p.add(f"hng{l}", fm(inp["hgrn_norm_g"][l], 2))
        ccw = np.asarray(inp["conv_c_w"][l], np.float32)
        p.add(f"ccw{l}", ccw.reshape(3, 2, 128).transpose(2, 1, 0).reshape(128, 6))
    return p.build()


def make_mpk(is_sample):
    p = Pack()
    rep = lambda v: np.broadcast_to(np.asarray(v, np.float32).reshape(1, -1), (128, np.asarray(v).size))
    s = 1.0 if is_sample else 0.0
    p.add("keep", rep([s]))
    p.add("fs", rep([s]))
    p.add("epsD", rep([EPS]))
    pos = np.arange(256)
    p.add("mCL", rep((pos != 0) * (1.0 - s)))
    p.add("mCR", rep((pos != 255) * (1.0 - s)))
    blk = np.arange(1, 32)
    p.add("FL", rep((blk % 4 != 0) * (1.0 - s)))
    blk0 = np.arange(0, 31)
    p.add("FR", rep((blk0 % 4 != 3) * (1.0 - s)))
    icp = np.zeros((128, 2, 256), np.float32)
    ics = np.zeros((128, 2, 32), np.float32)
    for ch in range(2):
        for half in range(2):
            w = (2, 4, 8, 16)[2 * ch + half]
            for Lseq, tab in ((256, icp), (32, ics)):
                t = np.arange(Lseq)
                lo = np.clip(t - w // 2, 0, Lseq)
                hi = np.clip(t + w - w // 2, 0, Lseq)
                tab[64 * half:64 * half + 64, ch, :] = (1.0 / (hi - lo).astype(np.float32))[None, :]
    p.add("icp", icp.reshape(128, 512) * (1.0 - s))
    p.add("ics", ics.reshape(128, 64) * s)
    p.add("rmask", rep((np.arange(512) % 32 != 0) * 1.0))
    sidx = np.arange(128)[:, None]
    tidx = np.arange(128)[None, :]
    same = (sidx // 32) == (tidx // 32)
    p.add("triF", (same & (sidx <= tidx)).astype(np.float32))
    p.add("triB", (same & (sidx >= tidx)).astype(np.float32))
    p.add("cmask", ((np.arange(128)[:, None] // 32) == np.arange(4)[None, :]).astype(np.float32))
    return p.build(), p.cols


def make_cbf():
    ident = np.eye(128, dtype=np.float32)
    ones = np.ones((128, 128), np.float32)
    blk = np.zeros((128, 128), np.float32)
    blk[:64, :64] = 1.0
    blk[64:, 64:] = 1.0
    return np.ascontiguousarray(np.concatenate([ident, ones, blk], axis=1))


class _Stop(Exception):
    pass


class OP:
    __slots__ = ("e", "fn", "reads", "writes", "expect", "n", "k", "aset")

    def __init__(self, e, fn, reads=(), writes=(), expect=(), n=512, k=None, aset=None):
        self.e, self.fn, self.reads, self.writes, self.expect, self.n, self.k, self.aset = e, fn, reads, writes, expect, n, k, aset

    def dur(self):
        if self.e == "pe":
            return max(0.06, self.n / 2300.0)
        if self.e == "act":
            return 0.22 + self.n * 0.00075
        if self.e == "dve":
            if self.k == "recip":
                return self.n * 0.0065
            d = 0.1 + self.n * 0.0011
            return d * 1.9 if self.k == "scan" else d
        return 0.1 + self.n * 0.0019


def run_sched(S, *gens, slack=0.05):
    pend = []
    for g_ in gens:
        if g_ is None:
            continue
        try:
            pend.append([g_, next(g_)])
        except StopIteration:
            pass
    while pend:
        best, bt = None, None
        for i_, (g_, o) in enumerate(pend):
            t = S.est_start(o.e, o.reads, o.writes, o.aset)
            if t <= S.efree[o.e] + slack:
                best = i_
                break
            if bt is None or t < bt - 1e-9:
                best, bt = i_, t
        g_, o = pend[best]
        tok = S.op(o.e, o.fn, o.reads, o.writes, o.expect, dur=o.dur(), aset=o.aset)
        try:
            pend[best][1] = g_.send(tok)
        except StopIteration:
            pend.pop(best)


def run_threads(*gens):
    live = [g_ for g_ in gens if g_ is not None]
    while live:
        for g_ in list(live):
            try:
                next(g_)
            except StopIteration:
                live.remove(g_)


def build_program(n_layers=L, taps=(), stop_after=None):
    nc = bass.Bass("TRN2", target_bir_lowering=False)

    def chk(name):
        if stop_after == name:
            raise _Stop()
    CC, NCPK = cpk_layout()
    _, MC = make_mpk(True)
    NMPK = sum(v[1] for v in MC.values())

    def din(name, shape):
        return nc.dram_tensor(name, shape, F32, kind="ExternalInput").ap()

    xT_d = din("xT", [D, NT])
    cpk_d = din("cpk", [128, NCPK])
    mpk_d = din("mpk", [128, NMPK])
    cbf_d = din("cbf", [128, 384])
    s0_d = din("s0", [L, 2, 2, 128, 64])
    ada_w = din("ada_w", [L, D, 6 * D])
    w_in = din("w_in", [L, D, N_IN])
    w_oa = din("w_out_a", [L, 256, D])
    w_ob = din("w_out_b", [L, 256, D])
    w_oc = din("w_out_c", [L, 256, D])
    pool_w = din("pool_w", [L, 4, 64, 256])
    w_o = din("w_o", [L, D, D])
    w13 = din("ffn_w13", [L, D, 2 * DFF])
    w2 = din("ffn_w2", [L, DFF, D])
    yT_d = nc.dram_tensor("yT", [D, NT], F32, kind="ExternalOutput").ap()
    st_d = nc.dram_tensor("st", [8, L, 2, 2, 128, 64], F32, kind="ExternalOutput").ap()
    tap_d = {}
    for (tname, shape) in taps:
        tap_d[tname] = nc.dram_tensor("tap_" + tname, list(shape), F32, kind="ExternalOutput").ap()

    S = Sched(nc)
    op = S.op
    uid = [0]

    def U(prefix):
        uid[0] += 1
        return f"{prefix}{uid[0]}"

    with ExitStack() as es0:
        def T(es, name, shape, dt=F32):
            cm = nc.sbuf_tensor(name, list(shape), dt)
            hnd = cm.__enter__()
            es.callback(lambda: cm.__exit__(None, None, None))
            return hnd

        x = T(es0, "x", [128, KC, NT])
        h = T(es0, "h", [128, KC, NT], BF16)
        wp = [T(es0, f"wp{i}", [128, WB], BF16) for i in range(NWB)]
        cpk = T(es0, "cpk_sb", [128, NCPK])
        mpk = T(es0, "mpk_sb", [128, NMPK])
        cbf = T(es0, "cbf_sb", [128, 384], BF16)
        mod = T(es0, "mod", [128, L, 48])
        am = T(es0, "am", [128, L, 8])
        af = T(es0, "af", [128, L, 8])
        lbt = T(es0, "lbt", [128, L, 4])
        omlt = T(es0, "omlt", [128, L, 4])
        cs_bf = T(es0, "cs_bf", [128, 8], BF16)
        ps = [es0.enter_context(nc.psum_tensor(f"ps{i}", [128, 512], F32)) for i in range(7)]
        pst = es0.enter_context(nc.psum_tensor("pst", [128, 1024], BF16))

        def cp(name, a=0, n=None):
            o, w = CC[name]
            n = w - a if n is None else n
            return cpk[:, o + a:o + a + n]

        def mp(name, a=0, n=None):
            o, w = MC[name]
            n = w - a if n is None else n
            return mpk[:, o + a:o + a + n]

        ident = cbf[:, 0:128]
        ones_bf = cbf[:, 128:256]
        blk_bf = cbf[:, 256:384]

        wstate = {"i": 0, "pinned": set()}

        def wload(src_ap, nparts, shape_free, pin=False, buf=None):
            i = wstate["i"] if buf is None else buf
            while i in wstate["pinned"]:
                assert buf is None
                i = (i + 1) % NWB
            wstate["i"] = (i + 1) % NWB
            if pin:
                wstate["pinned"].add(i)
            pieces = src_ap if isinstance(src_ap, list) else [src_ap]
            n = int(np.prod(shape_free))
            assert n * len(pieces) <= WB and len(pieces) <= 4
            keys = [("wp", i, k) for k in range(4)]
            for k, pc in enumerate(pieces):
                dst = wp[i][0:nparts, k * n:(k + 1) * n]
                if len(shape_free) == 2:
                    dst = dst.rearrange("p (a b) -> p a b", b=shape_free[1])
                wk_ = [keys[k]] + (keys[len(pieces):] if k == 0 else [])
                S.dma("pool", dst, pc, writes=wk_, dma_us=2.0 + nparts * n * 4 / 150e3)
            return wp[i], keys

        def wunpin():
            wstate["pinned"].clear()

        def w_kc(dram2d, c0, ncols):
            return dram2d.rearrange("(kc p) n -> p kc n", p=128)[:, :, c0:c0 + ncols]

        class Stream:
            def __init__(self, items):
                self.items = items
                self.loaded = {}
                self.n = 0

            def ensure(self, upto):
                while self.n <= upto and self.n < len(self.items):
                    src, npart, shp = self.items[self.n]
                    self.loaded[self.n] = wload(src, npart, shp)
                    self.n += 1

            def get(self, i, ahead=NWB - 1):
                self.ensure(i + ahead)
                return self.loaded.pop(i)

        zrot = {"i": 0}

        def zbank(lo=0, hi=2):
            k = (lo, hi)
            i = zrot.get(k, lo)
            zrot[k] = lo + (i + 1 - lo) % (hi - lo)
            return i

        def zmm(bank, wt, wkey, woff, M, g, hkeys=True, wstride=None):
            for kc in range(KC):
                lhsT = wt[:, woff(kc):woff(kc) + M]
                op("pe", lambda e: e.matmul(ps[bank][0:M, :], lhsT=lhsT, rhs=h[:, kc, g * GT:(g + 1) * GT],
                                            start=(kc == 0), stop=(kc == KC - 1)),
                   reads=[wkey, ("h", g)], writes=[("ps", bank)])

        S.dma("sp", cpk[:], cpk_d, writes=["cpk"])
        S.dma("sp", mpk[:], mpk_d, writes=["mpk"])
        S.dma("pool", cbf[:], cbf_d, writes=["cbf"])
        def x_load(g):
            for kc in range(KC):
                S.dma("sp", x[:, kc, g * GT:(g + 1) * GT], xT_d[kc * 128:(kc + 1) * 128, g * GT:(g + 1) * GT],
                      writes=[("x", kc, g)])

        x_load(0)
        op("act", lambda e: e.activation(out=cs_bf[:], in_=cp("cvec"), func=AF.Silu), reads=["cpk"], writes=["cs_bf"])
        op("dve", lambda e: e.memset(lbt[:, 0, :], 0.0), writes=["lbt"])
        op("dve", lambda e: e.tensor_tensor(out=lbt[:, 1, :], in0=cp("lbl", 4, 4), in1=cp("lbl", 0, 4), op=ALU.subtract),
           reads=["cpk"], writes=["lbt"])
        op("act", lambda e: e.activation(out=lbt[:, 1, :], in_=lbt[:, 1, :], func=AF.Sigmoid), reads=["lbt"], writes=["lbt"])
        op("dve", lambda e: e.tensor_scalar(out=omlt[:].rearrange("p a b -> p (a b)"), in0=lbt[:].rearrange("p a b -> p (a b)"),
                                            scalar1=-1.0, scalar2=1.0, op0=ALU.mult, op1=ALU.add),
           reads=["lbt"], writes=["omlt"])
        MODPARTS = {"sh_m": (0, 1), "sc_m": (2, 3), "g_m": (4, 5), "sh_f": (6, 7), "sc_f": (8, 9), "g_f": (10, 11)}

        def modkeys(l, part):
            return [("mod", l, b_) for b_ in MODPARTS[part]]

        mod_pending = {}

        def mod_issue(l, bi):
            mod_pending[(l, bi)] = wload(w_kc(ada_w[l], bi * 512, 512), 128, (8, 512))

        def mod_block(l, bi):
            if (l, bi) not in mod_pending:
                mod_issue(l, bi)
            wt, wkey = mod_pending.pop((l, bi))
            bank = zbank()
            for jj in range(4):
                for kc in range(KC):
                    lhsT = wt[:, kc * 512 + jj * 128: kc * 512 + jj * 128 + 128]
                    op("pe", lambda e: e.matmul(ps[bank][:, jj:jj + 1], lhsT=lhsT, rhs=cs_bf[:, kc:kc + 1],
                                                start=(kc == 0), stop=(kc == KC - 1)),
                       reads=[wkey, "cs_bf"], writes=[("ps", bank)], dur=0.07)
            o_, _ = CC[f"adab{l}"]
            op("dve", lambda e: e.tensor_tensor(out=mod[:, l, bi * 4:(bi + 1) * 4], in0=ps[bank][:, 0:4],
                                                in1=cpk[:, o_ + bi * 4:o_ + bi * 4 + 4], op=ALU.add),
               reads=[("ps", bank), "cpk"], writes=[("mod", l, bi)], dur=0.1)

        def mod_fin(l, which):
            dst, sc0, gname, part = (am, 8, f"nmg{l}", "sc_m") if which == "am" else (af, 32, f"nfg{l}", "sc_f")
            op("dve", lambda e: e.scalar_tensor_tensor(out=dst[:, l, :], in0=mod[:, l, sc0:sc0 + 8], scalar=1.0,
                                                       in1=cp(gname), op0=ALU.add, op1=ALU.mult),
               reads=modkeys(l, part) + ["cpk"], writes=[(which, l)], dur=0.1)

        mod_issue(0, 0)
        mod_issue(0, 1)
        mod_issue(0, 2)
        for bi in range(4):
            mod_block(0, bi)
            if bi == 0:
                mod_issue(0, 3)
        mod_fin(0, "am")
        mod_deferred = [(0, bi) for bi in range(4, 12)] + ([(1, bi) for bi in range(12)] if n_layers > 1 else [])

        def mod_step(n=1, prefetch=True):
            for _ in range(n):
                if not mod_deferred:
                    return
                l_, bi_ = mod_deferred.pop(0)
                mod_block(l_, bi_)
                if mod_deferred and prefetch:
                    mod_issue(*mod_deferred[0])
                if bi_ == 9:
                    mod_fin(l_, "af")
                if bi_ == 3:
                    mod_fin(l_, "am")

        epsD = mp("epsD")

        def rms_to_h(l, avec, shcol, es):
            NSQ, NTMP = 4, 6
            sq = [T(es, U("sq"), [128, GT], BF16) for _ in range(NSQ)]
            rstd = [T(es, U("rstd"), [128, GT]) for _ in range(NG)]
            tmp = [T(es, U("ntmp"), [128, GT]) for _ in range(NTMP)]
            rtok = {}
            SQ_ENG = ["act", "pool", "pool", "act", "pool", "dve", "act", "pool"]
            AF_ENG = ["act", "act", "pool", "act", "act", "pool", "act", "pool"]
            bank = 5

            def stat(g):
                gs = slice(g * GT, (g + 1) * GT)
                for kc in range(KC):
                    sqt = sq[kc % NSQ]
                    xin = x[:, kc, gs]
                    if SQ_ENG[kc] == "act":
                        yield OP("act", lambda e: e.activation(out=sqt[:], in_=xin, func=AF.Square),
                                 reads=[("x", kc, g)], writes=[sqt.name])
                    else:
                        yield OP(SQ_ENG[kc], lambda e: e.tensor_tensor(out=sqt[:], in0=xin, in1=xin, op=ALU.mult),
                                 reads=[("x", kc, g)], writes=[sqt.name])
                    yield OP("pe", lambda e: e.matmul(ps[bank][:], lhsT=ones_bf, rhs=sqt[:], start=(kc == 0), stop=(kc == KC - 1)),
                             reads=[sqt.name, "cbf"], writes=[("ps", bank)], n=512)
                rt = rstd[g]
                yield OP("act", lambda e: e.activation(out=rt[:], in_=ps[bank][:], func=AF.Ln, bias=epsD, scale=1.0 / D),
                         reads=[("ps", bank), "mpk"], writes=[rt.name], aset="lnexp")
                rtok[g] = yield OP("act", lambda e: e.activation(out=rt[:], in_=rt[:], func=AF.Exp, scale=-0.5),
                                   reads=[rt.name], writes=[rt.name], aset="lnexp")

            def apply(g):
                gs = slice(g * GT, (g + 1) * GT)
                rt = rstd[g]
                for kc in range(KC):
                    tt = tmp[(g * KC + kc) % NTMP]
                    yield OP("dve", lambda e: e.tensor_tensor(out=tt[:], in0=x[:, kc, gs], in1=rt[:], op=ALU.mult),
                             reads=[("x", kc, g), rt.name], writes=[tt.name], expect=[(rt.name, rtok[g])])
                    en = AF_ENG[kc]
                    if shcol is None:
                        if en == "act":
                            yield OP("act", lambda e: e.activation(out=tt[:], in_=tt[:], func=AF.Identity, scale=avec[:, kc:kc + 1]),
                                     reads=[tt.name, "cpk"], writes=[tt.name])
                        else:
                            yield OP(en, lambda e: e.tensor_scalar(out=tt[:], in0=tt[:], scalar1=avec[:, kc:kc + 1], scalar2=0.0,
                                                                   op0=ALU.mult, op1=ALU.add),
                                     reads=[tt.name, "cpk"], writes=[tt.name])
                        S.dma("sp", yT_d[kc * 128:(kc + 1) * 128, gs], tt[:], reads=[tt.name])
                    else:
                        rd = [tt.name, ("am", l), ("af", l)] + modkeys(l, "sh_m") + modkeys(l, "sh_f")
                        if en == "act":
                            yield OP("act", lambda e: e.activation(out=h[:, kc, gs], in_=tt[:], func=AF.Identity,
                                                                   bias=mod[:, l, shcol + kc:shcol + kc + 1], scale=avec[:, kc:kc + 1]),
                                     reads=rd, writes=[("h", g)])
                        else:
                            yield OP(en, lambda e: e.tensor_scalar(out=h[:, kc, gs], in0=tt[:], scalar1=avec[:, kc:kc + 1],
                                                                   scalar2=mod[:, l, shcol + kc:shcol + kc + 1],
                                                                   op0=ALU.mult, op1=ALU.add),
                                     reads=rd, writes=[("h", g)])

            run_sched(S, stat(0))
            for g in range(NG):
                run_sched(S, stat(g + 1) if g + 1 < NG else None, apply(g))

        def tap(name, src_ap, keys):
            if name in tap_d:
                S.dma("pool", tap_d[name], src_ap, reads=keys, scoped=True)

        for l in range(n_layers if stop_after != "prep" else 0):
          try:
            with ExitStack() as esl:
                S.fence()
                ob_out = T(esl, U("ob_out"), [128, 2, NT], BF16)
                pre = {}
                wcol = lambda nm, p_: w_kc(w_in[l], OFF[nm] + p_ * 128, 128)
                pre["BA0"] = wload([wcol("q", 0), wcol("f_fw", 0), wcol("f_bw", 0)], 128, (8, 128), buf=0)
                pre["BI"] = wload([wcol("i", 0), wcol("g", 0), wcol("i", 1), wcol("g", 1)], 128, (8, 128), buf=1)
                pre["BA1"] = wload([wcol("q", 1), wcol("f_fw", 1), wcol("f_bw", 1)], 128, (8, 128), buf=2)
                with ExitStack() as es:
                    S.fence()
                    if l == 0:
                        for g_ in range(1, NG):
                            x_load(g_)
                    rms_to_h(l, am[:, l, :], 0, es)
                chk("norm1")
                if l == 0:
                    tap("h0", h[:], [("h", g_) for g_ in range(NG)])

                with ExitStack() as es:
                    S.fence()
                    v_tm = T(es, U("v_tm"), [128, 16, 128], BF16)
                    o_acc = T(es, U("o_acc"), [128, NT])
                    f_t = T(es, U("f_t"), [128, GT])
                    k_t = T(es, U("k_t"), [128, GT])
                    b_t = T(es, U("b_t"), [128, GT])
                    bq_t = T(es, U("bq_t"), [128, GT])
                    kp32 = T(es, U("kp32"), [128, GT])
                    e1s = [T(es, U("e1_t"), [128, GT]) for _ in range(2)]
                    qps = [T(es, U("qp"), [128, GT], BF16) for _ in range(3)]
                    kps = [T(es, U("kp"), [128, GT], BF16) for _ in range(3)]
                    khs = [T(es, U("kh"), [128, GT], BF16) for _ in range(2)]
                    sqb = T(es, U("sqb"), [128, GT], BF16)
                    Sbuf = T(es, U("Sbuf"), [128, 16, 128])
                    Sbfs = [T(es, U("Sbf"), [128, 16, 128], BF16) for _ in range(2)]
                    Sk = [T(es, U("Sk"), [128, 128]) for _ in range(2)]
                    Skb = [T(es, U("Skb"), [128, 128], BF16) for _ in range(4)]
                    s0b = T(es, U("s0b"), [128, 4, 128], BF16)
                    khT = [T(es, U("khT"), [128, 4, 128], BF16) for _ in range(2)]
                    PT = [T(es, U("PT"), [128, 2, 128], BF16) for _ in range(2)]
                    skn = [0]
                    s0t = T(es, U("s0t"), [128, 4, 128])
                    op("dve", lambda e: e.memset(s0t[:].rearrange("p a b -> p (a b)"), 0.0), writes=["s0t"])
                    for dr_ in range(2):
                        for p_ in range(2):
                            S.dma("sp", s0t[0:64, dr_ * 2 + p_, 0:64], s0_d[l, dr_, p_, 0:64, :], writes=[("s0t", dr_, p_, 0)], reads=["s0t"])
                            S.dma("sp", s0t[64:128, dr_ * 2 + p_, 64:128], s0_d[l, dr_, p_, 64:128, :], writes=[("s0t", dr_, p_, 1)], reads=["s0t"])
                    s0b_tok = op("act", lambda e: e.activation(out=s0b[:].rearrange("p a b -> p (a b)"), in_=s0t[:].rearrange("p a b -> p (a b)"), func=AF.Copy),
                                 reads=["s0t"] + [("s0t", a_, b_, c_) for a_ in range(2) for b_ in range(2) for c_ in range(2)], writes=["s0b"])
                    PB = 0
                    KVB = (1, 4)
                    for p in range(2):
                        cols1 = [OFF["q"] + p * 128, OFF["f_fw"] + p * 128, OFF["f_bw"] + p * 128]
                        cols2 = [OFF["i"] + p * 128, OFF["g"] + p * 128]
                        wA, kA = pre["BA0"] if p == 0 else pre["BA1"]
                        wBb, kB = pre["BI"]
                        wq = lambda kc: (0 * 8 + kc) * 128
                        wf = [lambda kc: (1 * 8 + kc) * 128, lambda kc: (2 * 8 + kc) * 128]
                        wi = lambda kc, p=p: ((2 * p) * 8 + kc) * 128
                        wg = lambda kc, p=p: ((2 * p + 1) * 8 + kc) * 128
                        for tq in range(4):
                            bank = zbank()
                            for ti in range(4):
                                tt = tq * 4 + ti
                                for kc in range(KC):
                                    op("pe", lambda e: e.matmul(ps[bank][:, ti * 128:(ti + 1) * 128],
                                                                lhsT=h[:, kc, tt * 128:(tt + 1) * 128],
                                                                rhs=wBb[:, wi(kc):wi(kc) + 128],
                                                                start=(kc == 0), stop=(kc == KC - 1)),
                                       reads=[kB, ("h", tq)], writes=[("ps", bank)])
                            op("act", lambda e: e.activation(out=v_tm[:, tq * 4:(tq + 1) * 4, :].rearrange("p a b -> p (a b)"),
                                                             in_=ps[bank][:], func=AF.Copy),
                               reads=[("ps", bank)], writes=[("v_tm", tq)])
                        jobs = [(0, g_) for g_ in (0, 1, 2, 3)] + [(1, g_) for g_ in (3, 2, 1, 0)]
                        NJ = len(jobs)
                        chain = {}
                        s_start = {}

                        ptok = {}

                        def zmm_g(bank, wt, wkey, woff, g):
                            for kc in range(KC):
                                lhsT = wt[:, woff(kc):woff(kc) + 128]
                                yield OP("pe", lambda e: e.matmul(ps[bank][:], lhsT=lhsT, rhs=h[:, kc, g * GT:(g + 1) * GT],
                                                                  start=(kc == 0), stop=(kc == KC - 1)),
                                         reads=[wkey, ("h", g)], writes=[("ps", bank)], n=512)

                        def prep(k):
                            dr, g = jobs[k]
                            e1_t, qp, kp, kh = e1s[k % 2], qps[k % 3], kps[k % 3], khs[k % 2]
                            lbc = lbt[:, l, dr * 2 + p:dr * 2 + p + 1]
                            omc = omlt[:, l, dr * 2 + p:dr * 2 + p + 1]
                            endc = 31 if dr == 0 else 0
                            tk = ptok[k] = {}
                            yield from zmm_g(PB, wA, kA, wf[dr], g)
                            yield OP("act", lambda e: e.activation(out=f_t[:], in_=ps[PB][:], func=AF.Sigmoid),
                                     reads=[("ps", PB)], writes=["f_t"], aset="sig")
                            yield from zmm_g(PB, wA, kA, wq, g)
                            yield OP("dve", lambda e: e.tensor_scalar(out=f_t[:], in0=f_t[:], scalar1=omc, scalar2=lbc,
                                                                      op0=ALU.mult, op1=ALU.add),
                                     reads=["f_t", "lbt", "omlt"], writes=["f_t"])
                            yield OP("dve", lambda e: e.tensor_scalar(out=k_t[:], in0=f_t[:], scalar1=-1.0, scalar2=1.0, op0=ALU.mult, op1=ALU.add),
                                     reads=["f_t"], writes=["k_t"])
                            yield OP("act", lambda e: e.activation(out=f_t[:], in_=f_t[:], func=AF.Ln), reads=["f_t"], writes=["f_t"], aset="lnexp")
                            yield OP("dve", lambda e: e.tensor_tensor_scan(out=b_t[:], data0=mp("rmask"), data1=f_t[:], initial=0.0,
                                                                           op0=ALU.mult, op1=ALU.add),
                                     reads=["f_t", "mpk"], writes=["b_t"], k="scan")
                            if dr == 0:
                                bb, bbk = b_t, "b_t"
                            else:
                                yield OP("dve", lambda e: e.tensor_tensor(out=f_t[:], in0=f_t[:], in1=b_t[:], op=ALU.subtract),
                                         reads=["f_t", "b_t"], writes=["f_t"])
                                yield OP("dve", lambda e: e.tensor_tensor(out=bq_t[:].rearrange("p (c j) -> p c j", j=32),
                                                                          in0=f_t[:].rearrange("p (c j) -> p c j", j=32),
                                                                          in1=apx(b_t[:, 31:32], [[32, 16], [0, 32]]), op=ALU.add),
                                         reads=["f_t", "b_t"], writes=["bq_t"])
                                bb, bbk = bq_t, "bq_t"
                            tk[e1_t.name] = yield OP("act", lambda e: e.activation(out=e1_t[:], in_=bb[:], func=AF.Exp),
                                                     reads=[bbk], writes=[e1_t.name], aset="lnexp")
                            yield OP("act", lambda e: e.activation(out=kp32[:], in_=bb[:], func=AF.Exp, scale=-1.0),
                                     reads=[bbk], writes=["kp32"], aset="lnexp")
                            tk[qp.name] = yield OP("dve", lambda e: e.tensor_tensor(out=qp[:], in0=ps[PB][:], in1=e1_t[:], op=ALU.mult),
                                                   reads=[("ps", PB), e1_t.name], writes=[qp.name])
                            yield OP("dve", lambda e: e.tensor_tensor(out=kp32[:], in0=kp32[:], in1=k_t[:], op=ALU.mult),
                                     reads=["kp32", "k_t"], writes=["kp32"])
                            tk[kp.name] = yield OP("act", lambda e: e.activation(out=kp[:], in_=kp32[:], func=AF.Copy),
                                                   reads=["kp32"], writes=[kp.name])
                            tk[kh.name] = yield OP("dve", lambda e: e.tensor_tensor(out=kh[:].rearrange("p (c j) -> p c j", j=32),
                                                                                    in0=kp32[:].rearrange("p (c j) -> p c j", j=32),
                                                                                    in1=apx(e1_t[:, endc:endc + 1], [[32, 16], [0, 32]]), op=ALU.mult),
                                                   reads=["kp32", e1_t.name], writes=[kh.name])

                        def kv(k):
                            dr, g = jobs[k]
                            e1_t, kh, Sbf = e1s[k % 2], khs[k % 2], Sbfs[k % 2]
                            x_e1 = [(e1_t.name, ptok[k][e1_t.name])]
                            x_kh = [(kh.name, ptok[k][kh.name])]
                            endc = 31 if dr == 0 else 0
                            if dr not in chain:
                                chain[dr] = dict(Sprev=s0t[:, dr * 2 + p, :], Sprev_key=[("s0t", dr, p, 0), ("s0t", dr, p, 1)],
                                                 Sprev_b=s0b[:, dr * 2 + p, :], Sprev_bkey="s0b", Sprev_btok=s0b_tok, first=True)
                            cs = chain[dr]
                            tiles = [0, 1, 2, 3] if dr == 0 else [3, 2, 1, 0]
                            s_start[k] = {}
                            for ti in tiles:
                                tt = g * 4 + ti
                                yield OP("pe", lambda e: e.transpose(pst[:, ti * 128:(ti + 1) * 128], kh[:, ti * 128:(ti + 1) * 128], ident),
                                         reads=[kh.name, "cbf"], writes=[("pst", ti)], expect=x_kh, n=128)
                                kt = khT[ti % 2]
                                yield OP("dve", lambda e: e.tensor_tensor(out=kt[:], in0=apx(pst[:, ti * 128:ti * 128 + 1], [[0, 4], [1, 128]]),
                                                                          in1=apx(mp("cmask"), [[1, 4], [0, 128]]), op=ALU.mult),
                                         reads=[("pst", ti), "mpk"], writes=[kt.name], n=512)
                                kvb = KVB[ti % 2]
                                for j in range(4):
                                    yield OP("pe", lambda e: e.matmul(ps[kvb][:, j * 128:(j + 1) * 128], lhsT=kt[:, j, :],
                                                                      rhs=v_tm[:, tt, :], start=True, stop=True),
                                             reads=[kt.name, ("v_tm", tt // 4)], writes=[("ps", kvb)], n=128)
                                chunks = [0, 1, 2, 3] if dr == 0 else [3, 2, 1, 0]
                                for j in chunks:
                                    c = tt * 4 + j
                                    seg_start = (c % 8 == 0) if dr == 0 else (c % 8 == 7)
                                    if seg_start and not cs["first"]:
                                        seg_prev = (c // 8 - 1) if dr == 0 else (c // 8 + 1)
                                        Sprev, Sprev_key = cs["Sprev"], cs["Sprev_key"]
                                        S.dma("sp", st_d[seg_prev, l, dr, p, 0:64, :], Sprev[0:64, 0:64], reads=[Sprev_key])
                                        S.dma("sp", st_d[seg_prev, l, dr, p, 64:128, :], Sprev[64:128, 64:128], reads=[Sprev_key])
                                        skt = Sk[skn[0] % 2]
                                        sktb = Skb[skn[0] % 4]
                                        skn[0] += 1
                                        yield OP("dve", lambda e: e.tensor_scalar(out=skt[:], in0=Sprev, scalar1=mp("keep"), scalar2=0.0, op0=ALU.mult, op1=ALU.add),
                                                 reads=[Sprev_key, "mpk"], writes=[skt.name], n=128)
                                        cs["Sprev"], cs["Sprev_key"] = skt[:], skt.name
                                        tok_ = yield OP("act", lambda e: e.activation(out=sktb[:], in_=skt[:], func=AF.Copy),
                                                        reads=[skt.name], writes=[sktb.name], n=128)
                                        cs["Sprev_b"], cs["Sprev_bkey"], cs["Sprev_btok"] = sktb[:], sktb.name, tok_
                                    cs["first"] = False
                                    s_start[k][(ti, j)] = (cs["Sprev_b"], cs["Sprev_bkey"], cs["Sprev_btok"])
                                    dcol = ti * 128 + j * 32 + endc
                                    slot = ti * 4 + j
                                    sp_ap, sp_key = cs["Sprev"], cs["Sprev_key"]
                                    yield OP("dve", lambda e: e.scalar_tensor_tensor(out=Sbuf[:, slot, :], in0=sp_ap, scalar=e1_t[:, dcol:dcol + 1],
                                                                                     in1=ps[kvb][:, j * 128:(j + 1) * 128], op0=ALU.mult, op1=ALU.add),
                                             reads=[sp_key, e1_t.name, ("ps", kvb)], writes=[("Sbuf", slot)], expect=x_e1, n=128)
                                    cs["Sprev"], cs["Sprev_key"] = Sbuf[:, slot, :], ("Sbuf", slot)
                                    tok_ = yield OP("act", lambda e: e.activation(out=Sbf[:, slot, :], in_=Sbuf[:, slot, :], func=AF.Copy),
                                                    reads=[("Sbuf", slot)], writes=[(Sbf.name, slot)], n=128)
                                    cs["Sprev_b"], cs["Sprev_bkey"], cs["Sprev_btok"] = Sbf[:, slot, :], (Sbf.name, slot), tok_
                            if k == NJ - 1 or jobs[k + 1][0] != dr:
                                seg_last = 7 if dr == 0 else 0
                                Sprev, Sprev_key = cs["Sprev"], cs["Sprev_key"]
                                S.dma("sp", st_d[seg_last, l, dr, p, 0:64, :], Sprev[0:64, 0:64], reads=[Sprev_key])
                                S.dma("sp", st_d[seg_last, l, dr, p, 64:128, :], Sprev[64:128, 64:128], reads=[Sprev_key])

                        def sc(k):
                            dr, g = jobs[k]
                            qp, kp = qps[k % 3], kps[k % 3]
                            x_qk = [(qp.name, ptok[k][qp.name]), (kp.name, ptok[k][kp.name])]
                            gs = slice(g * GT, (g + 1) * GT)
                            tri = mp("triF") if dr == 0 else mp("triB")
                            tiles = [0, 1, 2, 3] if dr == 0 else [3, 2, 1, 0]
                            obk = (6, 5)
                            for ti in tiles:
                                tt = g * 4 + ti
                                tsl = slice(ti * 128, (ti + 1) * 128)
                                ptt = PT[ti % 2]
                                for hh in range(2):
                                    hp = slice(64 * hh, 64 * hh + 64)
                                    scb = 2 + hh
                                    yield OP("pe", lambda e: e.matmul(ps[scb][:, 0:128], lhsT=kp[hp, tsl], rhs=qp[hp, tsl],
                                                                      start=True, stop=True),
                                             reads=[kp.name, qp.name], writes=[("ps", scb)], expect=x_qk, n=128)
                                    yield OP("dve", lambda e: e.tensor_tensor(out=ptt[:, hh, :], in0=ps[scb][:, 0:128], in1=tri, op=ALU.mult),
                                             reads=[("ps", scb), "mpk"], writes=[(ptt.name, hh)], n=128)
                                for hh in range(2):
                                    hp = slice(64 * hh, 64 * hh + 64)
                                    ob = obk[hh]
                                    yield OP("pe", lambda e: e.matmul(ps[ob][hp, tsl], lhsT=v_tm[:, tt, hp], rhs=ptt[:, hh, :],
                                                                      start=True, stop=False),
                                             reads=[(ptt.name, hh), ("v_tm", tt // 4)], writes=[("ps", ob)], n=128)
                                    for j in range(4):
                                        sap, skey, stok = s_start[k][(ti, j)]
                                        csl = slice(ti * 128 + j * 32, ti * 128 + j * 32 + 32)
                                        yield OP("pe", lambda e: e.matmul(ps[ob][hp, csl], lhsT=sap[hp, hp], rhs=qp[hp, csl],
                                                                          start=False, stop=(j == 3)),
                                                 reads=[skey, qp.name], writes=[("ps", ob)], expect=[(skey, stok)], n=128)
                            for hh in range(2):
                                hp = slice(64 * hh, 64 * hh + 64)
                                ob = obk[hh]
                                if dr == 0:
                                    yield OP("act", lambda e: e.activation(out=o_acc[hp, gs], in_=ps[ob][hp, :], func=AF.Copy),
                                             reads=[("ps", ob)], writes=[("o_acc", g, hh)])
                                else:
                                    yield OP("dve", lambda e: e.tensor_tensor(out=o_acc[hp, gs], in0=o_acc[hp, gs], in1=ps[ob][hp, :], op=ALU.add),
                                             reads=[("ps", ob), ("o_acc", g, hh)], writes=[("o_acc", g, hh)])

                        run_sched(S, prep(0))
                        for step in range(NJ + 1):
                            run_sched(S, prep(step + 1) if step + 1 < NJ else None,
                                      kv(step) if step < NJ else None,
                                      sc(step - 1) if step >= 1 else None)
                        if p == 0:
                            pre["A"] = wload(w_kc(w_in[l], 0, 512), 128, (8, 512), pin=True, buf=0)
                        rst = [f_t, k_t, b_t, bq_t]
                        rstk = ["f_t", "k_t", "b_t", "bq_t"]
                        sqbs = [sqb, khs[0]]
                        sils = [kp32, e1s[0]]
                        silk = ["kp32", e1s[0].name]
                        ptok2 = {}

                        def post_stat(th):
                            bank = 5 if th == 0 else 6
                            sq_ = sqbs[th]
                            for g in (th, th + 2):
                                gs = slice(g * GT, (g + 1) * GT)
                                rg, rgk = rst[g], rstk[g]
                                yield OP("act", lambda e: e.activation(out=sq_[:], in_=o_acc[:, gs], func=AF.Square),
                                         reads=[("o_acc", g, 0), ("o_acc", g, 1)], writes=[sq_.name])
                                yield OP("pe", lambda e: e.matmul(ps[bank][:], lhsT=blk_bf, rhs=sq_[:], start=True, stop=True),
                                         reads=[sq_.name, "cbf"], writes=[("ps", bank)], n=512)
                                yield OP("act", lambda e: e.activation(out=rg[:], in_=ps[bank][:], func=AF.Ln, bias=epsD, scale=1.0 / 64),
                                         reads=[("ps", bank), "mpk"], writes=[rgk], aset="lnexp")
                                yield OP("act", lambda e: e.activation(out=rg[:], in_=rg[:], func=AF.Exp, scale=-0.5),
                                         reads=[rgk], writes=[rgk], aset="lnexp")
                                ptok2[g] = yield OP("dve", lambda e: e.tensor_tensor(out=rg[:], in0=rg[:], in1=o_acc[:, gs], op=ALU.mult),
                                                    reads=[rgk, ("o_acc", g, 0), ("o_acc", g, 1)], writes=[rgk])

                        def post_gate(th):
                            bank = th
                            sl_, slk = sils[th], silk[th]
                            for g in (th, th + 2):
                                gs = slice(g * GT, (g + 1) * GT)
                                rg, rgk = rst[g], rstk[g]
                                yield from zmm_g(bank, wBb, kB, wg, g)
                                yield OP("act", lambda e: e.activation(out=sl_[:], in_=ps[bank][:], func=AF.Silu),
                                         reads=[("ps", bank)], writes=[slk], aset="silu")
                                yield OP("dve", lambda e: e.scalar_tensor_tensor(out=ob_out[:, p, gs], in0=rg[:], scalar=cp(f"hng{l}", p, 1), in1=sl_[:],
                                                                                 op0=ALU.mult, op1=ALU.mult),
                                         reads=[rgk, slk, "cpk"], writes=[("ob_out", p, g)], expect=[(rgk, ptok2[g])])

                        run_sched(S, post_stat(0), post_stat(1))
                        run_sched(S, post_gate(0), post_gate(1))
                chk("B")
                if l == 0:
                    tap("ob0", ob_out[:], [("ob_out", p_, g_) for p_ in range(2) for g_ in range(NG)])

                ua_out = T(esl, U("ua_out"), [128, 2, NT], BF16)
                uc_out = T(esl, U("uc_out"), [128, 2, NT], BF16)
                ud_out = T(esl, U("ud_out"), [128, 2, NT], BF16)

                with ExitStack() as es:
                  S.fence()
                  acc = T(es, U("acc"), [128, 2, NT])
                  with ExitStack() as es2:
                    upad = T(es2, U("upad"), [128, 32, 94])
                    uhi = T(es2, U("uhi"), [128, 32, 94], BF16)
                    ulo = T(es2, U("ulo"), [128, 32, 94], BF16)
                    sgt = T(es2, U("sgt"), [128, GT])
                    ug = T(es2, U("ug"), [128, GT])
                    whb = T(es2, U("whb"), [128, 31], BF16)
                    wlo = T(es2, U("wlo"), [128, 31])
                    dgh = [T(es2, U("dgh"), [128, 128], BF16) for _ in range(4)]
                    dgl = [T(es2, U("dgl"), [128, 128], BF16) for _ in range(4)]
                    op("dve", lambda e: e.memset(upad[:].rearrange("p a b -> p (a b)"), 0.0), writes=["upad"])
                    op("dve", lambda e: e.memset(uhi[:].rearrange("p a b -> p (a b)"), 0.0), writes=["uhi"])
                    op("dve", lambda e: e.memset(ulo[:].rearrange("p a b -> p (a b)"), 0.0), writes=["ulo"])
                    wt, wkey = pre["A"]
                    w1c = cp(f"caw{l}", 31, 31)
                    op("act", lambda e: e.activation(out=whb[:], in_=w1c, func=AF.Copy), reads=["cpk"], writes=["whb"])
                    op("dve", lambda e: e.tensor_tensor(out=wlo[:], in0=w1c, in1=whb[:], op=ALU.subtract), reads=["cpk", "whb"], writes=["wlo"])
                    for ch in range(2):
                        for g in range(NG):
                            bank = zbank(0, 4)
                            zmm(bank, wt, wkey, lambda kc: kc * 512 + 256 + ch * 128, 128, g)
                            op("act", lambda e: e.activation(out=sgt[:], in_=ps[bank][:], func=AF.Sigmoid), reads=[("ps", bank)], writes=["sgt"])
                            bank2 = zbank(0, 4)
                            zmm(bank2, wt, wkey, lambda kc: kc * 512 + ch * 128, 128, g)
                            if ch == 0:
                                op("dve", lambda e: e.tensor_tensor(out=upad[:, g * 8:(g + 1) * 8, 15:79],
                                                                    in0=ps[bank2][:].rearrange("p (a b) -> p a b", b=64),
                                                                    in1=sgt[:].rearrange("p (a b) -> p a b", b=64), op=ALU.mult),
                                   reads=[("ps", bank2), "sgt"], writes=["upad"])
                            else:
                                op("dve", lambda e: e.tensor_tensor(out=ug[:], in0=ps[bank2][:], in1=sgt[:], op=ALU.mult),
                                   reads=[("ps", bank2), "sgt"], writes=["ug"])
                                op("act", lambda e: e.activation(out=uhi[:, g * 8:(g + 1) * 8, 15:79],
                                                                 in_=ug[:].rearrange("p (a b) -> p a b", b=64), func=AF.Copy),
                                   reads=["ug"], writes=["uhi"])
                                op("dve", lambda e: e.tensor_tensor(out=ulo[:, g * 8:(g + 1) * 8, 15:79],
                                                                    in0=ug[:].rearrange("p (a b) -> p a b", b=64),
                                                                    in1=uhi[:, g * 8:(g + 1) * 8, 15:79], op=ALU.subtract),
                                   reads=["ug", "uhi"], writes=["ulo"])
                    for ub, ukey in ((upad, "upad"), (uhi, "uhi"), (ulo, "ulo")):
                        op("dve", lambda e: e.tensor_tensor(out=ub[:, 1:32, 0:15], in0=ub[:, 0:31, 64:79],
                                                            in1=apx(mp("FL"), [[1, 31], [0, 15]]), op=ALU.mult),
                           reads=[ukey, "mpk"], writes=[ukey])
                        op("dve", lambda e: e.tensor_tensor(out=ub[:, 0:31, 79:94], in0=ub[:, 1:32, 15:30],
                                                            in1=apx(mp("FR"), [[1, 31], [0, 15]]), op=ALU.mult),
                           reads=[ukey, "mpk"], writes=[ukey])
                    accv = acc[:, 0, :].rearrange("p (a b) -> p a b", b=64)
                    CB = (2, 3, 4, 5)
                    for k in range(31):
                        if k == 0:
                            op("dve", lambda e: e.tensor_scalar(out=accv, in0=upad[:, :, 0:64], scalar1=cp(f"caw{l}", 0, 1),
                                                                scalar2=cp(f"cab{l}", 0, 1), op0=ALU.mult, op1=ALU.add),
                               reads=["upad", "cpk"], writes=[("acc", 0)], dur=2.35)
                        else:
                            op("dve", lambda e: e.scalar_tensor_tensor(out=accv, in0=upad[:, :, k:k + 64], scalar=cp(f"caw{l}", k, 1),
                                                                       in1=accv, op0=ALU.mult, op1=ALU.add),
                               reads=["upad", "cpk", ("acc", 0)], writes=[("acc", 0)], dur=2.35)
                        if k % 3 == 0 and k > 0:
                            mod_step(1, prefetch=(k < 30))
                        dh, dl = dgh[k % 4], dgl[k % 4]
                        op("act", lambda e: e.activation(out=dh[:], in_=ident, func=AF.Copy, scale=cp(f"caw{l}", 31 + k, 1)),
                           reads=["cbf", "cpk"], writes=[dh.name], dur=0.3)
                        op("act", lambda e: e.activation(out=dl[:], in_=ident, func=AF.Copy, scale=wlo[:, k:k + 1]),
                           reads=["cbf", "wlo"], writes=[dl.name], dur=0.3)
                        for g in range(NG):
                            win_hi = uhi[:, g * 8:(g + 1) * 8, k:k + 64]
                            win_lo = ulo[:, g * 8:(g + 1) * 8, k:k + 64]
                            op("pe", lambda e: e.matmul(ps[CB[g]][:], lhsT=dh[:], rhs=win_hi, start=(k == 0), stop=False),
                               reads=[dh.name, "uhi"], writes=[("ps", CB[g])], dur=0.22)
                            op("pe", lambda e: e.matmul(ps[CB[g]][:], lhsT=dh[:], rhs=win_lo, start=False, stop=False),
                               reads=[dh.name, "ulo"], writes=[("ps", CB[g])], dur=0.22)
                            op("pe", lambda e: e.matmul(ps[CB[g]][:], lhsT=dl[:], rhs=win_hi, start=False, stop=(k == 30)),
                               reads=[dl.name, "uhi"], writes=[("ps", CB[g])], dur=0.22)
                    for g in range(NG):
                        op("act", lambda e: e.activation(out=acc[:, 1, g * GT:(g + 1) * GT], in_=ps[CB[g]][:], func=AF.Identity,
                                                         bias=cp(f"cab{l}", 1, 1), scale=1.0),
                           reads=[("ps", CB[g]), "cpk"], writes=[("acc", 1)])
                    wunpin()
                    pre["C1"] = wload(w_kc(w_in[l], OFF["c_b"], 512), 128, (8, 512))
                    pre["C2"] = wload(w_kc(w_in[l], OFF["c_x"], 256), 128, (8, 256))
                  with ExitStack() as es3:
                    S.fence()
                    accb = [T(es3, U("accb"), [128, GT], BF16) for _ in range(4)]
                    sq2 = [T(es3, U("sq2"), [128, GT], BF16) for _ in range(4)]
                    mean = [T(es3, U("mean"), [128, GT]) for _ in range(NG)]
                    rstl = [T(es3, U("rstl"), [128, GT]) for _ in range(NG)]
                    tn = [T(es3, U("tn"), [128, GT]) for _ in range(2)]
                    ltok = {}

                    def ln_stat(th):
                        bs, bq = (4, 5) if th == 0 else (2, 3)
                        for g in (th, th + 2):
                            gs = slice(g * GT, (g + 1) * GT)
                            for ch in range(2):
                                ab, s2 = accb[th * 2 + ch], sq2[th * 2 + ch]
                                yield OP("dve", lambda e: e.tensor_copy(out=ab[:], in_=acc[:, ch, gs]), reads=[("acc", ch)], writes=[ab.name])
                                yield OP("pe", lambda e: e.matmul(ps[bs][:], lhsT=ones_bf, rhs=ab[:], start=(ch == 0), stop=(ch == 1)),
                                         reads=[ab.name, "cbf"], writes=[("ps", bs)], n=512)
                                yield OP("act", lambda e: e.activation(out=s2[:], in_=acc[:, ch, gs], func=AF.Square), reads=[("acc", ch)], writes=[s2.name])
                                yield OP("pe", lambda e: e.matmul(ps[bq][:], lhsT=ones_bf, rhs=s2[:], start=(ch == 0), stop=(ch == 1)),
                                         reads=[s2.name, "cbf"], writes=[("ps", bq)], n=512)
                            mg, rg = mean[g], rstl[g]
                            yield OP("dve", lambda e: e.tensor_scalar(out=mg[:], in0=ps[bs][:], scalar1=1.0 / 256, scalar2=0.0, op0=ALU.mult, op1=ALU.add),
                                     reads=[("ps", bs)], writes=[mg.name])
                            yield OP("dve", lambda e: e.tensor_tensor(out=rg[:], in0=mg[:], in1=mg[:], op=ALU.mult), reads=[mg.name], writes=[rg.name])
                            yield OP("dve", lambda e: e.scalar_tensor_tensor(out=rg[:], in0=ps[bq][:], scalar=1.0 / 256, in1=rg[:],
                                                                             op0=ALU.mult, op1=ALU.subtract),
                                     reads=[("ps", bq), rg.name], writes=[rg.name])
                            yield OP("act", lambda e: e.activation(out=rg[:], in_=rg[:], func=AF.Ln, bias=epsD, scale=1.0),
                                     reads=[rg.name, "mpk"], writes=[rg.name], aset="lnexp")
                            ltok[g] = yield OP("act", lambda e: e.activation(out=rg[:], in_=rg[:], func=AF.Exp, scale=-0.5),
                                               reads=[rg.name], writes=[rg.name], aset="lnexp")

                    def ln_apply(th):
                        t = tn[th]
                        for g in (th, th + 2):
                            gs = slice(g * GT, (g + 1) * GT)
                            mg, rg = mean[g], rstl[g]
                            for ch in range(2):
                                yield OP("dve", lambda e: e.tensor_tensor(out=t[:], in0=acc[:, ch, gs], in1=mg[:], op=ALU.subtract),
                                         reads=[("acc", ch), mg.name], writes=[t.name])
                                yield OP("dve", lambda e: e.tensor_tensor(out=t[:], in0=t[:], in1=rg[:], op=ALU.mult),
                                         reads=[t.name, rg.name], writes=[t.name], expect=[(rg.name, ltok[g])])
                                yield OP("act", lambda e: e.activation(out=ua_out[:, ch, gs], in_=t[:], func=AF.Silu,
                                                                       bias=cp(f"lab{l}", ch, 1), scale=cp(f"lag{l}", ch, 1)),
                                         reads=[t.name, "cpk"], writes=[("ua_out", ch, g)], aset="silu")

                    run_sched(S, ln_stat(0), ln_stat(1))
                    run_sched(S, ln_apply(0), ln_apply(1))
                chk("A")
                if l == 0:
                    tap("ua0", ua_out[:], [("ua_out", p_, g_) for p_ in range(2) for g_ in range(NG)])

                with ExitStack() as es:
                    S.fence()
                    ucps = [T(es, U("ucp"), [128, 64 + NT + 64]) for _ in range(2)]
                    t1 = T(es, U("t1"), [128, NT])
                    acccs = [T(es, U("accc"), [128, NT]) for _ in range(2)]
                    tmpcs = [T(es, U("tmpc"), [128, GT]) for _ in range(2)]
                    for ch in range(2):
                        op("dve", lambda e: e.memset(ucps[ch][:], 0.0), writes=[ucps[ch].name])
                    wt1, wk1 = pre["C1"]
                    wt2, wk2 = pre["C2"]
                    pre["D"] = wload(w_kc(w_in[l], OFF["d"], 256), 128, (8, 256), pin=True)
                    c0 = 64
                    seg = lambda a: a.rearrange("p (s j) -> p s j", j=256)

                    def c_proj(ch, g):
                        ucp, tmpc = ucps[ch], tmpcs[g % 2]
                        bank = zbank(0, 4)
                        zmm(bank, wt1, wk1, lambda kc: kc * 512 + 256 + ch * 128, 128, g)
                        op("act", lambda e: e.activation(out=tmpc[:], in_=ps[bank][:], func=AF.Copy), reads=[("ps", bank)], writes=[tmpc.name])
                        bank2 = zbank(0, 4)
                        zmm(bank2, wt2, wk2, lambda kc: kc * 256 + ch * 128, 128, g)
                        op("dve", lambda e: e.tensor_tensor(out=ucp[:, 64 + g * GT:64 + (g + 1) * GT], in0=ps[bank2][:], in1=tmpc[:], op=ALU.mult),
                           reads=[("ps", bank2), tmpc.name], writes=[ucp.name])

                    def c_final(ch, g):
                        accc = acccs[ch]
                        gs = slice(g * GT, (g + 1) * GT)
                        bank = zbank(0, 4)
                        zmm(bank, wt1, wk1, lambda kc: kc * 512 + ch * 128, 128, g)
                        op("dve", lambda e: e.tensor_tensor(out=uc_out[:, ch, gs], in0=ps[bank][:], in1=accc[:, gs], op=ALU.mult),
                           reads=[("ps", bank), accc.name], writes=[("uc_out", ch, g)])

                    def c_big(ch):
                        ucp, accc = ucps[ch], acccs[ch]
                        uk, ak = ucp.name, accc.name
                        w0 = cp(f"ccw{l}", ch * 3 + 0, 1)
                        w1 = cp(f"ccw{l}", ch * 3 + 1, 1)
                        w2c = cp(f"ccw{l}", ch * 3 + 2, 1)
                        return [
                            lambda: op("dve", lambda e: e.tensor_tensor(out=seg(t1[:]), in0=seg(ucp[:, c0 - 1:c0 - 1 + NT]),
                                                                        in1=apx(mp("mCL"), [[0, 8], [1, 256]]), op=ALU.mult),
                                       reads=[uk, "mpk"], writes=["t1"], dur=2.35),
                            lambda: op("dve", lambda e: e.scalar_tensor_tensor(out=t1[:], in0=ucp[:, c0 - 64:c0 - 64 + NT], scalar=mp("fs"), in1=t1[:],
                                                                               op0=ALU.mult, op1=ALU.add),
                                       reads=[uk, "mpk", "t1"], writes=["t1"], dur=2.35),
                            lambda: op("dve", lambda e: e.tensor_scalar(out=accc[:], in0=ucp[:, c0:c0 + NT], scalar1=w1, scalar2=0.0, op0=ALU.mult, op1=ALU.add),
                                       reads=[uk, "cpk"], writes=[ak], dur=2.35),
                            lambda: op("dve", lambda e: e.scalar_tensor_tensor(out=accc[:], in0=t1[:], scalar=w0, in1=accc[:], op0=ALU.mult, op1=ALU.add),
                                       reads=["t1", "cpk", ak], writes=[ak], dur=2.35),
                            lambda: op("dve", lambda e: e.tensor_tensor(out=seg(t1[:]), in0=seg(ucp[:, c0 + 1:c0 + 1 + NT]),
                                                                        in1=apx(mp("mCR"), [[0, 8], [1, 256]]), op=ALU.mult),
                                       reads=[uk, "mpk", ak], writes=["t1"], dur=2.35),
                            lambda: op("dve", lambda e: e.scalar_tensor_tensor(out=t1[:], in0=ucp[:, c0 + 64:c0 + 64 + NT], scalar=mp("fs"), in1=t1[:],
                                                                               op0=ALU.mult, op1=ALU.add),
                                       reads=[uk, "mpk", "t1"], writes=["t1"], dur=2.35),
                            lambda: op("dve", lambda e: e.scalar_tensor_tensor(out=accc[:], in0=t1[:], scalar=w2c, in1=accc[:], op0=ALU.mult, op1=ALU.add),
                                       reads=["t1", "cpk", ak], writes=[ak], dur=2.35),
                        ]

                    for g in range(NG):
                        c_proj(0, g)
                    for i_, emit in enumerate(c_big(0)):
                        emit()
                        if i_ < NG:
                            c_proj(1, i_)
                    for i_, emit in enumerate(c_big(1)):
                        emit()
                        if i_ < NG:
                            c_final(0, i_)
                    for g in range(NG):
                        c_final(1, g)
                chk("C")
                if l == 0:
                    tap("uc0", uc_out[:], [("uc_out", p_, g_) for p_ in range(2) for g_ in range(NG)])

                with ExitStack() as es:
                    S.fence()
                    PADS = 512
                    WS = NT + 2 * PADS
                    uds = [T(es, U(("ud", ch)), [128, WS]) for _ in range(2)]
                    wk = T(es, U("wk"), [128, WS])
                    rr = T(es, U("rr"), [128, NT])
                    for ch in range(2):
                        op("dve", lambda e: e.memset(uds[ch][:, 0:PADS], 0.0), writes=[("ud", ch)])
                        op("dve", lambda e: e.memset(uds[ch][:, PADS + NT:WS], 0.0), writes=[("ud", ch)])
                    op("dve", lambda e: e.memset(wk[:], 0.0), writes=["wk"])
                    wt, wkey = pre["D"]
                    if mod_deferred:
                        mod_issue(*mod_deferred[0])
                    for ch in range(2):
                        for g in range(NG):
                            bank = zbank(0, 4)
                            zmm(bank, wt, wkey, lambda kc: kc * 256 + ch * 128, 128, g)
                            op("act", lambda e: e.activation(out=uds[ch][:, PADS + g * GT:PADS + (g + 1) * GT], in_=ps[bank][:], func=AF.Copy),
                               reads=[("ps", bank)], writes=[("ud", ch)])
                    for ch in range(2):
                        ud = uds[ch]
                        nlev = (1, 2) if ch == 0 else (3, 4)
                        SW = 272
                        wseg = wk[:, 0:8 * SW].rearrange("p (s j) -> p s j", j=SW)
                        op("dve", lambda e: e.memset(wk[:, 0:8 * SW], 0.0), reads=[], writes=["wk"])
                        udseg = ud[:, PADS:PADS + NT].rearrange("p (s j) -> p s j", j=256)
                        op("dve", lambda e: e.tensor_copy(out=wseg[:, :, 8:264], in_=udseg), reads=[("ud", ch)], writes=["wk"])
                        for lev in range(1, nlev[1] + 1):
                            mod_step(1)
                            sh = 1 << (lev - 1)
                            op("dve", lambda e: e.tensor_tensor(out=wseg[:, :, 0:SW - sh], in0=wseg[:, :, 0:SW - sh], in1=wseg[:, :, sh:SW], op=ALU.add),
                               reads=["wk"], writes=["wk"])
                            for half in range(2):
                                if nlev[half] == lev:
                                    w = 1 << lev
                                    hp = slice(64 * half, 64 * half + 64)
                                    o0 = 8 - w // 2
                                    op("dve", lambda e: e.tensor_tensor(out=rr[hp, :].rearrange("p (s j) -> p s j", j=256),
                                                                        in0=wseg[hp, :, o0:o0 + 256],
                                                                        in1=apx(mp("icp", ch * 256, 256)[hp, :], [[0, 8], [1, 256]]), op=ALU.mult),
                                       reads=["wk", "mpk"], writes=["rr"])
                        st64 = 64
                        for lev in range(1, nlev[1] + 1):
                            mod_step(1)
                            sh = st64 << (lev - 1)
                            src = ud if lev == 1 else wk
                            op("dve", lambda e: e.tensor_tensor(out=wk[:, 0:WS - sh], in0=src[:, 0:WS - sh], in1=src[:, sh:WS], op=ALU.add),
                               reads=[("ud", ch), "wk"], writes=["wk"])
                            for half in range(2):
                                if nlev[half] == lev:
                                    w = 1 << lev
                                    hp = slice(64 * half, 64 * half + 64)
                                    o0 = PADS - (w // 2) * st64
                                    wv = wk[hp, o0:o0 + NT].rearrange("p (r c) -> p r c", c=64)
                                    op("dve", lambda e: e.tensor_tensor(out=wv, in0=wv, in1=apx(mp("ics", ch * 32, 32)[hp, :], [[1, 32], [0, 64]]), op=ALU.mult),
                                       reads=["wk", "mpk"], writes=["wk"])
                                    op("dve", lambda e: e.tensor_tensor(out=rr[hp, :], in0=rr[hp, :], in1=wk[hp, o0:o0 + NT], op=ALU.add),
                                       reads=["wk", "rr"], writes=["rr"])
                                    if half == 0 and nlev[1] > lev:
                                        pass
                        op("dve", lambda e: e.tensor_tensor(out=ud_out[:, ch, :], in0=rr[:], in1=ud[:, PADS:PADS + NT], op=ALU.subtract),
                           reads=["rr", ("ud", ch)], writes=[("ud_out", ch, gg) for gg in range(NG)])
                mod_step(100)
                wunpin()
                gitems = []
                for m in range(KC):
                    gitems.append(([w_kc(w_in[l], OFF["merge"] + jb * 1024 + m * 128, 128) for jb in range(4)], 128, (8, 128)))
                pre["stg"] = Stream(gitems)
                pre["stg"].ensure(2)
                chk("D")
                if l == 0:
                    tap("ud0", ud_out[:], [("ud_out", p_, g_) for p_ in range(2) for g_ in range(NG)])

                with ExitStack() as es:
                    S.fence()
                    mod_step(100)
                    merged = T(es, U("merged"), [128, KC, NT], BF16)
                    wsm = T(es, U("wsm"), [128, 2, 7, 128], BF16)
                    sg = [T(es, U("sg"), [128, GT]) for _ in range(1)]
                    pr = [T(es, U("pr"), [128, GT]) for _ in range(1)]
                    macc = T(es, U("macc"), [128, GT])
                    outs = (ua_out, ob_out, uc_out, ud_out)
                    onames = ("ua_out", "ob_out", "uc_out", "ud_out")
                    stg = pre["stg"]
                    sto = Stream([(w_kc(w_o[l], c0, 512), 128, (8, 512)) for c0 in range(0, D, 512)])

                    def wsm_load(m_):
                        sl_ = m_ % 2
                        for bi, wsrc in enumerate((w_oa, w_ob, w_oc)):
                            S.dma("pool", wsm[:, sl_, 2 * bi:2 * bi + 2, :],
                                  wsrc[l].rearrange("(kc p) n -> p kc n", p=128)[:, :, m_ * 128:(m_ + 1) * 128], writes=[("wsm", sl_)], scoped=True)
                        hpm_ = slice(64 * ((m_ // 2) % 2), 64 * ((m_ // 2) % 2) + 64)
                        S.dma("pool", wsm[hpm_, sl_, 6, :], pool_w[l, m_ // 2, :, (m_ % 2) * 128:(m_ % 2) * 128 + 128], writes=[("wsm", sl_)], scoped=True)

                    wsm_load(0)
                    for m in range(KC):
                        wg_t, wg_k = stg.get(m, ahead=2)
                        if m + 1 < KC:
                            wsm_load(m + 1)
                        if m == KC - 1:
                            sto.ensure(1)
                        sl = m % 2
                        for g in range(NG):
                            gs = slice(g * GT, (g + 1) * GT)
                            for j in range(4):
                                yb = zbank(0, 4)
                                if j < 3:
                                    for kc2 in range(2):
                                        op("pe", lambda e: e.matmul(ps[yb][:], lhsT=wsm[:, sl, 2 * j + kc2, :], rhs=outs[j][:, kc2, gs],
                                                                    start=(kc2 == 0), stop=(kc2 == 1)),
                                           reads=[("wsm", sl), (onames[j], kc2, g)], writes=[("ps", yb)])
                                else:
                                    half = (m // 2) % 2
                                    chd = (m // 2) // 2
                                    hp = slice(64 * half, 64 * half + 64)
                                    op("pe", lambda e: e.matmul(ps[yb][:], lhsT=wsm[hp, sl, 6, :], rhs=ud_out[hp, chd, gs], start=True, stop=True),
                                       reads=[("wsm", sl), ("ud_out", chd, g)], writes=[("ps", yb)])
                                gb = zbank(0, 4)
                                zmm(gb, wg_t, wg_k, lambda kc: (j * 8 + kc) * 128, 128, g)
                                sgt_ = sg[0]
                                op("act", lambda e: e.activation(out=sgt_[:], in_=ps[gb][:], func=AF.Sigmoid), reads=[("ps", gb)], writes=[sgt_.name])
                                dst = macc if j == 0 else pr[0]
                                if j < 3:
                                    op("dve", lambda e: e.tensor_tensor(out=dst[:], in0=ps[yb][:], in1=sgt_[:], op=ALU.mult),
                                       reads=[("ps", yb), sgt_.name], writes=[dst.name])
                                else:
                                    op("dve", lambda e: e.scalar_tensor_tensor(out=dst[:], in0=ps[yb][:], scalar=cp(f"psc{l}", m, 1), in1=sgt_[:],
                                                                               op0=ALU.mult, op1=ALU.mult),
                                       reads=[("ps", yb), sgt_.name, "cpk"], writes=[dst.name])
                                if j in (1, 2):
                                    op("dve", lambda e: e.tensor_tensor(out=macc[:], in0=macc[:], in1=dst[:], op=ALU.add),
                                       reads=[macc.name, dst.name], writes=[macc.name])
                                elif j == 3:
                                    op("dve", lambda e: e.tensor_tensor(out=merged[:, m, gs], in0=macc[:], in1=dst[:], op=ALU.add),
                                       reads=[macc.name, dst.name], writes=[("merged", m, g)])
                    if l == 0:
                        tap("mg0", merged[:], [("merged", m_, g_) for m_ in range(KC) for g_ in range(NG)])
                    for nb in range(2):
                        wt, wkey = sto.get(nb, ahead=2)
                        for nn in range(4):
                            n = nb * 4 + nn
                            for g in range(NG):
                                gs = slice(g * GT, (g + 1) * GT)
                                bank = zbank(0, 4)
                                for m in range(KC):
                                    op("pe", lambda e: e.matmul(ps[bank][:], lhsT=wt[:, m * 512 + nn * 128:m * 512 + nn * 128 + 128],
                                                                rhs=merged[:, m, gs], start=(m == 0), stop=(m == KC - 1)),
                                       reads=[wkey, ("merged", m, g)], writes=[("ps", bank)])
                                op("dve", lambda e: e.scalar_tensor_tensor(out=x[:, n, gs], in0=ps[bank][:], scalar=mod[:, l, 16 + n:16 + n + 1],
                                                                           in1=x[:, n, gs], op0=ALU.mult, op1=ALU.add),
                                   reads=[("ps", bank), ("x", n, g)] + modkeys(l, "g_m"), writes=[("x", n, g)])
            chk("merge")
            if l == 0:
                tap("x1", x[:], [("x", m_, g_) for m_ in range(KC) for g_ in range(NG)])
            fitems = [([w_kc(w13[l], j * 128, 128), w_kc(w13[l], DFF + j * 128, 128)], 128, (8, 128)) for j in range(11)]
            pre["stf0"] = Stream(fitems)
            pre["stf0"].ensure(2)
            with ExitStack() as es:
                S.fence()
                rms_to_h(l, af[:, l, :], 24, es)
            with ExitStack() as es:
                S.fence()
                act = T(es, U("act"), [128, 11, NT], BF16)
                sil = [T(es, U("sil"), [128, GT]) for _ in range(2)]
                for hf in range(2):
                    items = []
                    for jj in range(11):
                        j = hf * 11 + jj
                        items.append(([w_kc(w13[l], j * 128, 128), w_kc(w13[l], DFF + j * 128, 128)], 128, (8, 128)))
                    stf = pre.pop("stf0") if hf == 0 else Stream(items)
                    for jj in range(11):
                        wt, wkey = stf.get(jj)
                        for g in range(NG):
                            gs = slice(g * GT, (g + 1) * GT)
                            b1 = zbank(0, 4)
                            zmm(b1, wt, wkey, lambda kc: kc * 128, 128, g)
                            st_ = sil[g % 2]
                            op("act", lambda e: e.activation(out=st_[:], in_=ps[b1][:], func=AF.Silu), reads=[("ps", b1)], writes=[st_.name])
                            b2 = zbank(0, 4)
                            zmm(b2, wt, wkey, lambda kc: (8 + kc) * 128, 128, g)
                            op("dve", lambda e: e.tensor_tensor(out=act[:, jj, gs], in0=ps[b2][:], in1=st_[:], op=ALU.mult),
                               reads=[("ps", b2), st_.name], writes=[("act", jj, g)])
                    items = []
                    for npair in range(4):
                        items.append((bass.AP(w2.tensor, w2[l, hf * 1408:hf * 1408 + 1, npair * 256:npair * 256 + 1].offset,
                                              [[D, 128], [128 * D, 11], [1, 256]]), 128, (11, 256)))
                    st2 = Stream(items)
                    for npair in range(4):
                        wt, wkey = st2.get(npair)
                        for nn in range(2):
                            n = npair * 2 + nn
                            for g in range(NG):
                                gs = slice(g * GT, (g + 1) * GT)
                                bank = zbank(0, 4)
                                for jj in range(11):
                                    op("pe", lambda e: e.matmul(ps[bank][:], lhsT=wt[:, jj * 256 + nn * 128:jj * 256 + nn * 128 + 128],
                                                                rhs=act[:, jj, gs], start=(jj == 0), stop=(jj == 10)),
                                       reads=[wkey, ("act", jj, g)], writes=[("ps", bank)])
                                op("dve", lambda e: e.scalar_tensor_tensor(out=x[:, n, gs], in0=ps[bank][:], scalar=mod[:, l, 40 + n:40 + n + 1],
                                                                           in1=x[:, n, gs], op0=ALU.mult, op1=ALU.add),
                                   reads=[("ps", bank), ("x", n, g)] + modkeys(l, "g_f"), writes=[("x", n, g)])
            if l == 0:
                tap("x2", x[:], [("x", m_, g_) for m_ in range(KC) for g_ in range(NG)])
          except _Stop:
            break
        with ExitStack() as es:
            S.fence()
            rms_to_h(0, cp("fng"), None, es)
        S.finish("sp")
    print(f"[kernel] instructions={S.ninst} waits={S.nwaits} per-engine={S.cnt}")
    return nc


_NC_CACHE = {}


def make_in_maps(inp):
    x_prompt = np.asarray(inp["x_prompt"], np.float32)
    x_sample = np.asarray(inp["x_sample"], np.float32)
    state = np.asarray(inp["state_hgrn"], np.float32)
    c = np.asarray(inp["c"], np.float32)
    c_ctx = np.asarray(inp["c_ctx"], np.float32)
    cbf = make_cbf()
    mpk_s, _ = make_mpk(True)
    mpk_p, _ = make_mpk(False)
    wnames = ["ada_w", "w_in", "w_out_a", "w_out_b", "w_out_c", "pool_w", "w_o", "ffn_w13", "ffn_w2"]
    weights = {n: np.ascontiguousarray(np.asarray(inp[n], np.float32)) for n in wnames}
    in_maps = []
    for core in range(8):
        if core < 4:
            xt = x_sample[core]
            cvec = c[core]
            s0 = state[core].reshape(L, 2, 2, 128, 64)
            mpk = mpk_s
        else:
            xt = x_prompt[(core - 4) * 8:(core - 3) * 8].reshape(NT, D)
            cvec = c_ctx
            s0 = np.zeros((L, 2, 2, 128, 64), np.float32)
            mpk = mpk_p
        m = {"xT": np.ascontiguousarray(xt.T), "cpk": make_cpk(inp, cvec), "mpk": mpk, "cbf": cbf,
             "s0": np.ascontiguousarray(s0)}
        m.update(weights)
        in_maps.append(m)
    return in_maps


def kernel(**inputs):
    if "nc" not in _NC_CACHE:
        _NC_CACHE["nc"] = build_program()
    nc = _NC_CACHE["nc"]
    in_maps = make_in_maps(inputs)
    res = run_bass_kernel_spmd(nc, in_maps, core_ids=list(range(8)))
    rs = res.results
    y_sample = np.stack([np.ascontiguousarray(rs[i]["yT"].T) for i in range(4)], axis=0).astype(np.float32)
    y_prompt = np.concatenate([np.ascontiguousarray(rs[i]["yT"].T).reshape(8, 256, D) for i in range(4, 8)], axis=0).astype(np.float32)
    new_state = np.concatenate([rs[i]["st"].reshape(8, L, 2, 4, 64, 64) for i in range(4, 8)], axis=0).astype(np.float32)
    return (y_prompt, y_sample, new_state)
```

```python
import numpy as np
from contextlib import ExitStack
import concourse.bass as bass
import concourse.mybir as mybir
from concourse.bass_utils import run_bass_kernel_spmd

F32 = mybir.dt.float32
BF16 = mybir.dt.bfloat16
AF = mybir.ActivationFunctionType
ALU = mybir.AluOpType

D = 1024
NT = 2048
KC = 8
NG = 4
GT = 512
L = 2
N_IN = 6912
DFF = 2816
EPS = 1e-6
OFF = dict(a_val=0, a_gate=256, q=512, f_fw=768, f_bw=1024, i=1280, g=1536, c_b=1792, c_c=2048, c_x=2304,
           d=2560, merge=2816)
WB = 4096
NWB = 3


class Sched:
    def __init__(self, nc, n_dma_sems=32):
        self.nc = nc
        self.engs = {"pe": nc.tensor, "dve": nc.vector, "act": nc.scalar, "pool": nc.gpsimd, "sp": nc.sync}
        self.sem = {k: nc.alloc_semaphore(name="s_" + k) for k in self.engs}
        self.cnt = {k: 0 for k in self.engs}
        self.seen = {k: {} for k in self.engs}
        self.dsems = [nc.alloc_semaphore(name=f"d{i}") for i in range(n_dma_sems)]
        self.dcnt = [0] * n_dma_sems
        hn = n_dma_sems // 2
        self.drange = {"sp": (0, hn), "pool": (hn, hn + 4), "poolw": (hn + 4, n_dma_sems), "act": (0, hn)}
        self.dnext = {"sp": 0, "pool": hn, "poolw": hn + 4}
        self.lastw = {}
        self.reads = {}
        self.nwaits = 0
        self.ninst = 0
        self.tfin = {}
        self.efree = {k: 0.0 for k in self.engs}
        self.actset = None

    def _wait(self, e, tok):
        key, val, sem = tok
        if key == e and e == "pe":
            return
        if self.seen[e].get(key, 0) >= val:
            return
        self.engs[e].wait_ge(sem, val)
        self.nwaits += 1
        self.seen[e][key] = val

    @staticmethod
    def _flat(keys):
        out = []
        for k in keys:
            if isinstance(k, list):
                out.extend(k)
            else:
                out.append(k)
        return out

    def _deps(self, e, reads, writes):
        toks = []
        for r in reads:
            if r in self.lastw:
                toks.append(self.lastw[r])
        for w in writes:
            if w in self.lastw:
                toks.append(self.lastw[w])
            toks.extend(self.reads.get(w, {}).values())
        for t in toks:
            self._wait(e, t)

    def _commit(self, tok, reads, writes):
        for r in reads:
            d = self.reads.setdefault(r, {})
            old = d.get(tok[0])
            if old is None or old[1] < tok[1]:
                d[tok[0]] = tok
        for w in writes:
            self.lastw[w] = tok
            self.reads[w] = {}

    def _dep_toks(self, reads, writes):
        toks = []
        for r in reads:
            if r in self.lastw:
                toks.append(self.lastw[r])
        for w in writes:
            if w in self.lastw:
                toks.append(self.lastw[w])
            toks.extend(self.reads.get(w, {}).values())
        return toks

    def est_start(self, e, reads, writes, aset=None):
        t = self.efree[e]
        for tk in self._dep_toks(self._flat(reads), self._flat(writes)):
            t = max(t, self.tfin.get((tk[0], tk[1]), 0.0) + (0.12 if tk[0] != "pe" or e != "pe" else 0.0))
        if e == "act" and aset is not None and aset != self.actset:
            t += 1.3
        return t

    def op(self, e, fn, reads=(), writes=(), expect=(), dur=None, aset=None):
        reads, writes = self._flat(reads), self._flat(writes)
        for (k_, t_) in expect:
            assert self.lastw.get(k_) == t_, f"emission-order violation on {k_}"
        t0 = self.est_start(e, reads, writes, aset)
        if e == "act" and aset is not None:
            self.actset = aset
        if e == "pool":
            for t in getattr(self, "fence_toks", []):
                self._wait(e, t)
        self._deps(e, reads, writes)
        inst = fn(self.engs[e])
        self.cnt[e] += 1
        self.ninst += 1
        inst.then_inc(self.sem[e], 1)
        tok = (e, self.cnt[e], self.sem[e])
        self.efree[e] = t0 + (dur if dur is not None else {"pe": 0.25, "act": 0.6, "dve": 0.65, "pool": 1.2}.get(e, 0.5))
        self.tfin[(e, self.cnt[e])] = self.efree[e]
        self._commit(tok, reads, writes)
        return tok

    def dma(self, e, out, in_, reads=(), writes=(), scoped=False, dma_us=None):
        reads, writes = self._flat(reads), self._flat(writes)
        if e == "sp" or scoped:
            for t in getattr(self, "fence_toks", []):
                self._wait(e, t)
        self._deps(e, reads, writes)
        rk = "poolw" if (e == "pool" and dma_us is not None) else e
        lo, hi = self.drange[rk]
        i = self.dnext[rk]
        self.dnext[rk] = lo + (i + 1 - lo) % (hi - lo)
        if self.dcnt[i] > 0:
            self._wait(e, (("d", i), self.dcnt[i], self.dsems[i]))
        t0 = max(self.efree.values())
        for tk in self._dep_toks(reads, writes):
            t0 = max(t0, self.tfin.get((tk[0], tk[1]), 0.0))
        self.engs[e].dma_start(out=out, in_=in_).then_inc(self.dsems[i], 16)
        self.ninst += 1
        self.dcnt[i] += 16
        tok = (("d", i), self.dcnt[i], self.dsems[i])
        self.tfin[(("d", i), self.dcnt[i])] = t0 + (dma_us if dma_us is not None else 3.0)
        self._commit(tok, reads, writes)
        return tok

    def fence(self):
        toks = [(k, self.cnt[k], self.sem[k]) for k in ("pe", "dve", "act", "pool") if self.cnt[k] > 0]
        wlo, whi = self.drange["poolw"]
        toks += [(("d", i), self.dcnt[i], self.dsems[i]) for i in range(len(self.dsems)) if self.dcnt[i] > 0 and not (wlo <= i < whi)]
        for e in ("pe", "dve", "act"):
            for t in toks:
                self._wait(e, t)
        self.fence_toks = toks

    def finish(self, e="sp"):
        for k in self.engs:
            if self.cnt[k] > 0:
                self._wait(e, (k, self.cnt[k], self.sem[k]))
        for i, s in enumerate(self.dsems):
            if self.dcnt[i] > 0:
                self._wait(e, (("d", i), self.dcnt[i], s))


def apx(base, free):
    return bass.AP(base.tensor, base.offset, [list(base.ap[0])] + [list(f) for f in free])


class Pack:
    def __init__(self):
        self.cols = {}
        self.parts = []
        self.n = 0

    def add(self, name, arr):
        arr = np.ascontiguousarray(arr, dtype=np.float32).reshape(128, -1)
        self.cols[name] = (self.n, arr.shape[1])
        self.parts.append(arr)
        self.n += arr.shape[1]

    def build(self):
        return np.ascontiguousarray(np.concatenate(self.parts, axis=1))


def fm(vec, nch):
    return np.asarray(vec, np.float32).reshape(nch, 128).T


def cpk_layout():
    p = Pack()
    z = lambda n: np.zeros((128, n), np.float32)
    p.add("cvec", z(8))
    p.add("fng", z(8))
    p.add("lbl", z(L * 4))
    for l in range(L):
        p.add(f"adab{l}", z(48))
        p.add(f"nmg{l}", z(8))
        p.add(f"nfg{l}", z(8))
        p.add(f"psc{l}", z(8))
        p.add(f"caw{l}", z(62))
        p.add(f"cab{l}", z(2))
        p.add(f"lag{l}", z(2))
        p.add(f"lab{l}", z(2))
        p.add(f"hng{l}", z(2))
        p.add(f"ccw{l}", z(6))
    return p.cols, p.n


def make_cpk(inp, cvec):
    p = Pack()
    p.add("cvec", fm(cvec, 8))
    p.add("fng", fm(inp["final_norm_g"], 8))
    lbl = np.asarray(inp["hgrn_lb_logits"], np.float32)
    p.add("lbl", lbl.reshape(L, 2, 2, 128).transpose(3, 0, 1, 2).reshape(128, L * 4))
    for l in range(L):
        p.add(f"adab{l}", fm(inp["ada_b"][l], 48))
        p.add(f"nmg{l}", fm(inp["norm_mix_g"][l], 8))
        p.add(f"nfg{l}", fm(inp["norm_ffn_g"][l], 8))
        p.add(f"psc{l}", fm(inp["pool_scale"][l], 8))
        caw = np.asarray(inp["conv_a_w"][l], np.float32)
        p.add(f"caw{l}", caw.reshape(31, 2, 128).transpose(2, 1, 0).reshape(128, 62))
        p.add(f"cab{l}", fm(inp["conv_a_b"][l], 2))
        p.add(f"lag{l}", fm(inp["ln_a_g"][l], 2))
        p.add(f"lab{l}", fm(inp["ln_a_b"][l], 2))
        p.add(f"hng{l}", fm(inp["hgrn_norm_g"][l], 2))
        ccw = np.asarray(inp["conv_c_w"][l], np.float32)
        p.add(f"ccw{l}", ccw.reshape(3, 2, 128).transpose(2, 1, 0).reshape(128, 6))
    return p.build()


def make_mpk(is_sample):
    p = Pack()
    rep = lambda v: np.broadcast_to(np.asarray(v, np.float32).reshape(1, -1), (128, np.asarray(v).size))
    s = 1.0 if is_sample else 0.0
    p.add("keep", rep([s]))
    p.add("fs", rep([s]))
    p.add("epsD", rep([EPS]))
    pos = np.arange(256)
    p.add("mCL", rep((pos != 0) * (1.0 - s)))
    p.add("mCR", rep((pos != 255) * (1.0 - s)))
    blk = np.arange(1, 32)
    p.add("FL", rep((blk % 4 != 0) * (1.0 - s)))
    blk0 = np.arange(0, 31)
    p.add("FR", rep((blk0 % 4 != 3) * (1.0 - s)))
    icp = np.zeros((128, 2, 256), np.float32)
    ics = np.zeros((128, 2, 32), np.float32)
    for ch in range(2):
        for half in range(2):
            w = (2, 4, 8, 16)[2 * ch + half]
            for Lseq, tab in ((256, icp), (32, ics)):
                t = np.arange(Lseq)
                lo = np.clip(t - w // 2, 0, Lseq)
                hi = np.clip(t + w - w // 2, 0, Lseq)
                tab[64 * half:64 * half + 64, ch, :] = (1.0 / (hi - lo).astype(np.float32))[None, :]
    p.add("icp", icp.reshape(128, 512) * (1.0 - s))
    p.add("ics", ics.reshape(128, 64) * s)
    p.add("rmask", rep((np.arange(512) % 32 != 0) * 1.0))
    sidx = np.arange(128)[:, None]
    tidx = np.arange(128)[None, :]
    same = (sidx // 32) == (tidx // 32)
    p.add("triF", (same & (sidx <= tidx)).astype(np.float32))
    p.add("triB", (same & (sidx >= tidx)).astype(np.float32))
    p.add("cmask", ((np.arange(128)[:, None] // 32) == np.arange(4)[None, :]).astype(np.float32))
    return p.build(), p.cols


def make_cbf():
    ident = np.eye(128, dtype=np.float32)
    ones = np.ones((128, 128), np.float32)
    blk = np.zeros((128, 128), np.float32)
    blk[:64, :64] = 1.0
    blk[64:, 64:] = 1.0
    return np.ascontiguousarray(np.concatenate([ident, ones, blk], axis=1))


class _Stop(Exception):
    pass


class OP:
    __slots__ = ("e", "fn", "reads", "writes", "expect", "n", "k", "aset")

    def __init__(self, e, fn, reads=(), writes=(), expect=(), n=512, k=None, aset=None):
        self.e, self.fn, self.reads, self.writes, self.expect, self.n, self.k, self.aset = e, fn, reads, writes, expect, n, k, aset

    def dur(self):
        if self.e == "pe":
            return max(0.06, self.n / 2300.0)
        if self.e == "act":
            return 0.22 + self.n * 0.00075
        if self.e == "dve":
            if self.k == "recip":
                return self.n * 0.0065
            d = 0.1 + self.n * 0.0011
            return d * 1.9 if self.k == "scan" else d
        return 0.1 + self.n * 0.0019


def run_sched(S, *gens, slack=0.05):
    pend = []
    for g_ in gens:
        if g_ is None:
            continue
        try:
            pend.append([g_, next(g_)])
        except StopIteration:
            pass
    while pend:
        best, bt = None, None
        for i_, (g_, o) in enumerate(pend):
            t = S.est_start(o.e, o.reads, o.writes, o.aset)
            if t <= S.efree[o.e] + slack:
                best = i_
                break
            if bt is None or t < bt - 1e-9:
                best, bt = i_, t
        g_, o = pend[best]
        tok = S.op(o.e, o.fn, o.reads, o.writes, o.expect, dur=o.dur(), aset=o.aset)
        try:
            pend[best][1] = g_.send(tok)
        except StopIteration:
            pend.pop(best)


def run_threads(*gens):
    live = [g_ for g_ in gens if g_ is not None]
    while live:
        for g_ in list(live):
            try:
                next(g_)
            except StopIteration:
                live.remove(g_)


def build_program(n_layers=L, taps=(), stop_after=None):
    nc = bass.Bass("TRN2", target_bir_lowering=False)

    def chk(name):
        if stop_after == name:
            raise _Stop()
    CC, NCPK = cpk_layout()
    _, MC = make_mpk(True)
    NMPK = sum(v[1] for v in MC.values())

    def din(name, shape):
        return nc.dram_tensor(name, shape, F32, kind="ExternalInput").ap()

    xT_d = din("xT", [D, NT])
    cpk_d = din("cpk", [128, NCPK])
    mpk_d = din("mpk", [128, NMPK])
    cbf_d = din("cbf", [128, 384])
    s0_d = din("s0", [L, 2, 2, 128, 64])
    ada_w = din("ada_w", [L, D, 6 * D])
    w_in = din("w_in", [L, D, N_IN])
    w_oa = din("w_out_a", [L, 256, D])
    w_ob = din("w_out_b", [L, 256, D])
    w_oc = din("w_out_c", [L, 256, D])
    pool_w = din("pool_w", [L, 4, 64, 256])
    w_o = din("w_o", [L, D, D])
    w13 = din("ffn_w13", [L, D, 2 * DFF])
    w2 = din("ffn_w2", [L, DFF, D])
    yT_d = nc.dram_tensor("yT", [D, NT], F32, kind="ExternalOutput").ap()
    st_d = nc.dram_tensor("st", [8, L, 2, 2, 128, 64], F32, kind="ExternalOutput").ap()
    tap_d = {}
    for (tname, shape) in taps:
        tap_d[tname] = nc.dram_tensor("tap_" + tname, list(shape), F32, kind="ExternalOutput").ap()

    S = Sched(nc)
    op = S.op
    uid = [0]

    def U(prefix):
        uid[0] += 1
        return f"{prefix}{uid[0]}"

    with ExitStack() as es0:
        def T(es, name, shape, dt=F32):
            cm = nc.sbuf_tensor(name, list(shape), dt)
            hnd = cm.__enter__()
            es.callback(lambda: cm.__exit__(None, None, None))
            return hnd

        x = T(es0, "x", [128, KC, NT])
        h = T(es0, "h", [128, KC, NT], BF16)
        wp = [T(es0, f"wp{i}", [128, WB], BF16) for i in range(NWB)]
        cpk = T(es0, "cpk_sb", [128, NCPK])
        mpk = T(es0, "mpk_sb", [128, NMPK])
        cbf = T(es0, "cbf_sb", [128, 384], BF16)
        mod = T(es0, "mod", [128, L, 48])
        am = T(es0, "am", [128, L, 8])
        af = T(es0, "af", [128, L, 8])
        lbt = T(es0, "lbt", [128, L, 4])
        omlt = T(es0, "omlt", [128, L, 4])
        cs_bf = T(es0, "cs_bf", [128, 8], BF16)
        ps = [es0.enter_context(nc.psum_tensor(f"ps{i}", [128, 512], F32)) for i in range(7)]
        pst = es0.enter_context(nc.psum_tensor("pst", [128, 1024], BF16))

        def cp(name, a=0, n=None):
            o, w = CC[name]
            n = w - a if n is None else n
            return cpk[:, o + a:o + a + n]

        def mp(name, a=0, n=None):
            o, w = MC[name]
            n = w - a if n is None else n
            return mpk[:, o + a:o + a + n]

        ident = cbf[:, 0:128]
        ones_bf = cbf[:, 128:256]
        blk_bf = cbf[:, 256:384]

        wstate = {"i": 0, "pinned": set()}

        def wload(src_ap, nparts, shape_free, pin=False, buf=None):
            i = wstate["i"] if buf is None else buf
            while i in wstate["pinned"]:
                assert buf is None
                i = (i + 1) % NWB
            wstate["i"] = (i + 1) % NWB
            if pin:
                wstate["pinned"].add(i)
            pieces = src_ap if isinstance(src_ap, list) else [src_ap]
            n = int(np.prod(shape_free))
            assert n * len(pieces) <= WB and len(pieces) <= 4
            keys = [("wp", i, k) for k in range(4)]
            for k, pc in enumerate(pieces):
                dst = wp[i][0:nparts, k * n:(k + 1) * n]
                if len(shape_free) == 2:
                    dst = dst.rearrange("p (a b) -> p a b", b=shape_free[1])
                wk_ = [keys[k]] + (keys[len(pieces):] if k == 0 else [])
                S.dma("pool", dst, pc, writes=wk_, dma_us=2.0 + nparts * n * 4 / 150e3)
            return wp[i], keys

        def wunpin():
            wstate["pinned"].clear()

        def w_kc(dram2d, c0, ncols):
            return dram2d.rearrange("(kc p) n -> p kc n", p=128)[:, :, c0:c0 + ncols]

        class Stream:
            def __init__(self, items):
                self.items = items
                self.loaded = {}
                self.n = 0

            def ensure(self, upto):
                while self.n <= upto and self.n < len(self.items):
                    src, npart, shp = self.items[self.n]
                    self.loaded[self.n] = wload(src, npart, shp)
                    self.n += 1

            def get(self, i, ahead=NWB - 1):
                self.ensure(i + ahead)
                return self.loaded.pop(i)

        zrot = {"i": 0}

        def zbank(lo=0, hi=2):
            k = (lo, hi)
            i = zrot.get(k, lo)
            zrot[k] = lo + (i + 1 - lo) % (hi - lo)
            return i

        def zmm(bank, wt, wkey, woff, M, g, hkeys=True, wstride=None):
            for kc in range(KC):
                lhsT = wt[:, woff(kc):woff(kc) + M]
                op("pe", lambda e: e.matmul(ps[bank][0:M, :], lhsT=lhsT, rhs=h[:, kc, g * GT:(g + 1) * GT],
                                            start=(kc == 0), stop=(kc == KC - 1)),
                   reads=[wkey, ("h", g)], writes=[("ps", bank)])

        S.dma("sp", cpk[:], cpk_d, writes=["cpk"])
        S.dma("sp", mpk[:], mpk_d, writes=["mpk"])
        S.dma("pool", cbf[:], cbf_d, writes=["cbf"])
        def x_load(g):
            for kc in range(KC):
                S.dma("sp", x[:, kc, g * GT:(g + 1) * GT], xT_d[kc * 128:(kc + 1) * 128, g * GT:(g + 1) * GT],
                      writes=[("x", kc, g)])

        x_load(0)
        op("act", lambda e: e.activation(out=cs_bf[:], in_=cp("cvec"), func=AF.Silu), reads=["cpk"], writes=["cs_bf"])
        op("dve", lambda e: e.memset(lbt[:, 0, :], 0.0), writes=["lbt"])
        op("dve", lambda e: e.tensor_tensor(out=lbt[:, 1, :], in0=cp("lbl", 4, 4), in1=cp("lbl", 0, 4), op=ALU.subtract),
           reads=["cpk"], writes=["lbt"])
        op("act", lambda e: e.activation(out=lbt[:, 1, :], in_=lbt[:, 1, :], func=AF.Sigmoid), reads=["lbt"], writes=["lbt"])
        op("dve", lambda e: e.tensor_scalar(out=omlt[:].rearrange("p a b -> p (a b)"), in0=lbt[:].rearrange("p a b -> p (a b)"),
                                            scalar1=-1.0, scalar2=1.0, op0=ALU.mult, op1=ALU.add),
           reads=["lbt"], writes=["omlt"])
        MODPARTS = {"sh_m": (0, 1), "sc_m": (2, 3), "g_m": (4, 5), "sh_f": (6, 7), "sc_f": (8, 9), "g_f": (10, 11)}

        def modkeys(l, part):
            return [("mod", l, b_) for b_ in MODPARTS[part]]

        mod_pending = {}

        def mod_issue(l, bi):
            mod_pending[(l, bi)] = wload(w_kc(ada_w[l], bi * 512, 512), 128, (8, 512))

        def mod_block(l, bi):
            if (l, bi) not in mod_pending:
                mod_issue(l, bi)
            wt, wkey = mod_pending.pop((l, bi))
            bank = zbank()
            for jj in range(4):
                for kc in range(KC):
                    lhsT = wt[:, kc * 512 + jj * 128: kc * 512 + jj * 128 + 128]
                    op("pe", lambda e: e.matmul(ps[bank][:, jj:jj + 1], lhsT=lhsT, rhs=cs_bf[:, kc:kc + 1],
                                                start=(kc == 0), stop=(kc == KC - 1)),
                       reads=[wkey, "cs_bf"], writes=[("ps", bank)], dur=0.07)
            o_, _ = CC[f"adab{l}"]
            op("dve", lambda e: e.tensor_tensor(out=mod[:, l, bi * 4:(bi + 1) * 4], in0=ps[bank][:, 0:4],
                                                in1=cpk[:, o_ + bi * 4:o_ + bi * 4 + 4], op=ALU.add),
               reads=[("ps", bank), "cpk"], writes=[("mod", l, bi)], dur=0.1)

        def mod_fin(l, which):
            dst, sc0, gname, part = (am, 8, f"nmg{l}", "sc_m") if which == "am" else (af, 32, f"nfg{l}", "sc_f")
            op("dve", lambda e: e.scalar_tensor_tensor(out=dst[:, l, :], in0=mod[:, l, sc0:sc0 + 8], scalar=1.0,
                                                       in1=cp(gname), op0=ALU.add, op1=ALU.mult),
               reads=modkeys(l, part) + ["cpk"], writes=[(which, l)], dur=0.1)

        mod_issue(0, 0)
        mod_issue(0, 1)
        mod_issue(0, 2)
        for bi in range(4):
            mod_block(0, bi)
            if bi == 0:
                mod_issue(0, 3)
        mod_fin(0, "am")
        mod_deferred = [(0, bi) for bi in range(4, 12)] + ([(1, bi) for bi in range(12)] if n_layers > 1 else [])

        def mod_step(n=1, prefetch=True):
            for _ in range(n):
                if not mod_deferred:
                    return
                l_, bi_ = mod_deferred.pop(0)
                mod_block(l_, bi_)
                if mod_deferred and prefetch:
                    mod_issue(*mod_deferred[0])
                if bi_ == 9:
                    mod_fin(l_, "af")
                if bi_ == 3:
                    mod_fin(l_, "am")

        epsD = mp("epsD")

        def rms_to_h(l, avec, shcol, es):
            NSQ, NTMP = 4, 6
            sq = [T(es, U("sq"), [128, GT], BF16) for _ in range(NSQ)]
            rstd = [T(es, U("rstd"), [128, GT]) for _ in range(NG)]
            tmp = [T(es, U("ntmp"), [128, GT]) for _ in range(NTMP)]
            rtok = {}
            SQ_ENG = ["act", "pool", "pool", "act", "pool", "dve", "act", "pool"]
            AF_ENG = ["act", "act", "pool", "act", "act", "pool", "act", "pool"]
            bank = 5

            def stat(g):
                gs = slice(g * GT, (g + 1) * GT)
                for kc in range(KC):
                    sqt = sq[kc % NSQ]
                    xin = x[:, kc, gs]
                    if SQ_ENG[kc] == "act":
                        yield OP("act", lambda e: e.activation(out=sqt[:], in_=xin, func=AF.Square),
                                 reads=[("x", kc, g)], writes=[sqt.name])
                    else:
                        yield OP(SQ_ENG[kc], lambda e: e.tensor_tensor(out=sqt[:], in0=xin, in1=xin, op=ALU.mult),
                                 reads=[("x", kc, g)], writes=[sqt.name])
                    yield OP("pe", lambda e: e.matmul(ps[bank][:], lhsT=ones_bf, rhs=sqt[:], start=(kc == 0), stop=(kc == KC - 1)),
                             reads=[sqt.name, "cbf"], writes=[("ps", bank)], n=512)
                rt = rstd[g]
                yield OP("act", lambda e: e.activation(out=rt[:], in_=ps[bank][:], func=AF.Ln, bias=epsD, scale=1.0 / D),
                         reads=[("ps", bank), "mpk"], writes=[rt.name], aset="lnexp")
                rtok[g] = yield OP("act", lambda e: e.activation(out=rt[:], in_=rt[:], func=AF.Exp, scale=-0.5),
                                   reads=[rt.name], writes=[rt.name], aset="lnexp")

            def apply(g):
                gs = slice(g * GT, (g + 1) * GT)
                rt = rstd[g]
                for kc in range(KC):
                    tt = tmp[(g * KC + kc) % NTMP]
                    yield OP("dve", lambda e: e.tensor_tensor(out=tt[:], in0=x[:, kc, gs], in1=rt[:], op=ALU.mult),
                             reads=[("x", kc, g), rt.name], writes=[tt.name], expect=[(rt.name, rtok[g])])
                    en = AF_ENG[kc]
                    if shcol is None:
                        if en == "act":
                            yield OP("act", lambda e: e.activation(out=tt[:], in_=tt[:], func=AF.Identity, scale=avec[:, kc:kc + 1]),
                                     reads=[tt.name, "cpk"], writes=[tt.name])
                        else:
                            yield OP(en, lambda e: e.tensor_scalar(out=tt[:], in0=tt[:], scalar1=avec[:, kc:kc + 1], scalar2=0.0,
                                                                   op0=ALU.mult, op1=ALU.add),
                                     reads=[tt.name, "cpk"], writes=[tt.name])
                        S.dma("sp", yT_d[kc * 128:(kc + 1) * 128, gs], tt[:], reads=[tt.name])
                    else:
                        rd = [tt.name, ("am", l), ("af", l)] + modkeys(l, "sh_m") + modkeys(l, "sh_f")
                        if en == "act":
                            yield OP("act", lambda e: e.activation(out=h[:, kc, gs], in_=tt[:], func=AF.Identity,
                                                                   bias=mod[:, l, shcol + kc:shcol + kc + 1], scale=avec[:, kc:kc + 1]),
                                     reads=rd, writes=[("h", g)])
                        else:
                            yield OP(en, lambda e: e.tensor_scalar(out=h[:, kc, gs], in0=tt[:], scalar1=avec[:, kc:kc + 1],
                                                                   scalar2=mod[:, l, shcol + kc:shcol + kc + 1],
                                                                   op0=ALU.mult, op1=ALU.add),
                                     reads=rd, writes=[("h", g)])

            run_sched(S, stat(0))
            for g in range(NG):
                run_sched(S, stat(g + 1) if g + 1 < NG else None, apply(g))

        def tap(name, src_ap, keys):
            if name in tap_d:
                S.dma("pool", tap_d[name], src_ap, reads=keys, scoped=True)

        for l in range(n_layers if stop_after != "prep" else 0):
          try:
            with ExitStack() as esl:
                S.fence()
                ob_out = T(esl, U("ob_out"), [128, 2, NT], BF16)
                pre = {}
                wcol = lambda nm, p_: w_kc(w_in[l], OFF[nm] + p_ * 128, 128)
                pre["BA0"] = wload([wcol("q", 0), wcol("f_fw", 0), wcol("f_bw", 0)], 128, (8, 128), buf=0)
                pre["BI"] = wload([wcol("i", 0), wcol("g", 0), wcol("i", 1), wcol("g", 1)], 128, (8, 128), buf=1)
                pre["BA1"] = wload([wcol("q", 1), wcol("f_fw", 1), wcol("f_bw", 1)], 128, (8, 128), buf=2)
                with ExitStack() as es:
                    S.fence()
                    if l == 0:
                        for g_ in range(1, NG):
                            x_load(g_)
                    rms_to_h(l, am[:, l, :], 0, es)
                chk("norm1")
                if l == 0:
                    tap("h0", h[:], [("h", g_) for g_ in range(NG)])

                with ExitStack() as es:
                    S.fence()
                    v_tm = T(es, U("v_tm"), [128, 16, 128], BF16)
                    o_acc = T(es, U("o_acc"), [128, NT])
                    f_t = T(es, U("f_t"), [128, GT])
                    k_t = T(es, U("k_t"), [128, GT])
                    b_t = T(es, U("b_t"), [128, GT])
                    bq_t = T(es, U("bq_t"), [128, GT])
                    kp32 = T(es, U("kp32"), [128, GT])
                    e1s = [T(es, U("e1_t"), [128, GT]) for _ in range(2)]
                    qps = [T(es, U("qp"), [128, GT], BF16) for _ in range(3)]
                    kps = [T(es, U("kp"), [128, GT], BF16) for _ in range(3)]
                    khs = [T(es, U("kh"), [128, GT], BF16) for _ in range(2)]
                    sqb = T(es, U("sqb"), [128, GT], BF16)
                    Sbuf = T(es, U("Sbuf"), [128, 16, 128])
                    Sbfs = [T(es, U("Sbf"), [128, 16, 128], BF16) for _ in range(2)]
                    Sk = [T(es, U("Sk"), [128, 128]) for _ in range(2)]
                    Skb = [T(es, U("Skb"), [128, 128], BF16) for _ in range(4)]
                    s0b = T(es, U("s0b"), [128, 4, 128], BF16)
                    khT = [T(es, U("khT"), [128, 4, 128], BF16) for _ in range(2)]
                    PT = [T(es, U("PT"), [128, 2, 128], BF16) for _ in range(2)]
                    skn = [0]
                    s0t = T(es, U("s0t"), [128, 4, 128])
                    op("dve", lambda e: e.memset(s0t[:].rearrange("p a b -> p (a b)"), 0.0), writes=["s0t"])
                    for dr_ in range(2):
                        for p_ in range(2):
                            S.dma("sp", s0t[0:64, dr_ * 2 + p_, 0:64], s0_d[l, dr_, p_, 0:64, :], writes=[("s0t", dr_, p_, 0)], reads=["s0t"])
                            S.dma("sp", s0t[64:128, dr_ * 2 + p_, 64:128], s0_d[l, dr_, p_, 64:128, :], writes=[("s0t", dr_, p_, 1)], reads=["s0t"])
                    s0b_tok = op("act", lambda e: e.activation(out=s0b[:].rearrange("p a b -> p (a b)"), in_=s0t[:].rearrange("p a b -> p (a b)"), func=AF.Copy),
                                 reads=["s0t"] + [("s0t", a_, b_, c_) for a_ in range(2) for b_ in range(2) for c_ in range(2)], writes=["s0b"])
                    PB = 0
                    KVB = (1, 4)
                    for p in range(2):
                        cols1 = [OFF["q"] + p * 128, OFF["f_fw"] + p * 128, OFF["f_bw"] + p * 128]
                        cols2 = [OFF["i"] + p * 128, OFF["g"] + p * 128]
                        wA, kA = pre["BA0"] if p == 0 else pre["BA1"]
                        wBb, kB = pre["BI"]
                        wq = lambda kc: (0 * 8 + kc) * 128
                        wf = [lambda kc: (1 * 8 + kc) * 128, lambda kc: (2 * 8 + kc) * 128]
                        wi = lambda kc, p=p: ((2 * p) * 8 + kc) * 128
                        wg = lambda kc, p=p: ((2 * p + 1) * 8 + kc) * 128
                        for tq in range(4):
                            bank = zbank()
                            for ti in range(4):
                                tt = tq * 4 + ti
                                for kc in range(KC):
                                    op("pe", lambda e: e.matmul(ps[bank][:, ti * 128:(ti + 1) * 128],
                                                                lhsT=h[:, kc, tt * 128:(tt + 1) * 128],
                                                                rhs=wBb[:, wi(kc):wi(kc) + 128],
                                                                start=(kc == 0), stop=(kc == KC - 1)),
                                       reads=[kB, ("h", tq)], writes=[("ps", bank)])
                            op("act", lambda e: e.activation(out=v_tm[:, tq * 4:(tq + 1) * 4, :].rearrange("p a b -> p (a b)"),
                                                             in_=ps[bank][:], func=AF.Copy),
                               reads=[("ps", bank)], writes=[("v_tm", tq)])
                        jobs = [(0, g_) for g_ in (0, 1, 2, 3)] + [(1, g_) for g_ in (3, 2, 1, 0)]
                        NJ = len(jobs)
                        chain = {}
                        s_start = {}

                        ptok = {}

                        def zmm_g(bank, wt, wkey, woff, g):
                            for kc in range(KC):
                                lhsT = wt[:, woff(kc):woff(kc) + 128]
                                yield OP("pe", lambda e: e.matmul(ps[bank][:], lhsT=lhsT, rhs=h[:, kc, g * GT:(g + 1) * GT],
                                                                  start=(kc == 0), stop=(kc == KC - 1)),
                                         reads=[wkey, ("h", g)], writes=[("ps", bank)], n=512)

                        def prep(k):
                            dr, g = jobs[k]
                            e1_t, qp, kp, kh = e1s[k % 2], qps[k % 3], kps[k % 3], khs[k % 2]
                            lbc = lbt[:, l, dr * 2 + p:dr * 2 + p + 1]
                            omc = omlt[:, l, dr * 2 + p:dr * 2 + p + 1]
                            endc = 31 if dr == 0 else 0
                            tk = ptok[k] = {}
                            yield from zmm_g(PB, wA, kA, wf[dr], g)
                            yield OP("act", lambda e: e.activation(out=f_t[:], in_=ps[PB][:], func=AF.Sigmoid),
                                     reads=[("ps", PB)], writes=["f_t"], aset="sig")
                            yield from zmm_g(PB, wA, kA, wq, g)
                            yield OP("dve", lambda e: e.tensor_scalar(out=f_t[:], in0=f_t[:], scalar1=omc, scalar2=lbc,
                                                                      op0=ALU.mult, op1=ALU.add),
                                     reads=["f_t", "lbt", "omlt"], writes=["f_t"])
                            yield OP("dve", lambda e: e.tensor_scalar(out=k_t[:], in0=f_t[:], scalar1=-1.0, scalar2=1.0, op0=ALU.mult, op1=ALU.add),
                                     reads=["f_t"], writes=["k_t"])
                            yield OP("act", lambda e: e.activation(out=f_t[:], in_=f_t[:], func=AF.Ln), reads=["f_t"], writes=["f_t"], aset="lnexp")
                            yield OP("dve", lambda e: e.tensor_tensor_scan(out=b_t[:], data0=mp("rmask"), data1=f_t[:], initial=0.0,
                                                                           op0=ALU.mult, op1=ALU.add),
                                     reads=["f_t", "mpk"], writes=["b_t"], k="scan")
                            if dr == 0:
                                bb, bbk = b_t, "b_t"
                            else:
                                yield OP("dve", lambda e: e.tensor_tensor(out=f_t[:], in0=f_t[:], in1=b_t[:], op=ALU.subtract),
                                         reads=["f_t", "b_t"], writes=["f_t"])
                                yield OP("dve", lambda e: e.tensor_tensor(out=bq_t[:].rearrange("p (c j) -> p c j", j=32),
                                                                          in0=f_t[:].rearrange("p (c j) -> p c j", j=32),
                                                                          in1=apx(b_t[:, 31:32], [[32, 16], [0, 32]]), op=ALU.add),
                                         reads=["f_t", "b_t"], writes=["bq_t"])
                                bb, bbk = bq_t, "bq_t"
                            tk[e1_t.name] = yield OP("act", lambda e: e.activation(out=e1_t[:], in_=bb[:], func=AF.Exp),
                                                     reads=[bbk], writes=[e1_t.name], aset="lnexp")
                            yield OP("act", lambda e: e.activation(out=kp32[:], in_=bb[:], func=AF.Exp, scale=-1.0),
                                     reads=[bbk], writes=["kp32"], aset="lnexp")
                            tk[qp.name] = yield OP("dve", lambda e: e.tensor_tensor(out=qp[:], in0=ps[PB][:], in1=e1_t[:], op=ALU.mult),
                                                   reads=[("ps", PB), e1_t.name], writes=[qp.name])
                            yield OP("dve", lambda e: e.tensor_tensor(out=kp32[:], in0=kp32[:], in1=k_t[:], op=ALU.mult),
                                     reads=["kp32", "k_t"], writes=["kp32"])
                            tk[kp.name] = yield OP("act", lambda e: e.activation(out=kp[:], in_=kp32[:], func=AF.Copy),
                                                   reads=["kp32"], writes=[kp.name])
                            tk[kh.name] = yield OP("dve", lambda e: e.tensor_tensor(out=kh[:].rearrange("p (c j) -> p c j", j=32),
                                                                                    in0=kp32[:].rearrange("p (c j) -> p c j", j=32),
                                                                                    in1=apx(e1_t[:, endc:endc + 1], [[32, 16], [0, 32]]), op=ALU.mult),
                                                   reads=["kp32", e1_t.name], writes=[kh.name])

                        def kv(k):
                            dr, g = jobs[k]
                            e1_t, kh, Sbf = e1s[k % 2], khs[k % 2], Sbfs[k % 2]
                            x_e1 = [(e1_t.name, ptok[k][e1_t.name])]
                            x_kh = [(kh.name, ptok[k][kh.name])]
                            endc = 31 if dr == 0 else 0
                            if dr not in chain:
                                chain[dr] = dict(Sprev=s0t[:, dr * 2 + p, :], Sprev_key=[("s0t", dr, p, 0), ("s0t", dr, p, 1)],
                                                 Sprev_b=s0b[:, dr * 2 + p, :], Sprev_bkey="s0b", Sprev_btok=s0b_tok, first=True)
                            cs = chain[dr]
                            tiles = [0, 1, 2, 3] if dr == 0 else [3, 2, 1, 0]
                            s_start[k] = {}
                            for ti in tiles:
                                tt = g * 4 + ti
                                yield OP("pe", lambda e: e.transpose(pst[:, ti * 128:(ti + 1) * 128], kh[:, ti * 128:(ti + 1) * 128], ident),
                                         reads=[kh.name, "cbf"], writes=[("pst", ti)], expect=x_kh, n=128)
                                kt = khT[ti % 2]
                                yield OP("dve", lambda e: e.tensor_tensor(out=kt[:], in0=apx(pst[:, ti * 128:ti * 128 + 1], [[0, 4], [1, 128]]),
                                                                          in1=apx(mp("cmask"), [[1, 4], [0, 128]]), op=ALU.mult),
                                         reads=[("pst", ti), "mpk"], writes=[kt.name], n=512)
                                kvb = KVB[ti % 2]
                                for j in range(4):
                                    yield OP("pe", lambda e: e.matmul(ps[kvb][:, j * 128:(j + 1) * 128], lhsT=kt[:, j, :],
                                                                      rhs=v_tm[:, tt, :], start=True, stop=True),
                                             reads=[kt.name, ("v_tm", tt // 4)], writes=[("ps", kvb)], n=128)
                                chunks = [0, 1, 2, 3] if dr == 0 else [3, 2, 1, 0]
                                for j in chunks:
                                    c = tt * 4 + j
                                    seg_start = (c % 8 == 0) if dr == 0 else (c % 8 == 7)
                                    if seg_start and not cs["first"]:
                                        seg_prev = (c // 8 - 1) if dr == 0 else (c // 8 + 1)
                                        Sprev, Sprev_key = cs["Sprev"], cs["Sprev_key"]
                                        S.dma("sp", st_d[seg_prev, l, dr, p, 0:64, :], Sprev[0:64, 0:64], reads=[Sprev_key])
                                        S.dma("sp", st_d[seg_prev, l, dr, p, 64:128, :], Sprev[64:128, 64:128], reads=[Sprev_key])
                                        skt = Sk[skn[0] % 2]
                                        sktb = Skb[skn[0] % 4]
                                        skn[0] += 1
                                        yield OP("dve", lambda e: e.tensor_scalar(out=skt[:], in0=Sprev, scalar1=mp("keep"), scalar2=0.0, op0=ALU.mult, op1=ALU.add),
                                                 reads=[Sprev_key, "mpk"], writes=[skt.name], n=128)
                                        cs["Sprev"], cs["Sprev_key"] = skt[:], skt.name
                                        tok_ = yield OP("act", lambda e: e.activation(out=sktb[:], in_=skt[:], func=AF.Copy),
                                                        reads=[skt.name], writes=[sktb.name], n=128)
                                        cs["Sprev_b"], cs["Sprev_bkey"], cs["Sprev_btok"] = sktb[:], sktb.name, tok_
                                    cs["first"] = False
                                    s_start[k][(ti, j)] = (cs["Sprev_b"], cs["Sprev_bkey"], cs["Sprev_btok"])
                                    dcol = ti * 128 + j * 32 + endc
                                    slot = ti * 4 + j
                                    sp_ap, sp_key = cs["Sprev"], cs["Sprev_key"]
                                    yield OP("dve", lambda e: e.scalar_tensor_tensor(out=Sbuf[:, slot, :], in0=sp_ap, scalar=e1_t[:, dcol:dcol + 1],
                                                                                     in1=ps[kvb][:, j * 128:(j + 1) * 128], op0=ALU.mult, op1=ALU.add),
                                             reads=[sp_key, e1_t.name, ("ps", kvb)], writes=[("Sbuf", slot)], expect=x_e1, n=128)
                                    cs["Sprev"], cs["Sprev_key"] = Sbuf[:, slot, :], ("Sbuf", slot)
                                    tok_ = yield OP("act", lambda e: e.activation(out=Sbf[:, slot, :], in_=Sbuf[:, slot, :], func=AF.Copy),
                                                    reads=[("Sbuf", slot)], writes=[(Sbf.name, slot)], n=128)
                                    cs["Sprev_b"], cs["Sprev_bkey"], cs["Sprev_btok"] = Sbf[:, slot, :], (Sbf.name, slot), tok_
                            if k == NJ - 1 or jobs[k + 1][0] != dr:
                                seg_last = 7 if dr == 0 else 0
                                Sprev, Sprev_key = cs["Sprev"], cs["Sprev_key"]
                                S.dma("sp", st_d[seg_last, l, dr, p, 0:64, :], Sprev[0:64, 0:64], reads=[Sprev_key])
                                S.dma("sp", st_d[seg_last, l, dr, p, 64:128, :], Sprev[64:128, 64:128], reads=[Sprev_key])

                        def sc(k):
                            dr, g = jobs[k]
                            qp, kp = qps[k % 3], kps[k % 3]
                            x_qk = [(qp.name, ptok[k][qp.name]), (kp.name, ptok[k][kp.name])]
                            gs = slice(g * GT, (g + 1) * GT)
                            tri = mp("triF") if dr == 0 else mp("triB")
                            tiles = [0, 1, 2, 3] if dr == 0 else [3, 2, 1, 0]
                            obk = (6, 5)
                            for ti in tiles:
                                tt = g * 4 + ti
                                tsl = slice(ti * 128, (ti + 1) * 128)
                                ptt = PT[ti % 2]
                                for hh in range(2):
                                    hp = slice(64 * hh, 64 * hh + 64)
                                    scb = 2 + hh
                                    yield OP("pe", lambda e: e.matmul(ps[scb][:, 0:128], lhsT=kp[hp, tsl], rhs=qp[hp, tsl],
                                                                      start=True, stop=True),
                                             reads=[kp.name, qp.name], writes=[("ps", scb)], expect=x_qk, n=128)
                                    yield OP("dve", lambda e: e.tensor_tensor(out=ptt[:, hh, :], in0=ps[scb][:, 0:128], in1=tri, op=ALU.mult),
                                             reads=[("ps", scb), "mpk"], writes=[(ptt.name, hh)], n=128)
                                for hh in range(2):
                                    hp = slice(64 * hh, 64 * hh + 64)
                                    ob = obk[hh]
                                    yield OP("pe", lambda e: e.matmul(ps[ob][hp, tsl], lhsT=v_tm[:, tt, hp], rhs=ptt[:, hh, :],
                                                                      start=True, stop=False),
                                             reads=[(ptt.name, hh), ("v_tm", tt // 4)], writes=[("ps", ob)], n=128)
                                    for j in range(4):
                                        sap, skey, stok = s_start[k][(ti, j)]
                                        csl = slice(ti * 128 + j * 32, ti * 128 + j * 32 + 32)
                                        yield OP("pe", lambda e: e.matmul(ps[ob][hp, csl], lhsT=sap[hp, hp], rhs=qp[hp, csl],
                                                                          start=False, stop=(j == 3)),
                                                 reads=[skey, qp.name], writes=[("ps", ob)], expect=[(skey, stok)], n=128)
                            for hh in range(2):
                                hp = slice(64 * hh, 64 * hh + 64)
                                ob = obk[hh]
                                if dr == 0:
                                    yield OP("act", lambda e: e.activation(out=o_acc[hp, gs], in_=ps[ob][hp, :], func=AF.Copy),
                                             reads=[("ps", ob)], writes=[("o_acc", g, hh)])
                                else:
                                    yield OP("dve", lambda e: e.tensor_tensor(out=o_acc[hp, gs], in0=o_acc[hp, gs], in1=ps[ob][hp, :], op=ALU.add),
                                             reads=[("ps", ob), ("o_acc", g, hh)], writes=[("o_acc", g, hh)])

                        run_sched(S, prep(0))
                        for step in range(NJ + 1):
                            run_sched(S, kv(step) if step < NJ else None,
                                      prep(step + 1) if step + 1 < NJ else None,
                                      sc(step - 1) if step >= 1 else None)
                        if p == 0:
                            pre["A"] = wload(w_kc(w_in[l], 0, 512), 128, (8, 512), pin=True, buf=0)
                        rst = [f_t, k_t, b_t, bq_t]
                        rstk = ["f_t", "k_t", "b_t", "bq_t"]
                        sqbs = [sqb, khs[0]]
                        sils = [kp32, e1s[0]]
                        silk = ["kp32", e1s[0].name]
                        ptok2 = {}

                        def post_stat(th):
                            bank = 5 if th == 0 else 6
                            sq_ = sqbs[th]
                            for g in (th, th + 2):
                                gs = slice(g * GT, (g + 1) * GT)
                                rg, rgk = rst[g], rstk[g]
                                yield OP("act", lambda e: e.activation(out=sq_[:], in_=o_acc[:, gs], func=AF.Square),
                                         reads=[("o_acc", g, 0), ("o_acc", g, 1)], writes=[sq_.name])
                                yield OP("pe", lambda e: e.matmul(ps[bank][:], lhsT=blk_bf, rhs=sq_[:], start=True, stop=True),
                                         reads=[sq_.name, "cbf"], writes=[("ps", bank)], n=512)
                                yield OP("act", lambda e: e.activation(out=rg[:], in_=ps[bank][:], func=AF.Ln, bias=epsD, scale=1.0 / 64),
                                         reads=[("ps", bank), "mpk"], writes=[rgk], aset="lnexp")
                                yield OP("act", lambda e: e.activation(out=rg[:], in_=rg[:], func=AF.Exp, scale=-0.5),
                                         reads=[rgk], writes=[rgk], aset="lnexp")
                                ptok2[g] = yield OP("dve", lambda e: e.tensor_tensor(out=rg[:], in0=rg[:], in1=o_acc[:, gs], op=ALU.mult),
                                                    reads=[rgk, ("o_acc", g, 0), ("o_acc", g, 1)], writes=[rgk])

                        def post_gate(th):
                            bank = th
                            sl_, slk = sils[th], silk[th]
                            for g in (th, th + 2):
                                gs = slice(g * GT, (g + 1) * GT)
                                rg, rgk = rst[g], rstk[g]
                                yield from zmm_g(bank, wBb, kB, wg, g)
                                yield OP("act", lambda e: e.activation(out=sl_[:], in_=ps[bank][:], func=AF.Silu),
                                         reads=[("ps", bank)], writes=[slk], aset="silu")
                                yield OP("dve", lambda e: e.scalar_tensor_tensor(out=ob_out[:, p, gs], in0=rg[:], scalar=cp(f"hng{l}", p, 1), in1=sl_[:],
                                                                                 op0=ALU.mult, op1=ALU.mult),
                                         reads=[rgk, slk, "cpk"], writes=[("ob_out", p, g)], expect=[(rgk, ptok2[g])])

                        run_sched(S, post_stat(0), post_stat(1))
                        run_sched(S, post_gate(0), post_gate(1))
                chk("B")
                if l == 0:
                    tap("ob0", ob_out[:], [("ob_out", p_, g_) for p_ in range(2) for g_ in range(NG)])

                ua_out = T(esl, U("ua_out"), [128, 2, NT], BF16)
                uc_out = T(esl, U("uc_out"), [128, 2, NT], BF16)
                ud_out = T(esl, U("ud_out"), [128, 2, NT], BF16)

                with ExitStack() as es:
                  S.fence()
                  acc = T(es, U("acc"), [128, 2, NT])
                  with ExitStack() as es2:
                    upad = T(es2, U("upad"), [128, 32, 94])
                    uhi = T(es2, U("uhi"), [128, 32, 94], BF16)
                    ulo = T(es2, U("ulo"), [128, 32, 94], BF16)
                    sgt = T(es2, U("sgt"), [128, GT])
                    ug = T(es2, U("ug"), [128, GT])
                    whb = T(es2, U("whb"), [128, 31], BF16)
                    wlo = T(es2, U("wlo"), [128, 31])
                    dgh = [T(es2, U("dgh"), [128, 128], BF16) for _ in range(4)]
                    dgl = [T(es2, U("dgl"), [128, 128], BF16) for _ in range(4)]
                    op("dve", lambda e: e.memset(upad[:].rearrange("p a b -> p (a b)"), 0.0), writes=["upad"])
                    op("dve", lambda e: e.memset(uhi[:].rearrange("p a b -> p (a b)"), 0.0), writes=["uhi"])
                    op("dve", lambda e: e.memset(ulo[:].rearrange("p a b -> p (a b)"), 0.0), writes=["ulo"])
                    wt, wkey = pre["A"]
                    w1c = cp(f"caw{l}", 31, 31)
                    op("act", lambda e: e.activation(out=whb[:], in_=w1c, func=AF.Copy), reads=["cpk"], writes=["whb"])
                    op("dve", lambda e: e.tensor_tensor(out=wlo[:], in0=w1c, in1=whb[:], op=ALU.subtract), reads=["cpk", "whb"], writes=["wlo"])
                    for ch in range(2):
                        for g in range(NG):
                            bank = zbank(0, 4)
                            zmm(bank, wt, wkey, lambda kc: kc * 512 + 256 + ch * 128, 128, g)
                            op("act", lambda e: e.activation(out=sgt[:], in_=ps[bank][:], func=AF.Sigmoid), reads=[("ps", bank)], writes=["sgt"])
                            bank2 = zbank(0, 4)
                            zmm(bank2, wt, wkey, lambda kc: kc * 512 + ch * 128, 128, g)
                            if ch == 0:
                                op("dve", lambda e: e.tensor_tensor(out=upad[:, g * 8:(g + 1) * 8, 15:79],
                                                                    in0=ps[bank2][:].rearrange("p (a b) -> p a b", b=64),
                                                                    in1=sgt[:].rearrange("p (a b) -> p a b", b=64), op=ALU.mult),
                                   reads=[("ps", bank2), "sgt"], writes=["upad"])
                            else:
                                op("dve", lambda e: e.tensor_tensor(out=ug[:], in0=ps[bank2][:], in1=sgt[:], op=ALU.mult),
                                   reads=[("ps", bank2), "sgt"], writes=["ug"])
                                op("act", lambda e: e.activation(out=uhi[:, g * 8:(g + 1) * 8, 15:79],
                                                                 in_=ug[:].rearrange("p (a b) -> p a b", b=64), func=AF.Copy),
                                   reads=["ug"], writes=["uhi"])
                                op("dve", lambda e: e.tensor_tensor(out=ulo[:, g * 8:(g + 1) * 8, 15:79],
                                                                    in0=ug[:].rearrange("p (a b) -> p a b", b=64),
                                                                    in1=uhi[:, g * 8:(g + 1) * 8, 15:79], op=ALU.subtract),
                                   reads=["ug", "uhi"], writes=["ulo"])
                    for ub, ukey in ((upad, "upad"), (uhi, "uhi"), (ulo, "ulo")):
                        op("dve", lambda e: e.tensor_tensor(out=ub[:, 1:32, 0:15], in0=ub[:, 0:31, 64:79],
                                                            in1=apx(mp("FL"), [[1, 31], [0, 15]]), op=ALU.mult),
                           reads=[ukey, "mpk"], writes=[ukey])
                        op("dve", lambda e: e.tensor_tensor(out=ub[:, 0:31, 79:94], in0=ub[:, 1:32, 15:30],
                                                            in1=apx(mp("FR"), [[1, 31], [0, 15]]), op=ALU.mult),
                           reads=[ukey, "mpk"], writes=[ukey])
                    accv = acc[:, 0, :].rearrange("p (a b) -> p a b", b=64)
                    CB = (2, 3, 4, 5)
                    for k in range(31):
                        if k == 0:
                            op("dve", lambda e: e.tensor_scalar(out=accv, in0=upad[:, :, 0:64], scalar1=cp(f"caw{l}", 0, 1),
                                                                scalar2=cp(f"cab{l}", 0, 1), op0=ALU.mult, op1=ALU.add),
                               reads=["upad", "cpk"], writes=[("acc", 0)], dur=2.35)
                        else:
                            op("dve", lambda e: e.scalar_tensor_tensor(out=accv, in0=upad[:, :, k:k + 64], scalar=cp(f"caw{l}", k, 1),
                                                                       in1=accv, op0=ALU.mult, op1=ALU.add),
                               reads=["upad", "cpk", ("acc", 0)], writes=[("acc", 0)], dur=2.35)
                        if k % 3 == 0 and k > 0:
                            mod_step(1, prefetch=(k < 30))
                        dh, dl = dgh[k % 4], dgl[k % 4]
                        op("act", lambda e: e.activation(out=dh[:], in_=ident, func=AF.Copy, scale=cp(f"caw{l}", 31 + k, 1)),
                           reads=["cbf", "cpk"], writes=[dh.name], dur=0.3)
                        op("act", lambda e: e.activation(out=dl[:], in_=ident, func=AF.Copy, scale=wlo[:, k:k + 1]),
                           reads=["cbf", "wlo"], writes=[dl.name], dur=0.3)
                        for g in range(NG):
                            win_hi = uhi[:, g * 8:(g + 1) * 8, k:k + 64]
                            win_lo = ulo[:, g * 8:(g + 1) * 8, k:k + 64]
                            op("pe", lambda e: e.matmul(ps[CB[g]][:], lhsT=dh[:], rhs=win_hi, start=(k == 0), stop=False),
                               reads=[dh.name, "uhi"], writes=[("ps", CB[g])], dur=0.22)
                            op("pe", lambda e: e.matmul(ps[CB[g]][:], lhsT=dh[:], rhs=win_lo, start=False, stop=False),
                               reads=[dh.name, "ulo"], writes=[("ps", CB[g])], dur=0.22)
                            op("pe", lambda e: e.matmul(ps[CB[g]][:], lhsT=dl[:], rhs=win_hi, start=False, stop=(k == 30)),
                               reads=[dl.name, "uhi"], writes=[("ps", CB[g])], dur=0.22)
                    for g in range(NG):
                        op("act", lambda e: e.activation(out=acc[:, 1, g * GT:(g + 1) * GT], in_=ps[CB[g]][:], func=AF.Identity,
                                                         bias=cp(f"cab{l}", 1, 1), scale=1.0),
                           reads=[("ps", CB[g]), "cpk"], writes=[("acc", 1)])
                    wunpin()
                    pre["C1"] = wload(w_kc(w_in[l], OFF["c_b"], 512), 128, (8, 512))
                    pre["C2"] = wload(w_kc(w_in[l], OFF["c_x"], 256), 128, (8, 256))
                  with ExitStack() as es3:
                    S.fence()
                    accb = [T(es3, U("accb"), [128, GT], BF16) for _ in range(4)]
                    sq2 = [T(es3, U("sq2"), [128, GT], BF16) for _ in range(4)]
                    mean = [T(es3, U("mean"), [128, GT]) for _ in range(NG)]
                    rstl = [T(es3, U("rstl"), [128, GT]) for _ in range(NG)]
                    tn = [T(es3, U("tn"), [128, GT]) for _ in range(2)]
                    ltok = {}

                    def ln_stat(th):
                        bs, bq = (4, 5) if th == 0 else (2, 3)
                        for g in (th, th + 2):
                            gs = slice(g * GT, (g + 1) * GT)
                            for ch in range(2):
                                ab, s2 = accb[th * 2 + ch], sq2[th * 2 + ch]
                                yield OP("dve", lambda e: e.tensor_copy(out=ab[:], in_=acc[:, ch, gs]), reads=[("acc", ch)], writes=[ab.name])
                                yield OP("pe", lambda e: e.matmul(ps[bs][:], lhsT=ones_bf, rhs=ab[:], start=(ch == 0), stop=(ch == 1)),
                                         reads=[ab.name, "cbf"], writes=[("ps", bs)], n=512)
                                yield OP("act", lambda e: e.activation(out=s2[:], in_=acc[:, ch, gs], func=AF.Square), reads=[("acc", ch)], writes=[s2.name])
                                yield OP("pe", lambda e: e.matmul(ps[bq][:], lhsT=ones_bf, rhs=s2[:], start=(ch == 0), stop=(ch == 1)),
                                         reads=[s2.name, "cbf"], writes=[("ps", bq)], n=512)
                            mg, rg = mean[g], rstl[g]
                            yield OP("dve", lambda e: e.tensor_scalar(out=mg[:], in0=ps[bs][:], scalar1=1.0 / 256, scalar2=0.0, op0=ALU.mult, op1=ALU.add),
                                     reads=[("ps", bs)], writes=[mg.name])
                            yield OP("dve", lambda e: e.tensor_tensor(out=rg[:], in0=mg[:], in1=mg[:], op=ALU.mult), reads=[mg.name], writes=[rg.name])
                            yield OP("dve", lambda e: e.scalar_tensor_tensor(out=rg[:], in0=ps[bq][:], scalar=1.0 / 256, in1=rg[:],
                                                                             op0=ALU.mult, op1=ALU.subtract),
                                     reads=[("ps", bq), rg.name], writes=[rg.name])
                            yield OP("act", lambda e: e.activation(out=rg[:], in_=rg[:], func=AF.Ln, bias=epsD, scale=1.0),
                                     reads=[rg.name, "mpk"], writes=[rg.name], aset="lnexp")
                            ltok[g] = yield OP("act", lambda e: e.activation(out=rg[:], in_=rg[:], func=AF.Exp, scale=-0.5),
                                               reads=[rg.name], writes=[rg.name], aset="lnexp")

                    def ln_apply(th):
                        t = tn[th]
                        for g in (th, th + 2):
                            gs = slice(g * GT, (g + 1) * GT)
                            mg, rg = mean[g], rstl[g]
                            for ch in range(2):
                                yield OP("dve", lambda e: e.tensor_tensor(out=t[:], in0=acc[:, ch, gs], in1=mg[:], op=ALU.subtract),
                                         reads=[("acc", ch), mg.name], writes=[t.name])
                                yield OP("dve", lambda e: e.tensor_tensor(out=t[:], in0=t[:], in1=rg[:], op=ALU.mult),
                                         reads=[t.name, rg.name], writes=[t.name], expect=[(rg.name, ltok[g])])
                                yield OP("act", lambda e: e.activation(out=ua_out[:, ch, gs], in_=t[:], func=AF.Silu,
                                                                       bias=cp(f"lab{l}", ch, 1), scale=cp(f"lag{l}", ch, 1)),
                                         reads=[t.name, "cpk"], writes=[("ua_out", ch, g)], aset="silu")

                    run_sched(S, ln_stat(0), ln_stat(1))
                    run_sched(S, ln_apply(0), ln_apply(1))
                chk("A")
                if l == 0:
                    tap("ua0", ua_out[:], [("ua_out", p_, g_) for p_ in range(2) for g_ in range(NG)])

                with ExitStack() as es:
                    S.fence()
                    ucps = [T(es, U("ucp"), [128, 64 + NT + 64]) for _ in range(2)]
                    t1 = T(es, U("t1"), [128, NT])
                    acccs = [T(es, U("accc"), [128, NT]) for _ in range(2)]
                    tmpcs = [T(es, U("tmpc"), [128, GT]) for _ in range(2)]
                    for ch in range(2):
                        op("dve", lambda e: e.memset(ucps[ch][:], 0.0), writes=[ucps[ch].name])
                    wt1, wk1 = pre["C1"]
                    wt2, wk2 = pre["C2"]
                    pre["D"] = wload(w_kc(w_in[l], OFF["d"], 256), 128, (8, 256), pin=True)
                    c0 = 64
                    seg = lambda a: a.rearrange("p (s j) -> p s j", j=256)

                    def c_proj(ch, g):
                        ucp, tmpc = ucps[ch], tmpcs[g % 2]
                        bank = zbank(0, 4)
                        zmm(bank, wt1, wk1, lambda kc: kc * 512 + 256 + ch * 128, 128, g)
                        op("act", lambda e: e.activation(out=tmpc[:], in_=ps[bank][:], func=AF.Copy), reads=[("ps", bank)], writes=[tmpc.name])
                        bank2 = zbank(0, 4)
                        zmm(bank2, wt2, wk2, lambda kc: kc * 256 + ch * 128, 128, g)
                        op("dve", lambda e: e.tensor_tensor(out=ucp[:, 64 + g * GT:64 + (g + 1) * GT], in0=ps[bank2][:], in1=tmpc[:], op=ALU.mult),
                           reads=[("ps", bank2), tmpc.name], writes=[ucp.name])

                    def c_final(ch, g):
                        accc = acccs[ch]
                        gs = slice(g * GT, (g + 1) * GT)
                        bank = zbank(0, 4)
                        zmm(bank, wt1, wk1, lambda kc: kc * 512 + ch * 128, 128, g)
                        op("dve", lambda e: e.tensor_tensor(out=uc_out[:, ch, gs], in0=ps[bank][:], in1=accc[:, gs], op=ALU.mult),
                           reads=[("ps", bank), accc.name], writes=[("uc_out", ch, g)])

                    def c_big(ch):
                        ucp, accc = ucps[ch], acccs[ch]
                        uk, ak = ucp.name, accc.name
                        w0 = cp(f"ccw{l}", ch * 3 + 0, 1)
                        w1 = cp(f"ccw{l}", ch * 3 + 1, 1)
                        w2c = cp(f"ccw{l}", ch * 3 + 2, 1)
                        return [
                            lambda: op("dve", lambda e: e.tensor_tensor(out=seg(t1[:]), in0=seg(ucp[:, c0 - 1:c0 - 1 + NT]),
                                                                        in1=apx(mp("mCL"), [[0, 8], [1, 256]]), op=ALU.mult),
                                       reads=[uk, "mpk"], writes=["t1"], dur=2.35),
                            lambda: op("dve", lambda e: e.scalar_tensor_tensor(out=t1[:], in0=ucp[:, c0 - 64:c0 - 64 + NT], scalar=mp("fs"), in1=t1[:],
                                                                               op0=ALU.mult, op1=ALU.add),
                                       reads=[uk, "mpk", "t1"], writes=["t1"], dur=2.35),
                            lambda: op("dve", lambda e: e.tensor_scalar(out=accc[:], in0=ucp[:, c0:c0 + NT], scalar1=w1, scalar2=0.0, op0=ALU.mult, op1=ALU.add),
                                       reads=[uk, "cpk"], writes=[ak], dur=2.35),
                            lambda: op("dve", lambda e: e.scalar_tensor_tensor(out=accc[:], in0=t1[:], scalar=w0, in1=accc[:], op0=ALU.mult, op1=ALU.add),
                                       reads=["t1", "cpk", ak], writes=[ak], dur=2.35),
                            lambda: op("dve", lambda e: e.tensor_tensor(out=seg(t1[:]), in0=seg(ucp[:, c0 + 1:c0 + 1 + NT]),
                                                                        in1=apx(mp("mCR"), [[0, 8], [1, 256]]), op=ALU.mult),
                                       reads=[uk, "mpk", ak], writes=["t1"], dur=2.35),
                            lambda: op("dve", lambda e: e.scalar_tensor_tensor(out=t1[:], in0=ucp[:, c0 + 64:c0 + 64 + NT], scalar=mp("fs"), in1=t1[:],
                                                                               op0=ALU.mult, op1=ALU.add),
                                       reads=[uk, "mpk", "t1"], writes=["t1"], dur=2.35),
                            lambda: op("dve", lambda e: e.scalar_tensor_tensor(out=accc[:], in0=t1[:], scalar=w2c, in1=accc[:], op0=ALU.mult, op1=ALU.add),
                                       reads=["t1", "cpk", ak], writes=[ak], dur=2.35),
                        ]

                    for g in range(NG):
                        c_proj(0, g)
                    for i_, emit in enumerate(c_big(0)):
                        emit()
                        if i_ < NG:
                            c_proj(1, i_)
                    for i_, emit in enumerate(c_big(1)):
                        emit()
                        if i_ < NG:
                            c_final(0, i_)
                    for g in range(NG):
                        c_final(1, g)
                chk("C")
                if l == 0:
                    tap("uc0", uc_out[:], [("uc_out", p_, g_) for p_ in range(2) for g_ in range(NG)])

                with ExitStack() as es:
                    S.fence()
                    PADS = 512
                    WS = NT + 2 * PADS
                    uds = [T(es, U(("ud", ch)), [128, WS]) for _ in range(2)]
                    wk = T(es, U("wk"), [128, WS])
                    rr = T(es, U("rr"), [128, NT])
                    for ch in range(2):
                        op("dve", lambda e: e.memset(uds[ch][:, 0:PADS], 0.0), writes=[("ud", ch)])
                        op("dve", lambda e: e.memset(uds[ch][:, PADS + NT:WS], 0.0), writes=[("ud", ch)])
                    op("dve", lambda e: e.memset(wk[:], 0.0), writes=["wk"])
                    wt, wkey = pre["D"]
                    if mod_deferred:
                        mod_issue(*mod_deferred[0])
                    for ch in range(2):
                        for g in range(NG):
                            bank = zbank(0, 4)
                            zmm(bank, wt, wkey, lambda kc: kc * 256 + ch * 128, 128, g)
                            op("act", lambda e: e.activation(out=uds[ch][:, PADS + g * GT:PADS + (g + 1) * GT], in_=ps[bank][:], func=AF.Copy),
                               reads=[("ps", bank)], writes=[("ud", ch)])
                    for ch in range(2):
                        ud = uds[ch]
                        nlev = (1, 2) if ch == 0 else (3, 4)
                        SW = 272
                        wseg = wk[:, 0:8 * SW].rearrange("p (s j) -> p s j", j=SW)
                        op("dve", lambda e: e.memset(wk[:, 0:8 * SW], 0.0), reads=[], writes=["wk"])
                        udseg = ud[:, PADS:PADS + NT].rearrange("p (s j) -> p s j", j=256)
                        op("dve", lambda e: e.tensor_copy(out=wseg[:, :, 8:264], in_=udseg), reads=[("ud", ch)], writes=["wk"])
                        for lev in range(1, nlev[1] + 1):
                            mod_step(1)
                            sh = 1 << (lev - 1)
                            op("dve", lambda e: e.tensor_tensor(out=wseg[:, :, 0:SW - sh], in0=wseg[:, :, 0:SW - sh], in1=wseg[:, :, sh:SW], op=ALU.add),
                               reads=["wk"], writes=["wk"])
                            for half in range(2):
                                if nlev[half] == lev:
                                    w = 1 << lev
                                    hp = slice(64 * half, 64 * half + 64)
                                    o0 = 8 - w // 2
                                    op("dve", lambda e: e.tensor_tensor(out=rr[hp, :].rearrange("p (s j) -> p s j", j=256),
                                                                        in0=wseg[hp, :, o0:o0 + 256],
                                                                        in1=apx(mp("icp", ch * 256, 256)[hp, :], [[0, 8], [1, 256]]), op=ALU.mult),
                                       reads=["wk", "mpk"], writes=["rr"])
                        st64 = 64
                        for lev in range(1, nlev[1] + 1):
                            mod_step(1)
                            sh = st64 << (lev - 1)
                            src = ud if lev == 1 else wk
                            op("dve", lambda e: e.tensor_tensor(out=wk[:, 0:WS - sh], in0=src[:, 0:WS - sh], in1=src[:, sh:WS], op=ALU.add),
                               reads=[("ud", ch), "wk"], writes=["wk"])
                            for half in range(2):
                                if nlev[half] == lev:
                                    w = 1 << lev
                                    hp = slice(64 * half, 64 * half + 64)
                                    o0 = PADS - (w // 2) * st64
                                    wv = wk[hp, o0:o0 + NT].rearrange("p (r c) -> p r c", c=64)
                                    op("dve", lambda e: e.tensor_tensor(out=wv, in0=wv, in1=apx(mp("ics", ch * 32, 32)[hp, :], [[1, 32], [0, 64]]), op=ALU.mult),
                                       reads=["wk", "mpk"], writes=["wk"])
                                    op("dve", lambda e: e.tensor_tensor(out=rr[hp, :], in0=rr[hp, :], in1=wk[hp, o0:o0 + NT], op=ALU.add),
                                       reads=["wk", "rr"], writes=["rr"])
                                    if half == 0 and nlev[1] > lev:
                                        pass
                        op("dve", lambda e: e.tensor_tensor(out=ud_out[:, ch, :], in0=rr[:], in1=ud[:, PADS:PADS + NT], op=ALU.subtract),
                           reads=["rr", ("ud", ch)], writes=[("ud_out", ch, gg) for gg in range(NG)])
                mod_step(100)
                wunpin()
                gitems = []
                for m in range(KC):
                    gitems.append(([w_kc(w_in[l], OFF["merge"] + jb * 1024 + m * 128, 128) for jb in range(4)], 128, (8, 128)))
                pre["stg"] = Stream(gitems)
                pre["stg"].ensure(2)
                chk("D")
                if l == 0:
                    tap("ud0", ud_out[:], [("ud_out", p_, g_) for p_ in range(2) for g_ in range(NG)])

                with ExitStack() as es:
                    S.fence()
                    mod_step(100)
                    merged = T(es, U("merged"), [128, KC, NT], BF16)
                    wsm = T(es, U("wsm"), [128, 2, 7, 128], BF16)
                    sg = [T(es, U("sg"), [128, GT]) for _ in range(1)]
                    pr = [T(es, U("pr"), [128, GT]) for _ in range(1)]
                    macc = T(es, U("macc"), [128, GT])
                    outs = (ua_out, ob_out, uc_out, ud_out)
                    onames = ("ua_out", "ob_out", "uc_out", "ud_out")
                    stg = pre["stg"]
                    sto = Stream([(w_kc(w_o[l], c0, 512), 128, (8, 512)) for c0 in range(0, D, 512)])

                    def wsm_load(m_):
                        sl_ = m_ % 2
                        for bi, wsrc in enumerate((w_oa, w_ob, w_oc)):
                            S.dma("pool", wsm[:, sl_, 2 * bi:2 * bi + 2, :],
                                  wsrc[l].rearrange("(kc p) n -> p kc n", p=128)[:, :, m_ * 128:(m_ + 1) * 128], writes=[("wsm", sl_)], scoped=True)
                        hpm_ = slice(64 * ((m_ // 2) % 2), 64 * ((m_ // 2) % 2) + 64)
                        S.dma("pool", wsm[hpm_, sl_, 6, :], pool_w[l, m_ // 2, :, (m_ % 2) * 128:(m_ % 2) * 128 + 128], writes=[("wsm", sl_)], scoped=True)

                    wsm_load(0)
                    for m in range(KC):
                        wg_t, wg_k = stg.get(m, ahead=2)
                        if m + 1 < KC:
                            wsm_load(m + 1)
                        if m == KC - 1:
                            sto.ensure(1)
                        sl = m % 2
                        for g in range(NG):
                            gs = slice(g * GT, (g + 1) * GT)
                            for j in range(4):
                                yb = zbank(0, 4)
                                if j < 3:
                                    for kc2 in range(2):
                                        op("pe", lambda e: e.matmul(ps[yb][:], lhsT=wsm[:, sl, 2 * j + kc2, :], rhs=outs[j][:, kc2, gs],
                                                                    start=(kc2 == 0), stop=(kc2 == 1)),
                                           reads=[("wsm", sl), (onames[j], kc2, g)], writes=[("ps", yb)])
                                else:
                                    half = (m // 2) % 2
                                    chd = (m // 2) // 2
                                    hp = slice(64 * half, 64 * half + 64)
                                    op("pe", lambda e: e.matmul(ps[yb][:], lhsT=wsm[hp, sl, 6, :], rhs=ud_out[hp, chd, gs], start=True, stop=True),
                                       reads=[("wsm", sl), ("ud_out", chd, g)], writes=[("ps", yb)])
                                gb = zbank(0, 4)
                                zmm(gb, wg_t, wg_k, lambda kc: (j * 8 + kc) * 128, 128, g)
                                sgt_ = sg[0]
                                op("act", lambda e: e.activation(out=sgt_[:], in_=ps[gb][:], func=AF.Sigmoid), reads=[("ps", gb)], writes=[sgt_.name])
                                dst = macc if j == 0 else pr[0]
                                if j < 3:
                                    op("dve", lambda e: e.tensor_tensor(out=dst[:], in0=ps[yb][:], in1=sgt_[:], op=ALU.mult),
                                       reads=[("ps", yb), sgt_.name], writes=[dst.name])
                                else:
                                    op("dve", lambda e: e.scalar_tensor_tensor(out=dst[:], in0=ps[yb][:], scalar=cp(f"psc{l}", m, 1), in1=sgt_[:],
                                                                               op0=ALU.mult, op1=ALU.mult),
                                       reads=[("ps", yb), sgt_.name, "cpk"], writes=[dst.name])
                                if j in (1, 2):
                                    op("dve", lambda e: e.tensor_tensor(out=macc[:], in0=macc[:], in1=dst[:], op=ALU.add),
                                       reads=[macc.name, dst.name], writes=[macc.name])
                                elif j == 3:
                                    op("dve", lambda e: e.tensor_tensor(out=merged[:, m, gs], in0=macc[:], in1=dst[:], op=ALU.add),
                                       reads=[macc.name, dst.name], writes=[("merged", m, g)])
                    if l == 0:
                        tap("mg0", merged[:], [("merged", m_, g_) for m_ in range(KC) for g_ in range(NG)])
                    for nb in range(2):
                        wt, wkey = sto.get(nb, ahead=2)
                        for nn in range(4):
                            n = nb * 4 + nn
                            for g in range(NG):
                                gs = slice(g * GT, (g + 1) * GT)
                                bank = zbank(0, 4)
                                for m in range(KC):
                                    op("pe", lambda e: e.matmul(ps[bank][:], lhsT=wt[:, m * 512 + nn * 128:m * 512 + nn * 128 + 128],
                                                                rhs=merged[:, m, gs], start=(m == 0), stop=(m == KC - 1)),
                                       reads=[wkey, ("merged", m, g)], writes=[("ps", bank)])
                                op("dve", lambda e: e.scalar_tensor_tensor(out=x[:, n, gs], in0=ps[bank][:], scalar=mod[:, l, 16 + n:16 + n + 1],
                                                                           in1=x[:, n, gs], op0=ALU.mult, op1=ALU.add),
                                   reads=[("ps", bank), ("x", n, g)] + modkeys(l, "g_m"), writes=[("x", n, g)])
            chk("merge")
            if l == 0:
                tap("x1", x[:], [("x", m_, g_) for m_ in range(KC) for g_ in range(NG)])
            fitems = [([w_kc(w13[l], j * 128, 128), w_kc(w13[l], DFF + j * 128, 128)], 128, (8, 128)) for j in range(11)]
            pre["stf0"] = Stream(fitems)
            pre["stf0"].ensure(2)
            with ExitStack() as es:
                S.fence()
                rms_to_h(l, af[:, l, :], 24, es)
            with ExitStack() as es:
                S.fence()
                act = T(es, U("act"), [128, 11, NT], BF16)
                sil = [T(es, U("sil"), [128, GT]) for _ in range(2)]
                for hf in range(2):
                    items = []
                    for jj in range(11):
                        j = hf * 11 + jj
                        items.append(([w_kc(w13[l], j * 128, 128), w_kc(w13[l], DFF + j * 128, 128)], 128, (8, 128)))
                    stf = pre.pop("stf0") if hf == 0 else Stream(items)
                    for jj in range(11):
                        wt, wkey = stf.get(jj)
                        for g in range(NG):
                            gs = slice(g * GT, (g + 1) * GT)
                            b1 = zbank(0, 4)
                            zmm(b1, wt, wkey, lambda kc: kc * 128, 128, g)
                            st_ = sil[g % 2]
                            op("act", lambda e: e.activation(out=st_[:], in_=ps[b1][:], func=AF.Silu), reads=[("ps", b1)], writes=[st_.name])
                            b2 = zbank(0, 4)
                            zmm(b2, wt, wkey, lambda kc: (8 + kc) * 128, 128, g)
                            op("dve", lambda e: e.tensor_tensor(out=act[:, jj, gs], in0=ps[b2][:], in1=st_[:], op=ALU.mult),
                               reads=[("ps", b2), st_.name], writes=[("act", jj, g)])
                    items = []
                    for npair in range(4):
                        items.append((bass.AP(w2.tensor, w2[l, hf * 1408:hf * 1408 + 1, npair * 256:npair * 256 + 1].offset,
                                              [[D, 128], [128 * D, 11], [1, 256]]), 128, (11, 256)))
                    st2 = Stream(items)
                    for npair in range(4):
                        wt, wkey = st2.get(npair)
                        for nn in range(2):
                            n = npair * 2 + nn
                            for g in range(NG):
                                gs = slice(g * GT, (g + 1) * GT)
                                bank = zbank(0, 4)
                                for jj in range(11):
                                    op("pe", lambda e: e.matmul(ps[bank][:], lhsT=wt[:, jj * 256 + nn * 128:jj * 256 + nn * 128 + 128],
                                                                rhs=act[:, jj, gs], start=(jj == 0), stop=(jj == 10)),
                                       reads=[wkey, ("act", jj, g)], writes=[("ps", bank)])
                                op("dve", lambda e: e.scalar_tensor_tensor(out=x[:, n, gs], in0=ps[bank][:], scalar=mod[:, l, 40 + n:40 + n + 1],
                                                                           in1=x[:, n, gs], op0=ALU.mult, op1=ALU.add),
                                   reads=[("ps", bank), ("x", n, g)] + modkeys(l, "g_f"), writes=[("x", n, g)])
            if l == 0:
                tap("x2", x[:], [("x", m_, g_) for m_ in range(KC) for g_ in range(NG)])
          except _Stop:
            break
        with ExitStack() as es:
            S.fence()
            rms_to_h(0, cp("fng"), None, es)
        S.finish("sp")
    print(f"[kernel] instructions={S.ninst} waits={S.nwaits} per-engine={S.cnt}")
    return nc


_NC_CACHE = {}


def make_in_maps(inp):
    x_prompt = np.asarray(inp["x_prompt"], np.float32)
    x_sample = np.asarray(inp["x_sample"], np.float32)
    state = np.asarray(inp["state_hgrn"], np.float32)
    c = np.asarray(inp["c"], np.float32)
    c_ctx = np.asarray(inp["c_ctx"], np.float32)
    cbf = make_cbf()
    mpk_s, _ = make_mpk(True)
    mpk_p, _ = make_mpk(False)
    wnames = ["ada_w", "w_in", "w_out_a", "w_out_b", "w_out_c", "pool_w", "w_o", "ffn_w13", "ffn_w2"]
    weights = {n: np.ascontiguousarray(np.asarray(inp[n], np.float32)) for n in wnames}
    in_maps = []
    for core in range(8):
        if core < 4:
            xt = x_sample[core]
            cvec = c[core]
            s0 = state[core].reshape(L, 2, 2, 128, 64)
            mpk = mpk_s
        else:
            xt = x_prompt[(core - 4) * 8:(core - 3) * 8].reshape(NT, D)
            cvec = c_ctx
            s0 = np.zeros((L, 2, 2, 128, 64), np.float32)
            mpk = mpk_p
        m = {"xT": np.ascontiguousarray(xt.T), "cpk": make_cpk(inp, cvec), "mpk": mpk, "cbf": cbf,
             "s0": np.ascontiguousarray(s0)}
        m.update(weights)
        in_maps.append(m)
    return in_maps


def kernel(**inputs):
    if "nc" not in _NC_CACHE:
        _NC_CACHE["nc"] = build_program()
    nc = _NC_CACHE["nc"]
    in_maps = make_in_maps(inputs)
    res = run_bass_kernel_spmd(nc, in_maps, core_ids=list(range(8)))
    rs = res.results
    y_sample = np.stack([np.ascontiguousarray(rs[i]["yT"].T) for i in range(4)], axis=0).astype(np.float32)
    y_prompt = np.concatenate([np.ascontiguousarray(rs[i]["yT"].T).reshape(8, 256, D) for i in range(4, 8)], axis=0).astype(np.float32)
    new_state = np.concatenate([rs[i]["st"].reshape(8, L, 2, 4, 64, 64) for i in range(4, 8)], axis=0).astype(np.float32)
    return (y_prompt, y_sample, new_state)
```

```python
import numpy as np
from contextlib import ExitStack
import concourse.bass as bass
import concourse.mybir as mybir
from concourse.bass_utils import run_bass_kernel_spmd

F32 = mybir.dt.float32
BF16 = mybir.dt.bfloat16
AF = mybir.ActivationFunctionType
ALU = mybir.AluOpType

D = 1024
NT = 2048
KC = 8
NG = 4
GT = 512
L = 2
N_IN = 6912
DFF = 2816
EPS = 1e-6
OFF = dict(a_val=0, a_gate=256, q=512, f_fw=768, f_bw=1024, i=1280, g=1536, c_b=1792, c_c=2048, c_x=2304,
           d=2560, merge=2816)
WB = 4096
NWB = 3


class Sched:
    def __init__(self, nc, n_dma_sems=32):
        self.nc = nc
        self.engs = {"pe": nc.tensor, "dve": nc.vector, "act": nc.scalar, "pool": nc.gpsimd, "sp": nc.sync}
        self.sem = {k: nc.alloc_semaphore(name="s_" + k) for k in self.engs}
        self.cnt = {k: 0 for k in self.engs}
        self.seen = {k: {} for k in self.engs}
        self.dsems = [nc.alloc_semaphore(name=f"d{i}") for i in range(n_dma_sems)]
        self.dcnt = [0] * n_dma_sems
        hn = n_dma_sems // 2
        self.drange = {"sp": (0, hn), "pool": (hn, hn + 4), "poolw": (hn + 4, n_dma_sems), "act": (0, hn)}
        self.dnext = {"sp": 0, "pool": hn, "poolw": hn + 4}
        self.lastw = {}
        self.reads = {}
        self.nwaits = 0
        self.ninst = 0
        self.tfin = {}
        self.efree = {k: 0.0 for k in self.engs}
        self.actset = None

    def _wait(self, e, tok):
        key, val, sem = tok
        if key == e and e == "pe":
            return
        if self.seen[e].get(key, 0) >= val:
            return
        self.engs[e].wait_ge(sem, val)
        self.nwaits += 1
        self.seen[e][key] = val

    @staticmethod
    def _flat(keys):
        out = []
        for k in keys:
            if isinstance(k, list):
                out.extend(k)
            else:
                out.append(k)
        return out

    def _deps(self, e, reads, writes):
        toks = []
        for r in reads:
            if r in self.lastw:
                toks.append(self.lastw[r])
        for w in writes:
            if w in self.lastw:
                toks.append(self.lastw[w])
            toks.extend(self.reads.get(w, {}).values())
        for t in toks:
            self._wait(e, t)

    def _commit(self, tok, reads, writes):
        for r in reads:
            d = self.reads.setdefault(r, {})
            old = d.get(tok[0])
            if old is None or old[1] < tok[1]:
                d[tok[0]] = tok
        for w in writes:
            self.lastw[w] = tok
            self.reads[w] = {}

    def _dep_toks(self, reads, writes):
        toks = []
        for r in reads:
            if r in self.lastw:
                toks.append(self.lastw[r])
        for w in writes:
            if w in self.lastw:
                toks.append(self.lastw[w])
            toks.extend(self.reads.get(w, {}).values())
        return toks

    def est_start(self, e, reads, writes, aset=None):
        t = self.efree[e]
        for tk in self._dep_toks(self._flat(reads), self._flat(writes)):
            t = max(t, self.tfin.get((tk[0], tk[1]), 0.0) + (0.12 if tk[0] != "pe" or e != "pe" else 0.0))
        if e == "act" and aset is not None and aset != self.actset:
            t += 1.3
        return t

    def op(self, e, fn, reads=(), writes=(), expect=(), dur=None, aset=None):
        reads, writes = self._flat(reads), self._flat(writes)
        for (k_, t_) in expect:
            assert self.lastw.get(k_) == t_, f"emission-order violation on {k_}"
        t0 = self.est_start(e, reads, writes, aset)
        if e == "act" and aset is not None:
            self.actset = aset
        if e == "pool":
            for t in getattr(self, "fence_toks", []):
                self._wait(e, t)
        self._deps(e, reads, writes)
        inst = fn(self.engs[e])
        self.cnt[e] += 1
        self.ninst += 1
        inst.then_inc(self.sem[e], 1)
        tok = (e, self.cnt[e], self.sem[e])
        self.efree[e] = t0 + (dur if dur is not None else {"pe": 0.25, "act": 0.6, "dve": 0.65, "pool": 1.2}.get(e, 0.5))
        self.tfin[(e, self.cnt[e])] = self.efree[e]
        self._commit(tok, reads, writes)
        return tok

    def dma(self, e, out, in_, reads=(), writes=(), scoped=False, dma_us=None):
        reads, writes = self._flat(reads), self._flat(writes)
        if e == "sp" or scoped:
            for t in getattr(self, "fence_toks", []):
                self._wait(e, t)
        self._deps(e, reads, writes)
        rk = "poolw" if (e == "pool" and dma_us is not None) else e
        lo, hi = self.drange[rk]
        i = self.dnext[rk]
        self.dnext[rk] = lo + (i + 1 - lo) % (hi - lo)
        if self.dcnt[i] > 0:
            self._wait(e, (("d", i), self.dcnt[i], self.dsems[i]))
        t0 = max(self.efree.values())
        for tk in self._dep_toks(reads, writes):
            t0 = max(t0, self.tfin.get((tk[0], tk[1]), 0.0))
        self.engs[e].dma_start(out=out, in_=in_).then_inc(self.dsems[i], 16)
        self.ninst += 1
        self.dcnt[i] += 16
        tok = (("d", i), self.dcnt[i], self.dsems[i])
        self.tfin[(("d", i), self.dcnt[i])] = t0 + (dma_us if dma_us is not None else 3.0)
        self._commit(tok, reads, writes)
        return tok

    def fence(self):
        toks = [(k, self.cnt[k], self.sem[k]) for k in ("pe", "dve", "act", "pool") if self.cnt[k] > 0]
        wlo, whi = self.drange["poolw"]
        toks += [(("d", i), self.dcnt[i], self.dsems[i]) for i in range(len(self.dsems)) if self.dcnt[i] > 0 and not (wlo <= i < whi)]
        for e in ("pe", "dve", "act"):
            for t in toks:
                self._wait(e, t)
        self.fence_toks = toks

    def finish(self, e="sp"):
        for k in self.engs:
            if self.cnt[k] > 0:
                self._wait(e, (k, self.cnt[k], self.sem[k]))
        for i, s in enumerate(self.dsems):
            if self.dcnt[i] > 0:
                self._wait(e, (("d", i), self.dcnt[i], s))


def apx(base, free):
    return bass.AP(base.tensor, base.offset, [list(base.ap[0])] + [list(f) for f in free])


class Pack:
    def __init__(self):
        self.cols = {}
        self.parts = []
        self.n = 0

    def add(self, name, arr):
        arr = np.ascontiguousarray(arr, dtype=np.float32).reshape(128, -1)
        self.cols[name] = (self.n, arr.shape[1])
        self.parts.append(arr)
        self.n += arr.shape[1]

    def build(self):
        return np.ascontiguousarray(np.concatenate(self.parts, axis=1))


def fm(vec, nch):
    return np.asarray(vec, np.float32).reshape(nch, 128).T


def cpk_layout():
    p = Pack()
    z = lambda n: np.zeros((128, n), np.float32)
    p.add("cvec", z(8))
    p.add("fng", z(8))
    p.add("lbl", z(L * 4))
    for l in range(L):
        p.add(f"adab{l}", z(48))
        p.add(f"nmg{l}", z(8))
        p.add(f"nfg{l}", z(8))
        p.add(f"psc{l}", z(8))
        p.add(f"caw{l}", z(62))
        p.add(f"cab{l}", z(2))
        p.add(f"lag{l}", z(2))
        p.add(f"lab{l}", z(2))
        p.add(f"hng{l}", z(2))
        p.add(f"ccw{l}", z(6))
    return p.cols, p.n


def make_cpk(inp, cvec):
    p = Pack()
    p.add("cvec", fm(cvec, 8))
    p.add("fng", fm(inp["final_norm_g"], 8))
    lbl = np.asarray(inp["hgrn_lb_logits"], np.float32)
    p.add("lbl", lbl.reshape(L, 2, 2, 128).transpose(3, 0, 1, 2).reshape(128, L * 4))
    for l in range(L):
        p.add(f"adab{l}", fm(inp["ada_b"][l], 48))
        p.add(f"nmg{l}", fm(inp["norm_mix_g"][l], 8))
        p.add(f"nfg{l}", fm(inp["norm_ffn_g"][l], 8))
        p.add(f"psc{l}", fm(inp["pool_scale"][l], 8))
        caw = np.asarray(inp["conv_a_w"][l], np.float32)
        p.add(f"caw{l}", caw.reshape(31, 2, 128).transpose(2, 1, 0).reshape(128, 62))
        p.add(f"cab{l}", fm(inp["conv_a_b"][l], 2))
        p.add(f"lag{l}", fm(inp["ln_a_g"][l], 2))
        p.add(f"lab{l}", fm(inp["ln_a_b"][l], 2))
        p.add(f"hng{l}", fm(inp["hgrn_norm_g"][l], 2))
        ccw = np.asarray(inp["conv_c_w"][l], np.float32)
        p.add(f"ccw{l}", ccw.reshape(3, 2, 128).transpose(2, 1, 0).reshape(128, 6))
    return p.build()


def make_mpk(is_sample):
    p = Pack()
    rep = lambda v: np.broadcast_to(np.asarray(v, np.float32).reshape(1, -1), (128, np.asarray(v).size))
    s = 1.0 if is_sample else 0.0
    p.add("keep", rep([s]))
    p.add("fs", rep([s]))
    p.add("epsD", rep([EPS]))
    pos = np.arange(256)
    p.add("mCL", rep((pos != 0) * (1.0 - s)))
    p.add("mCR", rep((pos != 255) * (1.0 - s)))
    blk = np.arange(1, 32)
    p.add("FL", rep((blk % 4 != 0) * (1.0 - s)))
    blk0 = np.arange(0, 31)
    p.add("FR", rep((blk0 % 4 != 3) * (1.0 - s)))
    icp = np.zeros((128, 2, 256), np.float32)
    ics = np.zeros((128, 2, 32), np.float32)
    for ch in range(2):
        for half in range(2):
            w = (2, 4, 8, 16)[2 * ch + half]
            for Lseq, tab in ((256, icp), (32, ics)):
                t = np.arange(Lseq)
                lo = np.clip(t - w // 2, 0, Lseq)
                hi = np.clip(t + w - w // 2, 0, Lseq)
                tab[64 * half:64 * half + 64, ch, :] = (1.0 / (hi - lo).astype(np.float32))[None, :]
    p.add("icp", icp.reshape(128, 512) * (1.0 - s))
    p.add("ics", ics.reshape(128, 64) * s)
    p.add("rmask", rep((np.arange(512) % 32 != 0) * 1.0))
    sidx = np.arange(128)[:, None]
    tidx = np.arange(128)[None, :]
    same = (sidx // 32) == (tidx // 32)
    p.add("triF", (same & (sidx <= tidx)).astype(np.float32))
    p.add("triB", (same & (sidx >= tidx)).astype(np.float32))
    p.add("cmask", ((np.arange(128)[:, None] // 32) == np.arange(4)[None, :]).astype(np.float32))
    return p.build(), p.cols


def make_cbf():
    ident = np.eye(128, dtype=np.float32)
    ones = np.ones((128, 128), np.float32)
    blk = np.zeros((128, 128), np.float32)
    blk[:64, :64] = 1.0
    blk[64:, 64:] = 1.0
    return np.ascontiguousarray(np.concatenate([ident, ones, blk], axis=1))


class _Stop(Exception):
    pass


class OP:
    __slots__ = ("e", "fn", "reads", "writes", "expect", "n", "k", "aset")

    def __init__(self, e, fn, reads=(), writes=(), expect=(), n=512, k=None, aset=None):
        self.e, self.fn, self.reads, self.writes, self.expect, self.n, self.k, self.aset = e, fn, reads, writes, expect, n, k, aset

    def dur(self):
        if self.e == "pe":
            return max(0.06, self.n / 2300.0)
        if self.e == "act":
            return 0.22 + self.n * 0.00075
        if self.e == "dve":
            if self.k == "recip":
                return self.n * 0.0065
            d = 0.1 + self.n * 0.0011
            return d * 1.9 if self.k == "scan" else d
        return 0.1 + self.n * 0.0019


def run_sched(S, *gens, slack=0.05):
    pend = []
    for g_ in gens:
        if g_ is None:
            continue
        try:
            pend.append([g_, next(g_)])
        except StopIteration:
            pass
    while pend:
        best, bt = None, None
        for i_, (g_, o) in enumerate(pend):
            t = S.est_start(o.e, o.reads, o.writes, o.aset)
            if t <= S.efree[o.e] + slack:
                best = i_
                break
            if bt is None or t < bt - 1e-9:
                best, bt = i_, t
        g_, o = pend[best]
        tok = S.op(o.e, o.fn, o.reads, o.writes, o.expect, dur=o.dur(), aset=o.aset)
        try:
            pend[best][1] = g_.send(tok)
        except StopIteration:
            pend.pop(best)


def run_threads(*gens):
    live = [g_ for g_ in gens if g_ is not None]
    while live:
        for g_ in list(live):
            try:
                next(g_)
            except StopIteration:
                live.remove(g_)


def build_program(n_layers=L, taps=(), stop_after=None):
    nc = bass.Bass("TRN2", target_bir_lowering=False)

    def chk(name):
        if stop_after == name:
            raise _Stop()
    CC, NCPK = cpk_layout()
    _, MC = make_mpk(True)
    NMPK = sum(v[1] for v in MC.values())

    def din(name, shape):
        return nc.dram_tensor(name, shape, F32, kind="ExternalInput").ap()

    xT_d = din("xT", [D, NT])
    cpk_d = din("cpk", [128, NCPK])
    mpk_d = din("mpk", [128, NMPK])
    cbf_d = din("cbf", [128, 384])
    s0_d = din("s0", [L, 2, 2, 128, 64])
    ada_w = din("ada_w", [L, D, 6 * D])
    w_in = din("w_in", [L, D, N_IN])
    w_oa = din("w_out_a", [L, 256, D])
    w_ob = din("w_out_b", [L, 256, D])
    w_oc = din("w_out_c", [L, 256, D])
    pool_w = din("pool_w", [L, 4, 64, 256])
    w_o = din("w_o", [L, D, D])
    w13 = din("ffn_w13", [L, D, 2 * DFF])
    w2 = din("ffn_w2", [L, DFF, D])
    yT_d = nc.dram_tensor("yT", [D, NT], F32, kind="ExternalOutput").ap()
    st_d = nc.dram_tensor("st", [8, L, 2, 2, 128, 64], F32, kind="ExternalOutput").ap()
    tap_d = {}
    for (tname, shape) in taps:
        tap_d[tname] = nc.dram_tensor("tap_" + tname, list(shape), F32, kind="ExternalOutput").ap()

    S = Sched(nc)
    op = S.op
    uid = [0]

    def U(prefix):
        uid[0] += 1
        return f"{prefix}{uid[0]}"

    with ExitStack() as es0:
        def T(es, name, shape, dt=F32):
            cm = nc.sbuf_tensor(name, list(shape), dt)
            hnd = cm.__enter__()
            es.callback(lambda: cm.__exit__(None, None, None))
            return hnd

        x = T(es0, "x", [128, KC, NT])
        h = T(es0, "h", [128, KC, NT], BF16)
        wp = [T(es0, f"wp{i}", [128, WB], BF16) for i in range(NWB)]
        cpk = T(es0, "cpk_sb", [128, NCPK])
        mpk = T(es0, "mpk_sb", [128, NMPK])
        cbf = T(es0, "cbf_sb", [128, 384], BF16)
        mod = T(es0, "mod", [128, L, 48])
        am = T(es0, "am", [128, L, 8])
        af = T(es0, "af", [128, L, 8])
        lbt = T(es0, "lbt", [128, L, 4])
        omlt = T(es0, "omlt", [128, L, 4])
        cs_bf = T(es0, "cs_bf", [128, 8], BF16)
        ps = [es0.enter_context(nc.psum_tensor(f"ps{i}", [128, 512], F32)) for i in range(7)]
        pst = es0.enter_context(nc.psum_tensor("pst", [128, 1024], BF16))

        def cp(name, a=0, n=None):
            o, w = CC[name]
            n = w - a if n is None else n
            return cpk[:, o + a:o + a + n]

        def mp(name, a=0, n=None):
            o, w = MC[name]
            n = w - a if n is None else n
            return mpk[:, o + a:o + a + n]

        ident = cbf[:, 0:128]
        ones_bf = cbf[:, 128:256]
        blk_bf = cbf[:, 256:384]

        wstate = {"i": 0, "pinned": set()}

        def wload(src_ap, nparts, shape_free, pin=False, buf=None):
            i = wstate["i"] if buf is None else buf
            while i in wstate["pinned"]:
                assert buf is None
                i = (i + 1) % NWB
            wstate["i"] = (i + 1) % NWB
            if pin:
                wstate["pinned"].add(i)
            pieces = src_ap if isinstance(src_ap, list) else [src_ap]
            n = int(np.prod(shape_free))
            assert n * len(pieces) <= WB and len(pieces) <= 4
            keys = [("wp", i, k) for k in range(4)]
            for k, pc in enumerate(pieces):
                dst = wp[i][0:nparts, k * n:(k + 1) * n]
                if len(shape_free) == 2:
                    dst = dst.rearrange("p (a b) -> p a b", b=shape_free[1])
                wk_ = [keys[k]] + (keys[len(pieces):] if k == 0 else [])
                S.dma("pool", dst, pc, writes=wk_, dma_us=2.0 + nparts * n * 4 / 150e3)
            return wp[i], keys

        def wunpin():
            wstate["pinned"].clear()

        def w_kc(dram2d, c0, ncols):
            return dram2d.rearrange("(kc p) n -> p kc n", p=128)[:, :, c0:c0 + ncols]

        class Stream:
            def __init__(self, items):
                self.items = items
                self.loaded = {}
                self.n = 0

            def ensure(self, upto):
                while self.n <= upto and self.n < len(self.items):
                    src, npart, shp = self.items[self.n]
                    self.loaded[self.n] = wload(src, npart, shp)
                    self.n += 1

            def get(self, i, ahead=NWB - 1):
                self.ensure(i + ahead)
                return self.loaded.pop(i)

        zrot = {"i": 0}

        def zbank(lo=0, hi=2):
            k = (lo, hi)
            i = zrot.get(k, lo)
            zrot[k] = lo + (i + 1 - lo) % (hi - lo)
            return i

        def zmm(bank, wt, wkey, woff, M, g, hkeys=True, wstride=None):
            for kc in range(KC):
                lhsT = wt[:, woff(kc):woff(kc) + M]
                op("pe", lambda e: e.matmul(ps[bank][0:M, :], lhsT=lhsT, rhs=h[:, kc, g * GT:(g + 1) * GT],
                                            start=(kc == 0), stop=(kc == KC - 1)),
                   reads=[wkey, ("h", g)], writes=[("ps", bank)])

        S.dma("sp", cpk[:], cpk_d, writes=["cpk"])
        S.dma("sp", mpk[:], mpk_d, writes=["mpk"])
        S.dma("pool", cbf[:], cbf_d, writes=["cbf"])
        def x_load(g):
            for kc in range(KC):
                S.dma("sp", x[:, kc, g * GT:(g + 1) * GT], xT_d[kc * 128:(kc + 1) * 128, g * GT:(g + 1) * GT],
                      writes=[("x", kc, g)])

        x_load(0)
        op("act", lambda e: e.activation(out=cs_bf[:], in_=cp("cvec"), func=AF.Silu), reads=["cpk"], writes=["cs_bf"])
        op("dve", lambda e: e.memset(lbt[:, 0, :], 0.0), writes=["lbt"])
        op("dve", lambda e: e.tensor_tensor(out=lbt[:, 1, :], in0=cp("lbl", 4, 4), in1=cp("lbl", 0, 4), op=ALU.subtract),
           reads=["cpk"], writes=["lbt"])
        op("act", lambda e: e.activation(out=lbt[:, 1, :], in_=lbt[:, 1, :], func=AF.Sigmoid), reads=["lbt"], writes=["lbt"])
        op("dve", lambda e: e.tensor_scalar(out=omlt[:].rearrange("p a b -> p (a b)"), in0=lbt[:].rearrange("p a b -> p (a b)"),
                                            scalar1=-1.0, scalar2=1.0, op0=ALU.mult, op1=ALU.add),
           reads=["lbt"], writes=["omlt"])
        MODPARTS = {"sh_m": (0, 1), "sc_m": (2, 3), "g_m": (4, 5), "sh_f": (6, 7), "sc_f": (8, 9), "g_f": (10, 11)}

        def modkeys(l, part):
            return [("mod", l, b_) for b_ in MODPARTS[part]]

        mod_pending = {}

        def mod_issue(l, bi):
            mod_pending[(l, bi)] = wload(w_kc(ada_w[l], bi * 512, 512), 128, (8, 512))

        def mod_block(l, bi):
            if (l, bi) not in mod_pending:
                mod_issue(l, bi)
            wt, wkey = mod_pending.pop((l, bi))
            bank = zbank()
            for jj in range(4):
                for kc in range(KC):
                    lhsT = wt[:, kc * 512 + jj * 128: kc * 512 + jj * 128 + 128]
                    op("pe", lambda e: e.matmul(ps[bank][:, jj:jj + 1], lhsT=lhsT, rhs=cs_bf[:, kc:kc + 1],
                                                start=(kc == 0), stop=(kc == KC - 1)),
                       reads=[wkey, "cs_bf"], writes=[("ps", bank)], dur=0.07)
            o_, _ = CC[f"adab{l}"]
            op("dve", lambda e: e.tensor_tensor(out=mod[:, l, bi * 4:(bi + 1) * 4], in0=ps[bank][:, 0:4],
                                                in1=cpk[:, o_ + bi * 4:o_ + bi * 4 + 4], op=ALU.add),
               reads=[("ps", bank), "cpk"], writes=[("mod", l, bi)], dur=0.1)

        def mod_fin(l, which):
            dst, sc0, gname, part = (am, 8, f"nmg{l}", "sc_m") if which == "am" else (af, 32, f"nfg{l}", "sc_f")
            op("dve", lambda e: e.scalar_tensor_tensor(out=dst[:, l, :], in0=mod[:, l, sc0:sc0 + 8], scalar=1.0,
                                                       in1=cp(gname), op0=ALU.add, op1=ALU.mult),
               reads=modkeys(l, part) + ["cpk"], writes=[(which, l)], dur=0.1)

        mod_issue(0, 0)
        mod_issue(0, 1)
        mod_issue(0, 2)
        for bi in range(4):
            mod_block(0, bi)
            if bi == 0:
                mod_issue(0, 3)
        mod_fin(0, "am")
        mod_deferred = [(0, bi) for bi in range(4, 12)] + ([(1, bi) for bi in range(12)] if n_layers > 1 else [])

        def mod_step(n=1, prefetch=True):
            for _ in range(n):
                if not mod_deferred:
                    return
                l_, bi_ = mod_deferred.pop(0)
                mod_block(l_, bi_)
                if mod_deferred and prefetch:
                    mod_issue(*mod_deferred[0])
                if bi_ == 9:
                    mod_fin(l_, "af")
                if bi_ == 3:
                    mod_fin(l_, "am")

        epsD = mp("epsD")

        def rms_to_h(l, avec, shcol, es):
            NSQ, NTMP = 4, 6
            sq = [T(es, U("sq"), [128, GT], BF16) for _ in range(NSQ)]
            rstd = [T(es, U("rstd"), [128, GT]) for _ in range(NG)]
            tmp = [T(es, U("ntmp"), [128, GT]) for _ in range(NTMP)]
            rtok = {}
            SQ_ENG = ["act", "pool", "pool", "act", "pool", "dve", "act", "pool"]
            AF_ENG = ["act", "act", "pool", "act", "act", "pool", "act", "pool"]
            bank = 5

            def stat(g):
                gs = slice(g * GT, (g + 1) * GT)
                for kc in range(KC):
                    sqt = sq[kc % NSQ]
                    xin = x[:, kc, gs]
                    if SQ_ENG[kc] == "act":
                        yield OP("act", lambda e: e.activation(out=sqt[:], in_=xin, func=AF.Square),
                                 reads=[("x", kc, g)], writes=[sqt.name])
                    else:
                        yield OP(SQ_ENG[kc], lambda e: e.tensor_tensor(out=sqt[:], in0=xin, in1=xin, op=ALU.mult),
                                 reads=[("x", kc, g)], writes=[sqt.name])
                    yield OP("pe", lambda e: e.matmul(ps[bank][:], lhsT=ones_bf, rhs=sqt[:], start=(kc == 0), stop=(kc == KC - 1)),
                             reads=[sqt.name, "cbf"], writes=[("ps", bank)], n=512)
                rt = rstd[g]
                yield OP("act", lambda e: e.activation(out=rt[:], in_=ps[bank][:], func=AF.Ln, bias=epsD, scale=1.0 / D),
                         reads=[("ps", bank), "mpk"], writes=[rt.name], aset="lnexp")
                rtok[g] = yield OP("act", lambda e: e.activation(out=rt[:], in_=rt[:], func=AF.Exp, scale=-0.5),
                                   reads=[rt.name], writes=[rt.name], aset="lnexp")

            def apply(g):
                gs = slice(g * GT, (g + 1) * GT)
                rt = rstd[g]
                for kc in range(KC):
                    tt = tmp[(g * KC + kc) % NTMP]
                    yield OP("dve", lambda e: e.tensor_tensor(out=tt[:], in0=x[:, kc, gs], in1=rt[:], op=ALU.mult),
                             reads=[("x", kc, g), rt.name], writes=[tt.name], expect=[(rt.name, rtok[g])])
                    en = AF_ENG[kc]
                    if shcol is None:
                        if en == "act":
                            yield OP("act", lambda e: e.activation(out=tt[:], in_=tt[:], func=AF.Identity, scale=avec[:, kc:kc + 1]),
                                     reads=[tt.name, "cpk"], writes=[tt.name])
                        else:
                            yield OP(en, lambda e: e.tensor_scalar(out=tt[:], in0=tt[:], scalar1=avec[:, kc:kc + 1], scalar2=0.0,
                                                                   op0=ALU.mult, op1=ALU.add),
                                     reads=[tt.name, "cpk"], writes=[tt.name])
                        S.dma("sp", yT_d[kc * 128:(kc + 1) * 128, gs], tt[:], reads=[tt.name])
                    else:
                        rd = [tt.name, ("am", l), ("af", l)] + modkeys(l, "sh_m") + modkeys(l, "sh_f")
                        if en == "act":
                            yield OP("act", lambda e: e.activation(out=h[:, kc, gs], in_=tt[:], func=AF.Identity,
                                                                   bias=mod[:, l, shcol + kc:shcol + kc + 1], scale=avec[:, kc:kc + 1]),
                                     reads=rd, writes=[("h", g)])
                        else:
                            yield OP(en, lambda e: e.tensor_scalar(out=h[:, kc, gs], in0=tt[:], scalar1=avec[:, kc:kc + 1],
                                                                   scalar2=mod[:, l, shcol + kc:shcol + kc + 1],
                                                                   op0=ALU.mult, op1=ALU.add),
                                     reads=rd, writes=[("h", g)])

            run_sched(S, stat(0))
            for g in range(NG):
                run_sched(S, stat(g + 1) if g + 1 < NG else None, apply(g))

        def tap(name, src_ap, keys):
            if name in tap_d:
                S.dma("pool", tap_d[name], src_ap, reads=keys, scoped=True)

        for l in range(n_layers if stop_after != "prep" else 0):
          try:
            with ExitStack() as esl:
                S.fence()
                ob_out = T(esl, U("ob_out"), [128, 2, NT], BF16)
                pre = {}
                wcol = lambda nm, p_: w_kc(w_in[l], OFF[nm] + p_ * 128, 128)
                pre["BA0"] = wload([wcol("q", 0), wcol("f_fw", 0), wcol("f_bw", 0)], 128, (8, 128), buf=0)
                pre["BI"] = wload([wcol("i", 0), wcol("g", 0), wcol("i", 1), wcol("g", 1)], 128, (8, 128), buf=1)
                pre["BA1"] = wload([wcol("q", 1), wcol("f_fw", 1), wcol("f_bw", 1)], 128, (8, 128), buf=2)
                with ExitStack() as es:
                    S.fence()
                    if l == 0:
                        for g_ in range(1, NG):
                            x_load(g_)
                    rms_to_h(l, am[:, l, :], 0, es)
                chk("norm1")
                if l == 0:
                    tap("h0", h[:], [("h", g_) for g_ in range(NG)])

                with ExitStack() as es:
                    S.fence()
                    v_tm = T(es, U("v_tm"), [128, 16, 128], BF16)
                    o_acc = T(es, U("o_acc"), [128, NT])
                    f_t = T(es, U("f_t"), [128, GT])
                    k_t = T(es, U("k_t"), [128, GT])
                    b_t = T(es, U("b_t"), [128, GT])
                    bq_t = T(es, U("bq_t"), [128, GT])
                    kp32 = T(es, U("kp32"), [128, GT])
                    e1s = [T(es, U("e1_t"), [128, GT]) for _ in range(2)]
                    qps = [T(es, U("qp"), [128, GT], BF16) for _ in range(3)]
                    kps = [T(es, U("kp"), [128, GT], BF16) for _ in range(3)]
                    khs = [T(es, U("kh"), [128, GT], BF16) for _ in range(2)]
                    sqb = T(es, U("sqb"), [128, GT], BF16)
                    Sbuf = T(es, U("Sbuf"), [128, 16, 128])
                    Sbfs = [T(es, U("Sbf"), [128, 16, 128], BF16) for _ in range(2)]
                    Sk = [T(es, U("Sk"), [128, 128]) for _ in range(2)]
                    Skb = [T(es, U("Skb"), [128, 128], BF16) for _ in range(4)]
                    s0b = T(es, U("s0b"), [128, 4, 128], BF16)
                    khT = [T(es, U("khT"), [128, 4, 128], BF16) for _ in range(2)]
                    PT = [T(es, U("PT"), [128, 2, 128], BF16) for _ in range(2)]
                    skn = [0]
                    s0t = T(es, U("s0t"), [128, 4, 128])
                    op("dve", lambda e: e.memset(s0t[:].rearrange("p a b -> p (a b)"), 0.0), writes=["s0t"])
                    for dr_ in range(2):
                        for p_ in range(2):
                            S.dma("sp", s0t[0:64, dr_ * 2 + p_, 0:64], s0_d[l, dr_, p_, 0:64, :], writes=[("s0t", dr_, p_, 0)], reads=["s0t"])
                            S.dma("sp", s0t[64:128, dr_ * 2 + p_, 64:128], s0_d[l, dr_, p_, 64:128, :], writes=[("s0t", dr_, p_, 1)], reads=["s0t"])
                    s0b_tok = op("act", lambda e: e.activation(out=s0b[:].rearrange("p a b -> p (a b)"), in_=s0t[:].rearrange("p a b -> p (a b)"), func=AF.Copy),
                                 reads=["s0t"] + [("s0t", a_, b_, c_) for a_ in range(2) for b_ in range(2) for c_ in range(2)], writes=["s0b"])
                    PB = 0
                    KVB = (1, 4)
                    for p in range(2):
                        cols1 = [OFF["q"] + p * 128, OFF["f_fw"] + p * 128, OFF["f_bw"] + p * 128]
                        cols2 = [OFF["i"] + p * 128, OFF["g"] + p * 128]
                        wA, kA = pre["BA0"] if p == 0 else pre["BA1"]
                        wBb, kB = pre["BI"]
                        wq = lambda kc: (0 * 8 + kc) * 128
                        wf = [lambda kc: (1 * 8 + kc) * 128, lambda kc: (2 * 8 + kc) * 128]
                        wi = lambda kc, p=p: ((2 * p) * 8 + kc) * 128
                        wg = lambda kc, p=p: ((2 * p + 1) * 8 + kc) * 128
                        def vt_thread():
                            for tq in range(4):
                                bank = 2 + tq % 2
                                for ti in range(4):
                                    tt = tq * 4 + ti
                                    for kc in range(KC):
                                        yield OP("pe", lambda e: e.matmul(ps[bank][:, ti * 128:(ti + 1) * 128],
                                                                          lhsT=h[:, kc, tt * 128:(tt + 1) * 128],
                                                                          rhs=wBb[:, wi(kc):wi(kc) + 128],
                                                                          start=(kc == 0), stop=(kc == KC - 1)),
                                                 reads=[kB, ("h", tq)], writes=[("ps", bank)], n=128)
                                yield OP("act", lambda e: e.activation(out=v_tm[:, tq * 4:(tq + 1) * 4, :].rearrange("p a b -> p (a b)"),
                                                                       in_=ps[bank][:], func=AF.Copy),
                                         reads=[("ps", bank)], writes=[("v_tm", tq)])
                        jobs = [(0, g_) for g_ in (0, 1, 2, 3)] + [(1, g_) for g_ in (3, 2, 1, 0)]
                        NJ = len(jobs)
                        chain = {}
                        s_start = {}

                        ptok = {}

                        def zmm_g(bank, wt, wkey, woff, g):
                            for kc in range(KC):
                                lhsT = wt[:, woff(kc):woff(kc) + 128]
                                yield OP("pe", lambda e: e.matmul(ps[bank][:], lhsT=lhsT, rhs=h[:, kc, g * GT:(g + 1) * GT],
                                                                  start=(kc == 0), stop=(kc == KC - 1)),
                                         reads=[wkey, ("h", g)], writes=[("ps", bank)], n=512)

                        def prep(k):
                            dr, g = jobs[k]
                            e1_t, qp, kp, kh = e1s[k % 2], qps[k % 3], kps[k % 3], khs[k % 2]
                            lbc = lbt[:, l, dr * 2 + p:dr * 2 + p + 1]
                            omc = omlt[:, l, dr * 2 + p:dr * 2 + p + 1]
                            endc = 31 if dr == 0 else 0
                            tk = ptok[k] = {}
                            yield from zmm_g(PB, wA, kA, wf[dr], g)
                            yield OP("act", lambda e: e.activation(out=f_t[:], in_=ps[PB][:], func=AF.Sigmoid),
                                     reads=[("ps", PB)], writes=["f_t"], aset="sig")
                            yield from zmm_g(PB, wA, kA, wq, g)
                            yield OP("dve", lambda e: e.tensor_scalar(out=f_t[:], in0=f_t[:], scalar1=omc, scalar2=lbc,
                                                                      op0=ALU.mult, op1=ALU.add),
                                     reads=["f_t", "lbt", "omlt"], writes=["f_t"])
                            yield OP("dve", lambda e: e.tensor_scalar(out=k_t[:], in0=f_t[:], scalar1=-1.0, scalar2=1.0, op0=ALU.mult, op1=ALU.add),
                                     reads=["f_t"], writes=["k_t"])
                            yield OP("act", lambda e: e.activation(out=f_t[:], in_=f_t[:], func=AF.Ln), reads=["f_t"], writes=["f_t"], aset="lnexp")
                            yield OP("dve", lambda e: e.tensor_tensor_scan(out=b_t[:], data0=mp("rmask"), data1=f_t[:], initial=0.0,
                                                                           op0=ALU.mult, op1=ALU.add),
                                     reads=["f_t", "mpk"], writes=["b_t"], k="scan")
                            if dr == 0:
                                bb, bbk = b_t, "b_t"
                            else:
                                yield OP("dve", lambda e: e.tensor_tensor(out=f_t[:], in0=f_t[:], in1=b_t[:], op=ALU.subtract),
                                         reads=["f_t", "b_t"], writes=["f_t"])
                                yield OP("dve", lambda e: e.tensor_tensor(out=bq_t[:].rearrange("p (c j) -> p c j", j=32),
                                                                          in0=f_t[:].rearrange("p (c j) -> p c j", j=32),
                                                                          in1=apx(b_t[:, 31:32], [[32, 16], [0, 32]]), op=ALU.add),
                                         reads=["f_t", "b_t"], writes=["bq_t"])
                                bb, bbk = bq_t, "bq_t"
                            tk[e1_t.name] = yield OP("act", lambda e: e.activation(out=e1_t[:], in_=bb[:], func=AF.Exp),
                                                     reads=[bbk], writes=[e1_t.name], aset="lnexp")
                            yield OP("act", lambda e: e.activation(out=kp32[:], in_=bb[:], func=AF.Exp, scale=-1.0),
                                     reads=[bbk], writes=["kp32"], aset="lnexp")
                            tk[qp.name] = yield OP("dve", lambda e: e.tensor_tensor(out=qp[:], in0=ps[PB][:], in1=e1_t[:], op=ALU.mult),
                                                   reads=[("ps", PB), e1_t.name], writes=[qp.name])
                            yield OP("dve", lambda e: e.tensor_tensor(out=kp32[:], in0=kp32[:], in1=k_t[:], op=ALU.mult),
                                     reads=["kp32", "k_t"], writes=["kp32"])
                            tk[kp.name] = yield OP("act", lambda e: e.activation(out=kp[:], in_=kp32[:], func=AF.Copy),
                                                   reads=["kp32"], writes=[kp.name])
                            tk[kh.name] = yield OP("dve", lambda e: e.tensor_tensor(out=kh[:].rearrange("p (c j) -> p c j", j=32),
                                                                                    in0=kp32[:].rearrange("p (c j) -> p c j", j=32),
                                                                                    in1=apx(e1_t[:, endc:endc + 1], [[32, 16], [0, 32]]), op=ALU.mult),
                                                   reads=["kp32", e1_t.name], writes=[kh.name])

                        def kv(k):
                            dr, g = jobs[k]
                            e1_t, kh, Sbf = e1s[k % 2], khs[k % 2], Sbfs[k % 2]
                            x_e1 = [(e1_t.name, ptok[k][e1_t.name])]
                            x_kh = [(kh.name, ptok[k][kh.name])]
                            endc = 31 if dr == 0 else 0
                            if dr not in chain:
                                chain[dr] = dict(Sprev=s0t[:, dr * 2 + p, :], Sprev_key=[("s0t", dr, p, 0), ("s0t", dr, p, 1)],
                                                 Sprev_b=s0b[:, dr * 2 + p, :], Sprev_bkey="s0b", Sprev_btok=s0b_tok, first=True)
                            cs = chain[dr]
                            tiles = [0, 1, 2, 3] if dr == 0 else [3, 2, 1, 0]
                            s_start[k] = {}
                            for ti in tiles:
                                tt = g * 4 + ti
                                yield OP("pe", lambda e: e.transpose(pst[:, ti * 128:(ti + 1) * 128], kh[:, ti * 128:(ti + 1) * 128], ident),
                                         reads=[kh.name, "cbf"], writes=[("pst", ti)], expect=x_kh, n=128)
                                kt = khT[ti % 2]
                                yield OP("dve", lambda e: e.tensor_tensor(out=kt[:], in0=apx(pst[:, ti * 128:ti * 128 + 1], [[0, 4], [1, 128]]),
                                                                          in1=apx(mp("cmask"), [[1, 4], [0, 128]]), op=ALU.mult),
                                         reads=[("pst", ti), "mpk"], writes=[kt.name], n=512)
                                kvb = KVB[ti % 2]
                                for j in range(4):
                                    yield OP("pe", lambda e: e.matmul(ps[kvb][:, j * 128:(j + 1) * 128], lhsT=kt[:, j, :],
                                                                      rhs=v_tm[:, tt, :], start=True, stop=True),
                                             reads=[kt.name, ("v_tm", tt // 4)], writes=[("ps", kvb)], n=128)
                                chunks = [0, 1, 2, 3] if dr == 0 else [3, 2, 1, 0]
                                for j in chunks:
                                    c = tt * 4 + j
                                    seg_start = (c % 8 == 0) if dr == 0 else (c % 8 == 7)
                                    if seg_start and not cs["first"]:
                                        seg_prev = (c // 8 - 1) if dr == 0 else (c // 8 + 1)
                                        Sprev, Sprev_key = cs["Sprev"], cs["Sprev_key"]
                                        S.dma("sp", st_d[seg_prev, l, dr, p, 0:64, :], Sprev[0:64, 0:64], reads=[Sprev_key])
                                        S.dma("sp", st_d[seg_prev, l, dr, p, 64:128, :], Sprev[64:128, 64:128], reads=[Sprev_key])
                                        skt = Sk[skn[0] % 2]
                                        sktb = Skb[skn[0] % 4]
                                        skn[0] += 1
                                        yield OP("dve", lambda e: e.tensor_scalar(out=skt[:], in0=Sprev, scalar1=mp("keep"), scalar2=0.0, op0=ALU.mult, op1=ALU.add),
                                                 reads=[Sprev_key, "mpk"], writes=[skt.name], n=128)
                                        cs["Sprev"], cs["Sprev_key"] = skt[:], skt.name
                                        tok_ = yield OP("act", lambda e: e.activation(out=sktb[:], in_=skt[:], func=AF.Copy),
                                                        reads=[skt.name], writes=[sktb.name], n=128)
                                        cs["Sprev_b"], cs["Sprev_bkey"], cs["Sprev_btok"] = sktb[:], sktb.name, tok_
                                    cs["first"] = False
                                    s_start[k][(ti, j)] = (cs["Sprev_b"], cs["Sprev_bkey"], cs["Sprev_btok"])
                                    dcol = ti * 128 + j * 32 + endc
                                    slot = ti * 4 + j
                                    sp_ap, sp_key = cs["Sprev"], cs["Sprev_key"]
                                    yield OP("dve", lambda e: e.scalar_tensor_tensor(out=Sbuf[:, slot, :], in0=sp_ap, scalar=e1_t[:, dcol:dcol + 1],
                                                                                     in1=ps[kvb][:, j * 128:(j + 1) * 128], op0=ALU.mult, op1=ALU.add),
                                             reads=[sp_key, e1_t.name, ("ps", kvb)], writes=[("Sbuf", slot)], expect=x_e1, n=128)
                                    cs["Sprev"], cs["Sprev_key"] = Sbuf[:, slot, :], ("Sbuf", slot)
                                    tok_ = yield OP("act", lambda e: e.activation(out=Sbf[:, slot, :], in_=Sbuf[:, slot, :], func=AF.Copy),
                                                    reads=[("Sbuf", slot)], writes=[(Sbf.name, slot)], n=128)
                                    cs["Sprev_b"], cs["Sprev_bkey"], cs["Sprev_btok"] = Sbf[:, slot, :], (Sbf.name, slot), tok_
                            if k == NJ - 1 or jobs[k + 1][0] != dr:
                                seg_last = 7 if dr == 0 else 0
                                Sprev, Sprev_key = cs["Sprev"], cs["Sprev_key"]
                                S.dma("sp", st_d[seg_last, l, dr, p, 0:64, :], Sprev[0:64, 0:64], reads=[Sprev_key])
                                S.dma("sp", st_d[seg_last, l, dr, p, 64:128, :], Sprev[64:128, 64:128], reads=[Sprev_key])

                        def sc(k):
                            dr, g = jobs[k]
                            qp, kp = qps[k % 3], kps[k % 3]
                            x_qk = [(qp.name, ptok[k][qp.name]), (kp.name, ptok[k][kp.name])]
                            gs = slice(g * GT, (g + 1) * GT)
                            tri = mp("triF") if dr == 0 else mp("triB")
                            tiles = [0, 1, 2, 3] if dr == 0 else [3, 2, 1, 0]
                            obk = (6, 5)
                            for ti in tiles:
                                tt = g * 4 + ti
                                tsl = slice(ti * 128, (ti + 1) * 128)
                                ptt = PT[ti % 2]
                                for hh in range(2):
                                    hp = slice(64 * hh, 64 * hh + 64)
                                    scb = 2 + hh
                                    yield OP("pe", lambda e: e.matmul(ps[scb][:, 0:128], lhsT=kp[hp, tsl], rhs=qp[hp, tsl],
                                                                      start=True, stop=True),
                                             reads=[kp.name, qp.name], writes=[("ps", scb)], expect=x_qk, n=128)
                                    yield OP("dve", lambda e: e.tensor_tensor(out=ptt[:, hh, :], in0=ps[scb][:, 0:128], in1=tri, op=ALU.mult),
                                             reads=[("ps", scb), "mpk"], writes=[(ptt.name, hh)], n=128)
                                for hh in range(2):
                                    hp = slice(64 * hh, 64 * hh + 64)
                                    ob = obk[hh]
                                    yield OP("pe", lambda e: e.matmul(ps[ob][hp, tsl], lhsT=v_tm[:, tt, hp], rhs=ptt[:, hh, :],
                                                                      start=True, stop=False),
                                             reads=[(ptt.name, hh), ("v_tm", tt // 4)], writes=[("ps", ob)], n=128)
                                    for j in range(4):
                                        sap, skey, stok = s_start[k][(ti, j)]
                                        csl = slice(ti * 128 + j * 32, ti * 128 + j * 32 + 32)
                                        yield OP("pe", lambda e: e.matmul(ps[ob][hp, csl], lhsT=sap[hp, hp], rhs=qp[hp, csl],
                                                                          start=False, stop=(j == 3)),
                                                 reads=[skey, qp.name], writes=[("ps", ob)], expect=[(skey, stok)], n=128)
                            for hh in range(2):
                                hp = slice(64 * hh, 64 * hh + 64)
                                ob = obk[hh]
                                if dr == 0:
                                    yield OP("act", lambda e: e.activation(out=o_acc[hp, gs], in_=ps[ob][hp, :], func=AF.Copy),
                                             reads=[("ps", ob)], writes=[("o_acc", g, hh)])
                                else:
                                    yield OP("dve", lambda e: e.tensor_tensor(out=o_acc[hp, gs], in0=o_acc[hp, gs], in1=ps[ob][hp, :], op=ALU.add),
                                             reads=[("ps", ob), ("o_acc", g, hh)], writes=[("o_acc", g, hh)])

                        run_sched(S, prep(0), vt_thread())
                        for step in range(NJ + 1):
                            run_sched(S, prep(step + 1) if step + 1 < NJ else None,
                                      kv(step) if step < NJ else None,
                                      sc(step - 1) if step >= 1 else None)
                        if p == 0:
                            pre["A"] = wload(w_kc(w_in[l], 0, 512), 128, (8, 512), pin=True, buf=0)
                        rst = [f_t, k_t, b_t, bq_t]
                        rstk = ["f_t", "k_t", "b_t", "bq_t"]
                        sqbs = [sqb, khs[0]]
                        sils = [kp32, e1s[0]]
                        silk = ["kp32", e1s[0].name]
                        ptok2 = {}

                        def post_stat(th):
                            bank = 5 if th == 0 else 6
                            sq_ = sqbs[th]
                            for g in (th, th + 2):
                                gs = slice(g * GT, (g + 1) * GT)
                                rg, rgk = rst[g], rstk[g]
                                yield OP("act", lambda e: e.activation(out=sq_[:], in_=o_acc[:, gs], func=AF.Square),
                                         reads=[("o_acc", g, 0), ("o_acc", g, 1)], writes=[sq_.name])
                                yield OP("pe", lambda e: e.matmul(ps[bank][:], lhsT=blk_bf, rhs=sq_[:], start=True, stop=True),
                                         reads=[sq_.name, "cbf"], writes=[("ps", bank)], n=512)
                                yield OP("act", lambda e: e.activation(out=rg[:], in_=ps[bank][:], func=AF.Ln, bias=epsD, scale=1.0 / 64),
                                         reads=[("ps", bank), "mpk"], writes=[rgk], aset="lnexp")
                                yield OP("act", lambda e: e.activation(out=rg[:], in_=rg[:], func=AF.Exp, scale=-0.5),
                                         reads=[rgk], writes=[rgk], aset="lnexp")
                                ptok2[g] = yield OP("dve", lambda e: e.tensor_tensor(out=rg[:], in0=rg[:], in1=o_acc[:, gs], op=ALU.mult),
                                                    reads=[rgk, ("o_acc", g, 0), ("o_acc", g, 1)], writes=[rgk])

                        def post_gate(th):
                            bank = th
                            sl_, slk = sils[th], silk[th]
                            for g in (th, th + 2):
                                gs = slice(g * GT, (g + 1) * GT)
                                rg, rgk = rst[g], rstk[g]
                                yield from zmm_g(bank, wBb, kB, wg, g)
                                yield OP("act", lambda e: e.activation(out=sl_[:], in_=ps[bank][:], func=AF.Silu),
                                         reads=[("ps", bank)], writes=[slk], aset="silu")
                                yield OP("dve", lambda e: e.scalar_tensor_tensor(out=ob_out[:, p, gs], in0=rg[:], scalar=cp(f"hng{l}", p, 1), in1=sl_[:],
                                                                                 op0=ALU.mult, op1=ALU.mult),
                                         reads=[rgk, slk, "cpk"], writes=[("ob_out", p, g)], expect=[(rgk, ptok2[g])])

                        run_sched(S, post_stat(0), post_stat(1))
                        run_sched(S, post_gate(0), post_gate(1))
                chk("B")
                if l == 0:
                    tap("ob0", ob_out[:], [("ob_out", p_, g_) for p_ in range(2) for g_ in range(NG)])

                ua_out = T(esl, U("ua_out"), [128, 2, NT], BF16)
                uc_out = T(esl, U("uc_out"), [128, 2, NT], BF16)
                ud_out = T(esl, U("ud_out"), [128, 2, NT], BF16)

                with ExitStack() as es:
                  S.fence()
                  acc = T(es, U("acc"), [128, 2, NT])
                  with ExitStack() as es2:
                    upad = T(es2, U("upad"), [128, 32, 94])
                    uhi = T(es2, U("uhi"), [128, 32, 94], BF16)
                    ulo = T(es2, U("ulo"), [128, 32, 94], BF16)
                    sgt = T(es2, U("sgt"), [128, GT])
                    ug = T(es2, U("ug"), [128, GT])
                    whb = T(es2, U("whb"), [128, 31], BF16)
                    wlo = T(es2, U("wlo"), [128, 31])
                    dgh = [T(es2, U("dgh"), [128, 128], BF16) for _ in range(4)]
                    dgl = [T(es2, U("dgl"), [128, 128], BF16) for _ in range(4)]
                    op("dve", lambda e: e.memset(upad[:].rearrange("p a b -> p (a b)"), 0.0), writes=["upad"])
                    op("dve", lambda e: e.memset(uhi[:].rearrange("p a b -> p (a b)"), 0.0), writes=["uhi"])
                    op("dve", lambda e: e.memset(ulo[:].rearrange("p a b -> p (a b)"), 0.0), writes=["ulo"])
                    wt, wkey = pre["A"]
                    w1c = cp(f"caw{l}", 31, 31)
                    op("act", lambda e: e.activation(out=whb[:], in_=w1c, func=AF.Copy), reads=["cpk"], writes=["whb"])
                    op("dve", lambda e: e.tensor_tensor(out=wlo[:], in0=w1c, in1=whb[:], op=ALU.subtract), reads=["cpk", "whb"], writes=["wlo"])
                    for ch in range(2):
                        for g in range(NG):
                            bank = zbank(0, 4)
                            zmm(bank, wt, wkey, lambda kc: kc * 512 + 256 + ch * 128, 128, g)
                            op("act", lambda e: e.activation(out=sgt[:], in_=ps[bank][:], func=AF.Sigmoid), reads=[("ps", bank)], writes=["sgt"])
                            bank2 = zbank(0, 4)
                            zmm(bank2, wt, wkey, lambda kc: kc * 512 + ch * 128, 128, g)
                            if ch == 0:
                                op("dve", lambda e: e.tensor_tensor(out=upad[:, g * 8:(g + 1) * 8, 15:79],
                                                                    in0=ps[bank2][:].rearrange("p (a b) -> p a b", b=64),
                                                                    in1=sgt[:].rearrange("p (a b) -> p a b", b=64), op=ALU.mult),
                                   reads=[("ps", bank2), "sgt"], writes=["upad"])
                            else:
                                op("dve", lambda e: e.tensor_tensor(out=ug[:], in0=ps[bank2][:], in1=sgt[:], op=ALU.mult),
                                   reads=[("ps", bank2), "sgt"], writes=["ug"])
                                op("act", lambda e: e.activation(out=uhi[:, g * 8:(g + 1) * 8, 15:79],
                                                                 in_=ug[:].rearrange("p (a b) -> p a b", b=64), func=AF.Copy),
                                   reads=["ug"], writes=["uhi"])
                                op("dve", lambda e: e.tensor_tensor(out=ulo[:, g * 8:(g + 1) * 8, 15:79],
                                                                    in0=ug[:].rearrange("p (a b) -> p a b", b=64),
                                                                    in1=uhi[:, g * 8:(g + 1) * 8, 15:79], op=ALU.subtract),
                                   reads=["ug", "uhi"], writes=["ulo"])
                    for ub, ukey in ((upad, "upad"), (uhi, "uhi"), (ulo, "ulo")):
                        op("dve", lambda e: e.tensor_tensor(out=ub[:, 1:32, 0:15], in0=ub[:, 0:31, 64:79],
                                                            in1=apx(mp("FL"), [[1, 31], [0, 15]]), op=ALU.mult),
                           reads=[ukey, "mpk"], writes=[ukey])
                        op("dve", lambda e: e.tensor_tensor(out=ub[:, 0:31, 79:94], in0=ub[:, 1:32, 15:30],
                                                            in1=apx(mp("FR"), [[1, 31], [0, 15]]), op=ALU.mult),
                           reads=[ukey, "mpk"], writes=[ukey])
                    accv = acc[:, 0, :].rearrange("p (a b) -> p a b", b=64)
                    CB = (2, 3, 4, 5)
                    for k in range(31):
                        if k == 0:
                            op("dve", lambda e: e.tensor_scalar(out=accv, in0=upad[:, :, 0:64], scalar1=cp(f"caw{l}", 0, 1),
                                                                scalar2=cp(f"cab{l}", 0, 1), op0=ALU.mult, op1=ALU.add),
                               reads=["upad", "cpk"], writes=[("acc", 0)], dur=2.35)
                        else:
                            op("dve", lambda e: e.scalar_tensor_tensor(out=accv, in0=upad[:, :, k:k + 64], scalar=cp(f"caw{l}", k, 1),
                                                                       in1=accv, op0=ALU.mult, op1=ALU.add),
                               reads=["upad", "cpk", ("acc", 0)], writes=[("acc", 0)], dur=2.35)
                        if k % 3 == 0 and k > 0:
                            mod_step(1, prefetch=(k < 30))
                        dh, dl = dgh[k % 4], dgl[k % 4]
                        op("act", lambda e: e.activation(out=dh[:], in_=ident, func=AF.Copy, scale=cp(f"caw{l}", 31 + k, 1)),
                           reads=["cbf", "cpk"], writes=[dh.name], dur=0.3)
                        op("act", lambda e: e.activation(out=dl[:], in_=ident, func=AF.Copy, scale=wlo[:, k:k + 1]),
                           reads=["cbf", "wlo"], writes=[dl.name], dur=0.3)
                        for g in range(NG):
                            win_hi = uhi[:, g * 8:(g + 1) * 8, k:k + 64]
                            win_lo = ulo[:, g * 8:(g + 1) * 8, k:k + 64]
                            op("pe", lambda e: e.matmul(ps[CB[g]][:], lhsT=dh[:], rhs=win_hi, start=(k == 0), stop=False),
                               reads=[dh.name, "uhi"], writes=[("ps", CB[g])], dur=0.22)
                            op("pe", lambda e: e.matmul(ps[CB[g]][:], lhsT=dh[:], rhs=win_lo, start=False, stop=False),
                               reads=[dh.name, "ulo"], writes=[("ps", CB[g])], dur=0.22)
                            op("pe", lambda e: e.matmul(ps[CB[g]][:], lhsT=dl[:], rhs=win_hi, start=False, stop=(k == 30)),
                               reads=[dl.name, "uhi"], writes=[("ps", CB[g])], dur=0.22)
                    for g in range(NG):
                        op("act", lambda e: e.activation(out=acc[:, 1, g * GT:(g + 1) * GT], in_=ps[CB[g]][:], func=AF.Identity,
                                                         bias=cp(f"cab{l}", 1, 1), scale=1.0),
                           reads=[("ps", CB[g]), "cpk"], writes=[("acc", 1)])
                    wunpin()
                    pre["C1"] = wload(w_kc(w_in[l], OFF["c_b"], 512), 128, (8, 512))
                    pre["C2"] = wload(w_kc(w_in[l], OFF["c_x"], 256), 128, (8, 256))
                  with ExitStack() as es3:
                    S.fence()
                    accb = [T(es3, U("accb"), [128, GT], BF16) for _ in range(4)]
                    sq2 = [T(es3, U("sq2"), [128, GT], BF16) for _ in range(4)]
                    mean = [T(es3, U("mean"), [128, GT]) for _ in range(NG)]
                    rstl = [T(es3, U("rstl"), [128, GT]) for _ in range(NG)]
                    tn = [T(es3, U("tn"), [128, GT]) for _ in range(2)]
                    ltok = {}

                    def ln_stat(th):
                        bs, bq = (4, 5) if th == 0 else (2, 3)
                        for g in (th, th + 2):
                            gs = slice(g * GT, (g + 1) * GT)
                            for ch in range(2):
                                ab, s2 = accb[th * 2 + ch], sq2[th * 2 + ch]
                                yield OP("dve", lambda e: e.tensor_copy(out=ab[:], in_=acc[:, ch, gs]), reads=[("acc", ch)], writes=[ab.name])
                                yield OP("pe", lambda e: e.matmul(ps[bs][:], lhsT=ones_bf, rhs=ab[:], start=(ch == 0), stop=(ch == 1)),
                                         reads=[ab.name, "cbf"], writes=[("ps", bs)], n=512)
                                yield OP("act", lambda e: e.activation(out=s2[:], in_=acc[:, ch, gs], func=AF.Square), reads=[("acc", ch)], writes=[s2.name])
                                yield OP("pe", lambda e: e.matmul(ps[bq][:], lhsT=ones_bf, rhs=s2[:], start=(ch == 0), stop=(ch == 1)),
                                         reads=[s2.name, "cbf"], writes=[("ps", bq)], n=512)
                            mg, rg = mean[g], rstl[g]
                            yield OP("dve", lambda e: e.tensor_scalar(out=mg[:], in0=ps[bs][:], scalar1=1.0 / 256, scalar2=0.0, op0=ALU.mult, op1=ALU.add),
                                     reads=[("ps", bs)], writes=[mg.name])
                            yield OP("dve", lambda e: e.tensor_tensor(out=rg[:], in0=mg[:], in1=mg[:], op=ALU.mult), reads=[mg.name], writes=[rg.name])
                            yield OP("dve", lambda e: e.scalar_tensor_tensor(out=rg[:], in0=ps[bq][:], scalar=1.0 / 256, in1=rg[:],
                                                                             op0=ALU.mult, op1=ALU.subtract),
                                     reads=[("ps", bq), rg.name], writes=[rg.name])
                            yield OP("act", lambda e: e.activation(out=rg[:], in_=rg[:], func=AF.Ln, bias=epsD, scale=1.0),
                                     reads=[rg.name, "mpk"], writes=[rg.name], aset="lnexp")
                            ltok[g] = yield OP("act", lambda e: e.activation(out=rg[:], in_=rg[:], func=AF.Exp, scale=-0.5),
                                               reads=[rg.name], writes=[rg.name], aset="lnexp")

                    def ln_apply(th):
                        t = tn[th]
                        for g in (th, th + 2):
                            gs = slice(g * GT, (g + 1) * GT)
                            mg, rg = mean[g], rstl[g]
                            for ch in range(2):
                                yield OP("dve", lambda e: e.tensor_tensor(out=t[:], in0=acc[:, ch, gs], in1=mg[:], op=ALU.subtract),
                                         reads=[("acc", ch), mg.name], writes=[t.name])
                                yield OP("dve", lambda e: e.tensor_tensor(out=t[:], in0=t[:], in1=rg[:], op=ALU.mult),
                                         reads=[t.name, rg.name], writes=[t.name], expect=[(rg.name, ltok[g])])
                                yield OP("act", lambda e: e.activation(out=ua_out[:, ch, gs], in_=t[:], func=AF.Silu,
                                                                       bias=cp(f"lab{l}", ch, 1), scale=cp(f"lag{l}", ch, 1)),
                                         reads=[t.name, "cpk"], writes=[("ua_out", ch, g)], aset="silu")

                    run_sched(S, ln_stat(0), ln_stat(1))
                    run_sched(S, ln_apply(0), ln_apply(1))
                chk("A")
                if l == 0:
                    tap("ua0", ua_out[:], [("ua_out", p_, g_) for p_ in range(2) for g_ in range(NG)])

                with ExitStack() as es:
                    S.fence()
                    ucps = [T(es, U("ucp"), [128, 64 + NT + 64]) for _ in range(2)]
                    t1 = T(es, U("t1"), [128, NT])
                    acccs = [T(es, U("accc"), [128, NT]) for _ in range(2)]
                    tmpcs = [T(es, U("tmpc"), [128, GT]) for _ in range(2)]
                    for ch in range(2):
                        op("dve", lambda e: e.memset(ucps[ch][:], 0.0), writes=[ucps[ch].name])
                    wt1, wk1 = pre["C1"]
                    wt2, wk2 = pre["C2"]
                    pre["D"] = wload(w_kc(w_in[l], OFF["d"], 256), 128, (8, 256), pin=True)
                    c0 = 64
                    seg = lambda a: a.rearrange("p (s j) -> p s j", j=256)

                    def c_proj(ch, g):
                        ucp, tmpc = ucps[ch], tmpcs[g % 2]
                        bank = zbank(0, 4)
                        zmm(bank, wt1, wk1, lambda kc: kc * 512 + 256 + ch * 128, 128, g)
                        op("act", lambda e: e.activation(out=tmpc[:], in_=ps[bank][:], func=AF.Copy), reads=[("ps", bank)], writes=[tmpc.name])
                        bank2 = zbank(0, 4)
                        zmm(bank2, wt2, wk2, lambda kc: kc * 256 + ch * 128, 128, g)
                        op("dve", lambda e: e.tensor_tensor(out=ucp[:, 64 + g * GT:64 + (g + 1) * GT], in0=ps[bank2][:], in1=tmpc[:], op=ALU.mult),
                           reads=[("ps", bank2), tmpc.name], writes=[ucp.name])

                    def c_final(ch, g):
                        accc = acccs[ch]
                        gs = slice(g * GT, (g + 1) * GT)
                        bank = zbank(0, 4)
                        zmm(bank, wt1, wk1, lambda kc: kc * 512 + ch * 128, 128, g)
                        op("dve", lambda e: e.tensor_tensor(out=uc_out[:, ch, gs], in0=ps[bank][:], in1=accc[:, gs], op=ALU.mult),
                           reads=[("ps", bank), accc.name], writes=[("uc_out", ch, g)])

                    def c_big(ch):
                        ucp, accc = ucps[ch], acccs[ch]
                        uk, ak = ucp.name, accc.name
                        w0 = cp(f"ccw{l}", ch * 3 + 0, 1)
                        w1 = cp(f"ccw{l}", ch * 3 + 1, 1)
                        w2c = cp(f"ccw{l}", ch * 3 + 2, 1)
                        return [
                            lambda: op("dve", lambda e: e.tensor_tensor(out=seg(t1[:]), in0=seg(ucp[:, c0 - 1:c0 - 1 + NT]),
                                                                        in1=apx(mp("mCL"), [[0, 8], [1, 256]]), op=ALU.mult),
                                       reads=[uk, "mpk"], writes=["t1"], dur=2.35),
                            lambda: op("dve", lambda e: e.scalar_tensor_tensor(out=t1[:], in0=ucp[:, c0 - 64:c0 - 64 + NT], scalar=mp("fs"), in1=t1[:],
                                                                               op0=ALU.mult, op1=ALU.add),
                                       reads=[uk, "mpk", "t1"], writes=["t1"], dur=2.35),
                            lambda: op("dve", lambda e: e.tensor_scalar(out=accc[:], in0=ucp[:, c0:c0 + NT], scalar1=w1, scalar2=0.0, op0=ALU.mult, op1=ALU.add),
                                       reads=[uk, "cpk"], writes=[ak], dur=2.35),
                            lambda: op("dve", lambda e: e.scalar_tensor_tensor(out=accc[:], in0=t1[:], scalar=w0, in1=accc[:], op0=ALU.mult, op1=ALU.add),
                                       reads=["t1", "cpk", ak], writes=[ak], dur=2.35),
                            lambda: op("dve", lambda e: e.tensor_tensor(out=seg(t1[:]), in0=seg(ucp[:, c0 + 1:c0 + 1 + NT]),
                                                                        in1=apx(mp("mCR"), [[0, 8], [1, 256]]), op=ALU.mult),
                                       reads=[uk, "mpk", ak], writes=["t1"], dur=2.35),
                            lambda: op("dve", lambda e: e.scalar_tensor_tensor(out=t1[:], in0=ucp[:, c0 + 64:c0 + 64 + NT], scalar=mp("fs"), in1=t1[:],
                                                                               op0=ALU.mult, op1=ALU.add),
                                       reads=[uk, "mpk", "t1"], writes=["t1"], dur=2.35),
                            lambda: op("dve", lambda e: e.scalar_tensor_tensor(out=accc[:], in0=t1[:], scalar=w2c, in1=accc[:], op0=ALU.mult, op1=ALU.add),
                                       reads=["t1", "cpk", ak], writes=[ak], dur=2.35),
                        ]

                    for g in range(NG):
                        c_proj(0, g)
                    for i_, emit in enumerate(c_big(0)):
                        emit()
                        if i_ < NG:
                            c_proj(1, i_)
                    for i_, emit in enumerate(c_big(1)):
                        emit()
                        if i_ < NG:
                            c_final(0, i_)
                    for g in range(NG):
                        c_final(1, g)
                chk("C")
                if l == 0:
                    tap("uc0", uc_out[:], [("uc_out", p_, g_) for p_ in range(2) for g_ in range(NG)])

                with ExitStack() as es:
                    S.fence()
                    PADS = 512
                    WS = NT + 2 * PADS
                    uds = [T(es, U(("ud", ch)), [128, WS]) for _ in range(2)]
                    wk = T(es, U("wk"), [128, WS])
                    rr = T(es, U("rr"), [128, NT])
                    for ch in range(2):
                        op("dve", lambda e: e.memset(uds[ch][:, 0:PADS], 0.0), writes=[("ud", ch)])
                        op("dve", lambda e: e.memset(uds[ch][:, PADS + NT:WS], 0.0), writes=[("ud", ch)])
                    op("dve", lambda e: e.memset(wk[:], 0.0), writes=["wk"])
                    wt, wkey = pre["D"]
                    if mod_deferred:
                        mod_issue(*mod_deferred[0])
                    for ch in range(2):
                        for g in range(NG):
                            bank = zbank(0, 4)
                            zmm(bank, wt, wkey, lambda kc: kc * 256 + ch * 128, 128, g)
                            op("act", lambda e: e.activation(out=uds[ch][:, PADS + g * GT:PADS + (g + 1) * GT], in_=ps[bank][:], func=AF.Copy),
                               reads=[("ps", bank)], writes=[("ud", ch)])
                    for ch in range(2):
                        ud = uds[ch]
                        nlev = (1, 2) if ch == 0 else (3, 4)
                        SW = 272
                        wseg = wk[:, 0:8 * SW].rearrange("p (s j) -> p s j", j=SW)
                        op("dve", lambda e: e.memset(wk[:, 0:8 * SW], 0.0), reads=[], writes=["wk"])
                        udseg = ud[:, PADS:PADS + NT].rearrange("p (s j) -> p s j", j=256)
                        op("dve", lambda e: e.tensor_copy(out=wseg[:, :, 8:264], in_=udseg), reads=[("ud", ch)], writes=["wk"])
                        for lev in range(1, nlev[1] + 1):
                            mod_step(1)
                            sh = 1 << (lev - 1)
                            op("dve", lambda e: e.tensor_tensor(out=wseg[:, :, 0:SW - sh], in0=wseg[:, :, 0:SW - sh], in1=wseg[:, :, sh:SW], op=ALU.add),
                               reads=["wk"], writes=["wk"])
                            for half in range(2):
                                if nlev[half] == lev:
                                    w = 1 << lev
                                    hp = slice(64 * half, 64 * half + 64)
                                    o0 = 8 - w // 2
                                    op("dve", lambda e: e.tensor_tensor(out=rr[hp, :].rearrange("p (s j) -> p s j", j=256),
                                                                        in0=wseg[hp, :, o0:o0 + 256],
                                                                        in1=apx(mp("icp", ch * 256, 256)[hp, :], [[0, 8], [1, 256]]), op=ALU.mult),
                                       reads=["wk", "mpk"], writes=["rr"])
                        st64 = 64
                        for lev in range(1, nlev[1] + 1):
                            mod_step(1)
                            sh = st64 << (lev - 1)
                            src = ud if lev == 1 else wk
                            op("dve", lambda e: e.tensor_tensor(out=wk[:, 0:WS - sh], in0=src[:, 0:WS - sh], in1=src[:, sh:WS], op=ALU.add),
                               reads=[("ud", ch), "wk"], writes=["wk"])
                            for half in range(2):
                                if nlev[half] == lev:
                                    w = 1 << lev
                                    hp = slice(64 * half, 64 * half + 64)
                                    o0 = PADS - (w // 2) * st64
                                    wv = wk[hp, o0:o0 + NT].rearrange("p (r c) -> p r c", c=64)
                                    op("dve", lambda e: e.tensor_tensor(out=wv, in0=wv, in1=apx(mp("ics", ch * 32, 32)[hp, :], [[1, 32], [0, 64]]), op=ALU.mult),
                                       reads=["wk", "mpk"], writes=["wk"])
                                    op("dve", lambda e: e.tensor_tensor(out=rr[hp, :], in0=rr[hp, :], in1=wk[hp, o0:o0 + NT], op=ALU.add),
                                       reads=["wk", "rr"], writes=["rr"])
                                    if half == 0 and nlev[1] > lev:
                                        pass
                        op("dve", lambda e: e.tensor_tensor(out=ud_out[:, ch, :], in0=rr[:], in1=ud[:, PADS:PADS + NT], op=ALU.subtract),
                           reads=["rr", ("ud", ch)], writes=[("ud_out", ch, gg) for gg in range(NG)])
                mod_step(100)
                wunpin()
                gitems = []
                for m in range(KC):
                    gitems.append(([w_kc(w_in[l], OFF["merge"] + jb * 1024 + m * 128, 128) for jb in range(4)], 128, (8, 128)))
                pre["stg"] = Stream(gitems)
                pre["stg"].ensure(2)
                chk("D")
                if l == 0:
                    tap("ud0", ud_out[:], [("ud_out", p_, g_) for p_ in range(2) for g_ in range(NG)])

                with ExitStack() as es:
                    S.fence()
                    mod_step(100)
                    merged = T(es, U("merged"), [128, KC, NT], BF16)
                    wsm = T(es, U("wsm"), [128, 2, 7, 128], BF16)
                    sg = [T(es, U("sg"), [128, GT]) for _ in range(1)]
                    pr = [T(es, U("pr"), [128, GT]) for _ in range(1)]
                    macc = T(es, U("macc"), [128, GT])
                    outs = (ua_out, ob_out, uc_out, ud_out)
                    onames = ("ua_out", "ob_out", "uc_out", "ud_out")
                    stg = pre["stg"]
                    sto = Stream([(w_kc(w_o[l], c0, 512), 128, (8, 512)) for c0 in range(0, D, 512)])

                    def wsm_load(m_):
                        sl_ = m_ % 2
                        for bi, wsrc in enumerate((w_oa, w_ob, w_oc)):
                            S.dma("pool", wsm[:, sl_, 2 * bi:2 * bi + 2, :],
                                  wsrc[l].rearrange("(kc p) n -> p kc n", p=128)[:, :, m_ * 128:(m_ + 1) * 128], writes=[("wsm", sl_)], scoped=True)
                        hpm_ = slice(64 * ((m_ // 2) % 2), 64 * ((m_ // 2) % 2) + 64)
                        S.dma("pool", wsm[hpm_, sl_, 6, :], pool_w[l, m_ // 2, :, (m_ % 2) * 128:(m_ % 2) * 128 + 128], writes=[("wsm", sl_)], scoped=True)

                    wsm_load(0)
                    for m in range(KC):
                        wg_t, wg_k = stg.get(m, ahead=2)
                        if m + 1 < KC:
                            wsm_load(m + 1)
                        if m == KC - 1:
                            sto.ensure(1)
                        sl = m % 2
                        for g in range(NG):
                            gs = slice(g * GT, (g + 1) * GT)
                            for j in range(4):
                                yb = zbank(0, 4)
                                if j < 3:
                                    for kc2 in range(2):
                                        op("pe", lambda e: e.matmul(ps[yb][:], lhsT=wsm[:, sl, 2 * j + kc2, :], rhs=outs[j][:, kc2, gs],
                                                                    start=(kc2 == 0), stop=(kc2 == 1)),
                                           reads=[("wsm", sl), (onames[j], kc2, g)], writes=[("ps", yb)])
                                else:
                                    half = (m // 2) % 2
                                    chd = (m // 2) // 2
                                    hp = slice(64 * half, 64 * half + 64)
                                    op("pe", lambda e: e.matmul(ps[yb][:], lhsT=wsm[hp, sl, 6, :], rhs=ud_out[hp, chd, gs], start=True, stop=True),
                                       reads=[("wsm", sl), ("ud_out", chd, g)], writes=[("ps", yb)])
                                gb = zbank(0, 4)
                                zmm(gb, wg_t, wg_k, lambda kc: (j * 8 + kc) * 128, 128, g)
                                sgt_ = sg[0]
                                op("act", lambda e: e.activation(out=sgt_[:], in_=ps[gb][:], func=AF.Sigmoid), reads=[("ps", gb)], writes=[sgt_.name])
                                dst = macc if j == 0 else pr[0]
                                if j < 3:
                                    op("dve", lambda e: e.tensor_tensor(out=dst[:], in0=ps[yb][:], in1=sgt_[:], op=ALU.mult),
                                       reads=[("ps", yb), sgt_.name], writes=[dst.name])
                                else:
                                    op("dve", lambda e: e.scalar_tensor_tensor(out=dst[:], in0=ps[yb][:], scalar=cp(f"psc{l}", m, 1), in1=sgt_[:],
                                                                               op0=ALU.mult, op1=ALU.mult),
                                       reads=[("ps", yb), sgt_.name, "cpk"], writes=[dst.name])
                                if j in (1, 2):
                                    op("dve", lambda e: e.tensor_tensor(out=macc[:], in0=macc[:], in1=dst[:], op=ALU.add),
                                       reads=[macc.name, dst.name], writes=[macc.name])
                                elif j == 3:
                                    op("dve", lambda e: e.tensor_tensor(out=merged[:, m, gs], in0=macc[:], in1=dst[:], op=ALU.add),
                                       reads=[macc.name, dst.name], writes=[("merged", m, g)])
                    if l == 0:
                        tap("mg0", merged[:], [("merged", m_, g_) for m_ in range(KC) for g_ in range(NG)])
                    for nb in range(2):
                        wt, wkey = sto.get(nb, ahead=2)
                        for nn in range(4):
                            n = nb * 4 + nn
                            for g in range(NG):
                                gs = slice(g * GT, (g + 1) * GT)
                                bank = zbank(0, 4)
                                for m in range(KC):
                                    op("pe", lambda e: e.matmul(ps[bank][:], lhsT=wt[:, m * 512 + nn * 128:m * 512 + nn * 128 + 128],
                                                                rhs=merged[:, m, gs], start=(m == 0), stop=(m == KC - 1)),
                                       reads=[wkey, ("merged", m, g)], writes=[("ps", bank)])
                                op("dve", lambda e: e.scalar_tensor_tensor(out=x[:, n, gs], in0=ps[bank][:], scalar=mod[:, l, 16 + n:16 + n + 1],
                                                                           in1=x[:, n, gs], op0=ALU.mult, op1=ALU.add),
                                   reads=[("ps", bank), ("x", n, g)] + modkeys(l, "g_m"), writes=[("x", n, g)])
            chk("merge")
            if l == 0:
                tap("x1", x[:], [("x", m_, g_) for m_ in range(KC) for g_ in range(NG)])
            fitems = [([w_kc(w13[l], j * 128, 128), w_kc(w13[l], DFF + j * 128, 128)], 128, (8, 128)) for j in range(11)]
            pre["stf0"] = Stream(fitems)
            pre["stf0"].ensure(2)
            with ExitStack() as es:
                S.fence()
                rms_to_h(l, af[:, l, :], 24, es)
            with ExitStack() as es:
                S.fence()
                act = T(es, U("act"), [128, 11, NT], BF16)
                sil = [T(es, U("sil"), [128, GT]) for _ in range(2)]
                for hf in range(2):
                    items = []
                    for jj in range(11):
                        j = hf * 11 + jj
                        items.append(([w_kc(w13[l], j * 128, 128), w_kc(w13[l], DFF + j * 128, 128)], 128, (8, 128)))
                    stf = pre.pop("stf0") if hf == 0 else Stream(items)
                    for jj in range(11):
                        wt, wkey = stf.get(jj)
                        for g in range(NG):
                            gs = slice(g * GT, (g + 1) * GT)
                            b1 = zbank(0, 4)
                            zmm(b1, wt, wkey, lambda kc: kc * 128, 128, g)
                            st_ = sil[g % 2]
                            op("act", lambda e: e.activation(out=st_[:], in_=ps[b1][:], func=AF.Silu), reads=[("ps", b1)], writes=[st_.name])
                            b2 = zbank(0, 4)
                            zmm(b2, wt, wkey, lambda kc: (8 + kc) * 128, 128, g)
                            op("dve", lambda e: e.tensor_tensor(out=act[:, jj, gs], in0=ps[b2][:], in1=st_[:], op=ALU.mult),
                               reads=[("ps", b2), st_.name], writes=[("act", jj, g)])
                    items = []
                    for npair in range(4):
                        items.append((bass.AP(w2.tensor, w2[l, hf * 1408:hf * 1408 + 1, npair * 256:npair * 256 + 1].offset,
                                              [[D, 128], [128 * D, 11], [1, 256]]), 128, (11, 256)))
                    st2 = Stream(items)
                    for npair in range(4):
                        wt, wkey = st2.get(npair)
                        for nn in range(2):
                            n = npair * 2 + nn
                            for g in range(NG):
                                gs = slice(g * GT, (g + 1) * GT)
                                bank = zbank(0, 4)
                                for jj in range(11):
                                    op("pe", lambda e: e.matmul(ps[bank][:], lhsT=wt[:, jj * 256 + nn * 128:jj * 256 + nn * 128 + 128],
                                                                rhs=act[:, jj, gs], start=(jj == 0), stop=(jj == 10)),
                                       reads=[wkey, ("act", jj, g)], writes=[("ps", bank)])
                                op("dve", lambda e: e.scalar_tensor_tensor(out=x[:, n, gs], in0=ps[bank][:], scalar=mod[:, l, 40 + n:40 + n + 1],
                                                                           in1=x[:, n, gs], op0=ALU.mult, op1=ALU.add),
                                   reads=[("ps", bank), ("x", n, g)] + modkeys(l, "g_f"), writes=[("x", n, g)])
            if l == 0:
                tap("x2", x[:], [("x", m_, g_) for m_ in range(KC) for g_ in range(NG)])
          except _Stop:
            break
        with ExitStack() as es:
            S.fence()
            rms_to_h(0, cp("fng"), None, es)
        S.finish("sp")
    print(f"[kernel] instructions={S.ninst} waits={S.nwaits} per-engine={S.cnt}")
    return nc


_NC_CACHE = {}


def make_in_maps(inp):
    x_prompt = np.asarray(inp["x_prompt"], np.float32)
    x_sample = np.asarray(inp["x_sample"], np.float32)
    state = np.asarray(inp["state_hgrn"], np.float32)
    c = np.asarray(inp["c"], np.float32)
    c_ctx = np.asarray(inp["c_ctx"], np.float32)
    cbf = make_cbf()
    mpk_s, _ = make_mpk(True)
    mpk_p, _ = make_mpk(False)
    wnames = ["ada_w", "w_in", "w_out_a", "w_out_b", "w_out_c", "pool_w", "w_o", "ffn_w13", "ffn_w2"]
    weights = {n: np.ascontiguousarray(np.asarray(inp[n], np.float32)) for n in wnames}
    in_maps = []
    for core in range(8):
        if core < 4:
            xt = x_sample[core]
            cvec = c[core]
            s0 = state[core].reshape(L, 2, 2, 128, 64)
            mpk = mpk_s
        else:
            xt = x_prompt[(core - 4) * 8:(core - 3) * 8].reshape(NT, D)
            cvec = c_ctx
            s0 = np.zeros((L, 2, 2, 128, 64), np.float32)
            mpk = mpk_p
        m = {"xT": np.ascontiguousarray(xt.T), "cpk": make_cpk(inp, cvec), "mpk": mpk, "cbf": cbf,
             "s0": np.ascontiguousarray(s0)}
        m.update(weights)
        in_maps.append(m)
    return in_maps


def kernel(**inputs):
    if "nc" not in _NC_CACHE:
        _NC_CACHE["nc"] = build_program()
    nc = _NC_CACHE["nc"]
    in_maps = make_in_maps(inputs)
    res = run_bass_kernel_spmd(nc, in_maps, core_ids=list(range(8)))
    rs = res.results
    y_sample = np.stack([np.ascontiguousarray(rs[i]["yT"].T) for i in range(4)], axis=0).astype(np.float32)
    y_prompt = np.concatenate([np.ascontiguousarray(rs[i]["yT"].T).reshape(8, 256, D) for i in range(4, 8)], axis=0).astype(np.float32)
    new_state = np.concatenate([rs[i]["st"].reshape(8, L, 2, 4, 64, 64) for i in range(4, 8)], axis=0).astype(np.float32)
    return (y_prompt, y_sample, new_state)
```
